# Optimizing a Trainium2 kernel written in Bass

```python
import math
import numpy as np
import jax
import jax.numpy as jnp
from jax import lax

D_MODEL = 1024
BATCH = 4
SEQ = 8192
DEPTH = 2

GRID_W = 64
CTX_LEN = 256
N_EVEN = (DEPTH + 1) // 2
N_ODD = DEPTH // 2
N_MOD = 9
DN_ALPHA = float((2 * DEPTH) ** 0.25)
DN_BETA = float((8 * DEPTH) ** -0.25)
LN_EPS = 1e-6
RMS_EPS = 1e-6
SUBLN_EPS = 1e-5
ROPE_BASE = 10000.0
Q_BLOCK = 128
NEG_INF = -1e30

D_FF = 2816

A_HEADS = 8
A_NOPE = 64
A_ROPE = 32
A_V = 64
A_Q_RANK = 256
A_KV_RANK = 128
A_SCALE = float((A_NOPE + A_ROPE) ** -0.5)
A_WIDTH = A_HEADS * A_V

B_HEADS = 4
B_HD = 64
B_SCALE = float(B_HD ** -0.5)
B_QK = B_HEADS * 2 * B_HD
B_WIDTH = B_HEADS * 2 * B_HD

C_GROUPS = 4
C_WINDOWS = (2, 4, 8, 16)
C_GW = 128
C_WIDTH = C_GROUPS * C_GW

D_HEADS = 8
D_HD = 64
D_SCALE = float(D_HD ** -0.5)
D_WIDTH = D_HEADS * D_HD
NA_ROWS = 8
NA_COLS = 16

EV_Q_COLS = A_Q_RANK + B_QK
EV_KV_COLS = A_KV_RANK + A_ROPE + B_QK + B_QK
EV_COLS = EV_Q_COLS + EV_KV_COLS
EV_MIX = A_WIDTH + B_WIDTH
OD_Q_COLS = C_WIDTH + D_WIDTH
OD_COLS = OD_Q_COLS + 2 * D_WIDTH
OD_MIX = C_WIDTH + D_WIDTH

kernel_name = 'hybrid_mla_diff_pool_natten_prefix_trunk'


def layer_norm(x, g, b):
    xf = x.astype(jnp.float32)
    mu = jnp.mean(xf, -1, keepdims=True)
    var = jnp.mean(jnp.square(xf - mu), -1, keepdims=True)
    y = (xf - mu) * lax.rsqrt(var + LN_EPS)
    return (y * g.astype(jnp.float32) + b.astype(jnp.float32)).astype(x.dtype)


def rms_norm(x, g, eps=RMS_EPS):
    xf = x.astype(jnp.float32)
    y = xf * lax.rsqrt(jnp.mean(jnp.square(xf), -1, keepdims=True) + eps)
    return (y * g.astype(jnp.float32)).astype(x.dtype)


def modulate(x, shift, scale):
    return x * (1 + scale) + shift


def post_norm(x, y, g, b):
    return layer_norm(DN_ALPHA * x + y, g, b)


def swiglu(h, w_gate, w_up, w_down):
    return (jax.nn.silu(h @ w_gate) * (h @ w_up)) @ w_down


def axial_rope_tables(n, rot_dim):
    t = jnp.arange(n, dtype=jnp.int32)
    row = (t // GRID_W).astype(jnp.float32)
    col = (t % GRID_W).astype(jnp.float32)
    quarter = rot_dim // 4
    inv = ROPE_BASE ** (-jnp.arange(quarter, dtype=jnp.float32) / quarter)
    ang_r = row[:, None] * inv[None, :]
    ang_c = col[:, None] * inv[None, :]
    ang = jnp.concatenate([ang_r, ang_r, ang_c, ang_c], -1)
    return jnp.cos(ang), jnp.sin(ang)


def maybe_rope(x, tables):
    if tables is None:
        return x
    cos, sin = tables
    if x.ndim == 4:
        cos, sin = cos[:, None, :], sin[:, None, :]
    xf = x.astype(jnp.float32)
    x1, x2, x3, x4 = jnp.split(xf, 4, axis=-1)
    rot = jnp.concatenate([-x2, x1, -x4, x3], -1)
    return (xf * cos + rot * sin).astype(x.dtype)


def softmax_f32(s, scale):
    return jax.nn.softmax(s.astype(jnp.float32) * scale, axis=-1)


def softmax_attend(q, k, v, scale):
    p = softmax_f32(jnp.einsum('bqhd,bkhd->bhqk', q, k), scale)
    return jnp.einsum('bhqk,bkhd->bqhd', p.astype(v.dtype), v)


def mla_attend(q_nope, q_pe, k_nope, k_pe, v):
    s = jnp.einsum('bqhd,bkhd->bhqk', q_nope, k_nope) + jnp.einsum('bqhr,bkr->bhqk', q_pe, k_pe)
    p = softmax_f32(s, A_SCALE)
    return jnp.einsum('bhqk,bkhd->bqhd', p.astype(v.dtype), v)


def diff_attend(q1, q2, k1, k2, v, lam, lam_init, g_sub):
    p1 = softmax_f32(jnp.einsum('bqhd,bkhd->bhqk', q1, k1), B_SCALE)
    p2 = softmax_f32(jnp.einsum('bqhd,bkhd->bhqk', q2, k2), B_SCALE)
    o = jnp.einsum('bhqk,bkhd->bqhd', (p1 - lam * p2).astype(v.dtype), v)
    return rms_norm(o, g_sub, SUBLN_EPS) * (1.0 - lam_init)


def sweep_query_blocks(fn, *qs):
    b, n = qs[0].shape[:2]
    nb = n // Q_BLOCK
    blocks = tuple(jnp.moveaxis(q.reshape((b, nb, Q_BLOCK) + q.shape[2:]), 1, 0) for q in qs)
    out = lax.map(lambda blk: fn(*blk), blocks)
    out = jnp.moveaxis(out, 0, 1)
    return out.reshape((b, n) + out.shape[3:])


def even_queries(qp, g_qlat, w_uq, rope_a, rope_b):
    b, n, _ = qp.shape
    q = (rms_norm(qp[..., :A_Q_RANK], g_qlat) @ w_uq).reshape(b, n, A_HEADS, A_NOPE + A_ROPE)
    bq = qp[..., A_Q_RANK:].reshape(b, n, B_HEADS, 2, B_HD)
    return (q[..., :A_NOPE], maybe_rope(q[..., A_NOPE:], rope_a),
            maybe_rope(bq[..., 0, :], rope_b), maybe_rope(bq[..., 1, :], rope_b))


def even_keys(kp, g_kvlat, w_ukv, rope_a, rope_b):
    b, n, _ = kp.shape
    kv = (rms_norm(kp[..., :A_KV_RANK], g_kvlat) @ w_ukv).reshape(b, n, A_HEADS, A_NOPE + A_V)
    k_pe = maybe_rope(kp[..., A_KV_RANK:A_KV_RANK + A_ROPE], rope_a)
    o = A_KV_RANK + A_ROPE
    bk = kp[..., o:o + B_QK].reshape(b, n, B_HEADS, 2, B_HD)
    v_b = kp[..., o + B_QK:].reshape(b, n, B_HEADS, 2 * B_HD)
    return (kv[..., :A_NOPE], k_pe, kv[..., A_NOPE:],
            maybe_rope(bk[..., 0, :], rope_b), maybe_rope(bk[..., 1, :], rope_b), v_b)


def even_mixer(h, hc, w_in, w_out, g_qlat, g_kvlat, w_uq, w_ukv, lam_vec, g_sub, lam_init, ctx_out):
    b, n, _ = h.shape
    m = hc.shape[1]
    rope_a = axial_rope_tables(n, A_ROPE)
    rope_b = axial_rope_tables(n, B_HD)
    lv = lam_vec.astype(jnp.float32)
    lam = jnp.exp(jnp.sum(lv[0] * lv[1])) - jnp.exp(jnp.sum(lv[2] * lv[3])) + lam_init
    p = h @ w_in
    pc = hc @ (w_in if ctx_out else w_in[:, EV_Q_COLS:])
    keys_ctx = even_keys(pc[..., -EV_KV_COLS:], g_kvlat, w_ukv, None, None)
    keys_lat = even_keys(p[..., EV_Q_COLS:], g_kvlat, w_ukv, rope_a, rope_b)
    kn, kpe, va, k1, k2, vb = [jnp.concatenate([kl, kc], axis=1) for kl, kc in zip(keys_lat, keys_ctx)]

    def block(qn, qpe, q1, q2):
        bq = qn.shape[1]
        oa = mla_attend(qn, qpe, kn, kpe, va).reshape(b, bq, A_WIDTH)
        ob = diff_attend(q1, q2, k1, k2, vb, lam, lam_init, g_sub).reshape(b, bq, B_WIDTH)
        return jnp.concatenate([oa, ob], -1)

    y = sweep_query_blocks(block, *even_queries(p[..., :EV_Q_COLS], g_qlat, w_uq, rope_a, rope_b)) @ w_out
    if not ctx_out:
        return y, None
    kn_c, kpe_c, va_c, k1_c, k2_c, vb_c = keys_ctx
    qn_c, qpe_c, q1_c, q2_c = even_queries(pc[..., :EV_Q_COLS], g_qlat, w_uq, None, None)
    oa_c = mla_attend(qn_c, qpe_c, kn_c, kpe_c, va_c).reshape(b, m, A_WIDTH)
    ob_c = diff_attend(q1_c, q2_c, k1_c, k2_c, vb_c, lam, lam_init, g_sub).reshape(b, m, B_WIDTH)
    yc = jnp.concatenate([oa_c, ob_c], -1) @ w_out
    return y, yc


def window_mean(u, w):
    n = u.shape[1]
    left = w // 2
    right = w - 1 - left
    t = np.arange(n)
    lo = np.clip(t - left, 0, n)
    hi = np.clip(t + right + 1, 0, n)
    uf = u.astype(jnp.float32)
    csum = jnp.concatenate([jnp.zeros_like(uf[:, :1]), jnp.cumsum(uf, axis=1)], axis=1)
    cnt = jnp.asarray(hi - lo, jnp.float32)
    return ((csum[:, hi] - csum[:, lo]) / cnt[None, :, None]).astype(u.dtype)


def multiscale_pool(u, w_pool, pool_scale):
    b, n, _ = u.shape
    ug = u.reshape(b, n, C_GROUPS, C_GW)
    pooled = jnp.stack([window_mean(ug[:, :, g], w) for g, w in enumerate(C_WINDOWS)], axis=2) - ug
    return jnp.einsum('bngc,gcd->bngd', pooled, w_pool).reshape(b, n, C_WIDTH) * pool_scale


def neighbourhood_attend(q, k, v, k_ctx, v_ctx, rpb):
    b, n, h, d = q.shape
    rows = n // GRID_W
    wr = min(NA_ROWS, rows)
    rs = np.clip(np.arange(rows) - wr // 2, 0, rows - wr)
    j = np.arange(GRID_W)
    cs = np.clip(j - NA_COLS // 2, 0, GRID_W - NA_COLS)
    col_valid = (j[None, :] >= cs[:, None]) & (j[None, :] < cs[:, None] + NA_COLS)
    col_idx = np.clip(j[None, :] - j[:, None] + NA_COLS - 1, 0, 2 * NA_COLS - 2)
    q_rows = jnp.moveaxis(q.reshape(b, rows, GRID_W, h, d), 1, 0)
    k_grid = k.reshape(b, rows, GRID_W, h, d)
    v_grid = v.reshape(b, rows, GRID_W, h, d)
    nk = wr * GRID_W

    def row_fn(args):
        q_r, r, r0 = args
        k_blk = lax.dynamic_slice_in_dim(k_grid, r0, wr, axis=1).reshape(b, nk, h, d)
        v_blk = lax.dynamic_slice_in_dim(v_grid, r0, wr, axis=1).reshape(b, nk, h, d)
        row_off = r0 + jnp.arange(wr, dtype=jnp.int32) - r + NA_ROWS - 1
        bias = rpb[:, row_off[:, None, None], col_idx[None]].astype(jnp.float32)
        bias = jnp.where(col_valid[None, None], bias, NEG_INF)
        bias = bias.transpose(0, 2, 1, 3).reshape(h, GRID_W, nk)
        s_nb = jnp.einsum('bqhd,bkhd->bhqk', q_r, k_blk).astype(jnp.float32) * D_SCALE + bias
        s_cx = jnp.einsum('bqhd,bkhd->bhqk', q_r, k_ctx).astype(jnp.float32) * D_SCALE
        p = jax.nn.softmax(jnp.concatenate([s_nb, s_cx], -1), axis=-1).astype(v.dtype)
        return (jnp.einsum('bhqk,bkhd->bqhd', p[..., :nk], v_blk)
                + jnp.einsum('bhqk,bkhd->bqhd', p[..., nk:], v_ctx))

    out = lax.map(row_fn, (q_rows, jnp.arange(rows, dtype=jnp.int32), jnp.asarray(rs, jnp.int32)))
    return jnp.moveaxis(out, 0, 1).reshape(b, n, h, d)


def odd_mixer(h, hc, w_in, w_out, w_pool, pool_scale, rpb, ctx_out):
    b, n, _ = h.shape
    m = hc.shape[1]
    p = h @ w_in
    pc = hc @ (w_in if ctx_out else w_in[:, OD_Q_COLS:])
    q = p[..., C_WIDTH:OD_Q_COLS].reshape(b, n, D_HEADS, D_HD)
    k = p[..., OD_Q_COLS:OD_Q_COLS + D_WIDTH].reshape(b, n, D_HEADS, D_HD)
    v = p[..., OD_Q_COLS + D_WIDTH:].reshape(b, n, D_HEADS, D_HD)
    k_c = pc[..., -2 * D_WIDTH:-D_WIDTH].reshape(b, m, D_HEADS, D_HD)
    v_c = pc[..., -D_WIDTH:].reshape(b, m, D_HEADS, D_HD)
    y_pool = multiscale_pool(p[..., :C_WIDTH], w_pool, pool_scale)
    y_na = neighbourhood_attend(q, k, v, k_c, v_c, rpb).reshape(b, n, D_WIDTH)
    y = jnp.concatenate([y_pool, y_na], -1) @ w_out
    if not ctx_out:
        return y, None
    q_c = pc[..., C_WIDTH:OD_Q_COLS].reshape(b, m, D_HEADS, D_HD)
    yc_pool = multiscale_pool(pc[..., :C_WIDTH], w_pool, pool_scale)
    yc_att = softmax_attend(q_c, k_c, v_c, D_SCALE).reshape(b, m, D_WIDTH)
    yc = jnp.concatenate([yc_pool, yc_att], -1) @ w_out
    return y, yc


def setup_inputs(seed: int = 0) -> dict:
    key = jax.random.key(seed)
    ks = jax.random.split(key, 26)

    def nrm(k, shape, s):
        return jax.random.normal(k, shape, jnp.float32) * s

    return {
        'x': nrm(ks[0], (BATCH, SEQ, D_MODEL), 1.0),
        'c': nrm(ks[1], (BATCH, D_MODEL), 1.0),
        'ctx': nrm(ks[2], (BATCH, CTX_LEN, D_MODEL), 1.0),
        'c_ctx': nrm(ks[3], (D_MODEL,), 1.0),
        'ada_w': nrm(ks[4], (DEPTH, D_MODEL, N_MOD * D_MODEL), 0.5 * D_MODEL ** -0.5),
        'ada_b': nrm(ks[5], (DEPTH, N_MOD * D_MODEL), 0.02),
        'ln_g': 1.0 + nrm(ks[6], (DEPTH, 3, D_MODEL), 0.02),
        'ln_b': nrm(ks[7], (DEPTH, 3, D_MODEL), 0.02),
        'ffn_w_gate': nrm(ks[8], (DEPTH, 2, D_MODEL, D_FF), D_MODEL ** -0.5),
        'ffn_w_up': nrm(ks[9], (DEPTH, 2, D_MODEL, D_FF), D_MODEL ** -0.5),
        'ffn_w_down': nrm(ks[10], (DEPTH, 2, D_FF, D_MODEL), DN_BETA * D_FF ** -0.5),
        'ev_w_in': nrm(ks[11], (N_EVEN, D_MODEL, EV_COLS), D_MODEL ** -0.5),
        'ev_w_out': nrm(ks[12], (N_EVEN, EV_MIX, D_MODEL), DN_BETA * EV_MIX ** -0.5),
        'ev_g_qlat': 1.0 + nrm(ks[13], (N_EVEN, A_Q_RANK), 0.02),
        'ev_g_kvlat': 1.0 + nrm(ks[14], (N_EVEN, A_KV_RANK), 0.02),
        'ev_w_uq': nrm(ks[15], (N_EVEN, A_Q_RANK, A_HEADS * (A_NOPE + A_ROPE)), A_Q_RANK ** -0.5),
        'ev_w_ukv': nrm(ks[16], (N_EVEN, A_KV_RANK, A_HEADS * (A_NOPE + A_V)), A_KV_RANK ** -0.5),
        'ev_lam': nrm(ks[17], (N_EVEN, 4, B_HD), 0.1),
        'ev_g_sub': 1.0 + nrm(ks[18], (N_EVEN, 2 * B_HD), 0.02),
        'od_w_in': nrm(ks[19], (N_ODD, D_MODEL, OD_COLS), D_MODEL ** -0.5),
        'od_w_out': nrm(ks[20], (N_ODD, OD_MIX, D_MODEL), DN_BETA * OD_MIX ** -0.5),
        'od_w_pool': nrm(ks[21], (N_ODD, C_GROUPS, C_GW, C_GW), C_GW ** -0.5),
        'od_pool_scale': 1.0 + nrm(ks[22], (N_ODD, C_WIDTH), 0.02),
        'od_rpb': nrm(ks[23], (N_ODD, D_HEADS, 2 * NA_ROWS - 1, 2 * NA_COLS - 1), 0.02),
    }


def reference(x, c, ctx, c_ctx, ada_w, ada_b, ln_g, ln_b, ffn_w_gate, ffn_w_up, ffn_w_down,
              ev_w_in, ev_w_out, ev_g_qlat, ev_g_kvlat, ev_w_uq, ev_w_ukv, ev_lam, ev_g_sub,
              od_w_in, od_w_out, od_w_pool, od_pool_scale, od_rpb):
    bsz = x.shape[0]
    sc = jax.nn.silu(c)
    sc_ctx = jax.nn.silu(c_ctx)
    h, hc = x, ctx
    for l in range(DEPTH):
        last = l == DEPTH - 1
        mod = (sc @ ada_w[l] + ada_b[l]).reshape(bsz, N_MOD, 1, D_MODEL)
        mod_c = (sc_ctx @ ada_w[l] + ada_b[l]).reshape(N_MOD, D_MODEL)

        h = post_norm(h, 0.5 * mod[:, 2] * swiglu(modulate(h, mod[:, 0], mod[:, 1]),
                                                   ffn_w_gate[l, 0], ffn_w_up[l, 0], ffn_w_down[l, 0]),
                      ln_g[l, 0], ln_b[l, 0])
        hc = post_norm(hc, 0.5 * mod_c[2] * swiglu(modulate(hc, mod_c[0], mod_c[1]),
                                                    ffn_w_gate[l, 0], ffn_w_up[l, 0], ffn_w_down[l, 0]),
                       ln_g[l, 0], ln_b[l, 0])

        hm = modulate(h, mod[:, 3], mod[:, 4])
        hcm = modulate(hc, mod_c[3], mod_c[4])
        if l % 2 == 0:
            e = l // 2
            lam_init = 0.8 - 0.6 * math.exp(-0.3 * l)
            y, yc = even_mixer(hm, hcm, ev_w_in[e], ev_w_out[e], ev_g_qlat[e], ev_g_kvlat[e],
                               ev_w_uq[e], ev_w_ukv[e], ev_lam[e], ev_g_sub[e], lam_init, not last)
        else:
            o = l // 2
            y, yc = odd_mixer(hm, hcm, od_w_in[o], od_w_out[o], od_w_pool[o], od_pool_scale[o],
                              od_rpb[o], not last)
        h = post_norm(h, mod[:, 5] * y, ln_g[l, 1], ln_b[l, 1])

        h = post_norm(h, 0.5 * mod[:, 8] * swiglu(modulate(h, mod[:, 6], mod[:, 7]),
                                                   ffn_w_gate[l, 1], ffn_w_up[l, 1], ffn_w_down[l, 1]),
                      ln_g[l, 2], ln_b[l, 2])
        if not last:
            hc = post_norm(hc, mod_c[5] * yc, ln_g[l, 1], ln_b[l, 1])
            hc = post_norm(hc, 0.5 * mod_c[8] * swiglu(modulate(hc, mod_c[6], mod_c[7]),
                                                        ffn_w_gate[l, 1], ffn_w_up[l, 1], ffn_w_down[l, 1]),
                           ln_g[l, 2], ln_b[l, 2])
    return h
```

```python
import math
import os
import contextlib
import numpy as np
import concourse.bass as bass
import concourse.mybir as mybir
from concourse.bass_utils import run_bass_kernel_spmd

F32 = mybir.dt.float32
BF16 = mybir.dt.bfloat16
ALU = mybir.AluOpType
AF = mybir.ActivationFunctionType
AX = mybir.AxisListType

ENGS = ("pe", "act", "dve", "pool", "sp")

D = 1024
DFF = 2816
NFF = 22
T = 256
NL = 8192
NT = 8448
NQL = 4352
NQ = 4608
TILES_ALL = list(range(33))
TILES_Q = list(range(17)) + [32]
DN_ALPHA = float(4 ** 0.25)
LN_EPS = 1e-6
A_SCALE = float(96 ** -0.5)
B_SCALE = float(64 ** -0.5)
D_SCALE = float(64 ** -0.5)
LAM_INIT0 = 0.8 - 0.6 * math.exp(0.0)
NPADR = 76
NKP = NPADR * 64


def qcol(ti):
    return ti * 256 if ti < 17 else 4352


class Buf:
    __slots__ = ("name", "w", "r", "sem", "cnt", "excl")

    def __init__(self, name):
        self.name = name
        self.excl = False
        self.w = None
        self.r = []
        self.sem = None
        self.cnt = 0


class Prog:
    def __init__(self, nc):
        self.nc = nc
        self.streams = {e: [] for e in ENGS}
        self.count = {e: 0 for e in ENGS}
        self.known = {e: {} for e in ENGS}
        self.nsem_dma = 0
        self.dma_sems = []
        self.nflush = 0
        self.bufs = []

    def buf(self, name="b"):
        b = Buf(name)
        self.bufs.append(b)
        return b

    def _deps(self, eng, reads, writes):
        deps = {}

        def add(ev):
            if ev is None:
                return
            k, v = ev
            if deps.get(k, 0) < v:
                deps[k] = v
        for b in reads:
            add(b.w)
        for b in writes:
            add(b.w)
            for ev in b.r:
                add(ev)
        out = []
        kn = self.known[eng]
        for k, v in deps.items():
            if k == eng and eng == "pe":
                continue
            if kn.get(k, 0) >= v:
                continue
            kn[k] = v
            out.append((k, v))
        return out

    def _post(self, ev, reads, writes):
        for b in reads:
            if len(b.r) > 64:
                mx = {}
                for k, v in b.r:
                    if mx.get(k, 0) < v:
                        mx[k] = v
                b.r = list(mx.items())
            b.r.append(ev)
        for b in writes:
            b.w = ev
            b.r = []

    def op(self, eng, fn, reads=(), writes=()):
        if any(b.excl for b in reads):
            writes = list(writes) + [b for b in reads if b.excl]
            reads = [b for b in reads if not b.excl]
        waits = self._deps(eng, reads, writes)
        self.count[eng] += 1
        ev = (eng, self.count[eng])
        self.streams[eng].append((waits, fn, ev))
        self._post(ev, reads, writes)
        return ev

    def dma(self, q, fn, reads=(), writes=(), sembuf=None):
        sb = sembuf if sembuf is not None else writes[0]
        if sb.sem is None:
            sb.sem = "d%d" % self.nsem_dma
            self.nsem_dma += 1
            self.dma_sems.append(sb.sem)
        waits = self._deps(q, reads, writes)
        sb.cnt += 16
        ev = (sb.sem, sb.cnt)
        self.streams[q].append((waits, fn, ev))
        self._post(ev, reads, writes)
        return ev

    def flush(self):
        nc = self.nc
        with contextlib.ExitStack() as st:
            st.enter_context(nc.cleanup_on_exit())
            sems = {}
            for e in ENGS:
                sems[e] = nc.alloc_semaphore(name="s%d_%s" % (self.nflush, e))
            for k in self.dma_sems:
                sems[k] = nc.alloc_semaphore(name="s%d_%s" % (self.nflush, k))
            block = st.enter_context(nc.Block())
            final = {k: 0 for k in sems}
            for e in ENGS:
                for (_, _, ev) in self.streams[e]:
                    if ev is not None:
                        final[ev[0]] = max(final[ev[0]], ev[1])

            def run(eng_name):
                def body(eng):
                    for waits, fn, ev in self.streams[eng_name]:
                        for k, v in waits:
                            eng.wait_ge(sems[k], v)
                        ins = fn(eng)
                        if ev[0] in ENGS:
                            ins.then_inc(sems[ev[0]], 1)
                        else:
                            ins.then_inc(sems[ev[0]], 16)
                    if eng_name == "sp":
                        for k, v in final.items():
                            if v > 0:
                                eng.wait_ge(sems[k], v)
                return body
            block.tensor(run("pe"))
            block.scalar(run("act"))
            block.vector(run("dve"))
            block.gpsimd(run("pool"))
            block.sync(run("sp"))
        self.nflush += 1
        self.streams = {e: [] for e in ENGS}
        self.count = {e: 0 for e in ENGS}
        self.known = {e: {} for e in ENGS}
        self.nsem_dma = 0
        self.dma_sems = []
        for b in self.bufs:
            b.w = None
            b.r = []
            b.sem = None
            b.cnt = 0


class KB:
    def __init__(self, nc):
        self.nc = nc
        self.P = Prog(nc)
        self.dr = {}

    def din(self, name, shape, dt=F32):
        ap = self.nc.dram_tensor(name, list(shape), dt, kind="ExternalInput").ap()
        self.dr[name] = (ap, self.P.buf(name))
        return ap

    def dout(self, name, shape, dt=F32):
        ap = self.nc.dram_tensor(name, list(shape), dt, kind="ExternalOutput").ap()
        self.dr[name] = (ap, self.P.buf(name))
        return ap

    def dscr(self, name, shape, dt=F32, debug=False):
        if debug:
            return self.dout(name, shape, dt)
        ap = self.nc.dram_tensor(name, list(shape), dt).ap()
        self.dr[name] = (ap, self.P.buf(name))
        return ap

    def db(self, name):
        return self.dr[name][1]

    def mm(self, out, lhsT, rhs, start, stop, r, w):
        self.P.op("pe", lambda e: e.matmul(out, lhsT=lhsT, rhs=rhs, start=start, stop=stop), r, w)

    def act(self, out, in_, func, r, w, bias=None, scale=None):
        kw = {}
        if bias is not None:
            kw["bias"] = bias
        if scale is not None:
            kw["scale"] = scale
        self.P.op("act", lambda e: e.activation(out=out, in_=in_, func=func, **kw), r, w)

    def tt(self, eng, out, in0, in1, op, r, w):
        self.P.op(eng, lambda e: e.tensor_tensor(out=out, in0=in0, in1=in1, op=op), r, w)

    def ts(self, eng, out, in0, s1, op0, r, w, s2=None, op1=None):
        if op1 is None:
            self.P.op(eng, lambda e: e.tensor_scalar(out=out, in0=in0, scalar1=s1, scalar2=None, op0=op0), r, w)
        else:
            self.P.op(eng, lambda e: e.tensor_scalar(out=out, in0=in0, scalar1=s1, scalar2=s2, op0=op0, op1=op1), r, w)

    def stt(self, eng, out, in0, scalar, in1, op0, op1, r, w):
        self.P.op(eng, lambda e: e.scalar_tensor_tensor(out=out, in0=in0, scalar=scalar, in1=in1, op0=op0, op1=op1), r, w)

    def copy(self, eng, out, in_, r, w):
        if eng == "act":
            self.P.op("act", lambda e: e.copy(out=out, in_=in_), r, w)
        else:
            self.P.op(eng, lambda e: e.tensor_copy(out=out, in_=in_), r, w)

    def recip(self, out, in_, r, w):
        self.P.op("dve", lambda e: e.reciprocal(out=out, in_=in_), r, w)

    def memset(self, eng, ap, val, w):
        self.P.op(eng, lambda e: e.memset(ap, val), (), w)

    def dma(self, q, out, in_, r, w, sembuf=None):
        self.P.dma(q, lambda e: e.dma_start(out=out, in_=in_), r, w, sembuf)


def build_program(debug=False):
    nc = bass.Bass("TRN2", target_bir_lowering=False)
    K = KB(nc)
    P = K.P
    dbg = debug

    xt = K.din("xt", [D, NT])
    ct = K.din("ct", [128, 8, 2])
    ada_w = K.din("ada_w", [2, D, 9 * D])
    ada_b_t = K.din("ada_b_t", [128, 2, 72, 2])
    ln_g_t = K.din("ln_g_t", [128, 48])
    ln_b_t = K.din("ln_b_t", [128, 48])
    wg = K.din("wg", [2, 2, D, DFF])
    wu = K.din("wu", [2, 2, D, DFF])
    wd = K.din("wd", [2, 2, DFF, D])
    ev_w_in = K.din("ev_w_in", [D, 1952])
    ev_w_in_perm = K.din("ev_w_in_perm", [D, 1024])
    ev_w_kpe = K.din("ev_w_kpe", [D, 192])
    ev_w_uq = K.din("ev_w_uq", [256, 768])
    ev_w_uq_perm = K.din("ev_w_uq_perm", [256, 768])
    ev_w_ukv_k = K.din("ev_w_ukv_k", [128, 768])
    ev_w_ukv_v = K.din("ev_w_ukv_v", [128, 512])
    ev_small = K.din("ev_small", [128, 4])
    ev_lam_bc = K.din("ev_lam_bc", [128, 256])
    ev_w_out = K.din("ev_w_out", [D, D])
    ca = K.din("ca", [96, NT])
    sa = K.din("sa", [96, NT])
    cb = K.din("cb", [128, NT])
    sb_ = K.din("sb", [128, NT])
    od_w_in = K.din("od_w_in", [D, 2048])
    od_w_out = K.din("od_w_out", [D, D])
    od_w_pool = K.din("od_w_pool", [4, 128, 128])
    od_ps_t = K.din("od_ps_t", [128, 4])
    icnt = K.din("icnt", [128, 4, NQL])
    na_b = K.din("na_b", [3, 128, 6 * 8 * 256])

    outT = K.dout("outT", [D, NQL])

    H1 = K.dscr("H1", [D, NT], F32, dbg)
    QA = K.dscr("QA", [8, 96, NQ], BF16)
    KA = K.dscr("KA", [8, 96, NT], BF16)
    VA = K.dscr("VA", [NT, 8 * 128], BF16)
    QB = K.dscr("QB", [4, 128, NQ], BF16)
    KBs = K.dscr("KBs", [4, 128, NT], BF16)
    VB = K.dscr("VB", [NT, 512], BF16)
    MIX = K.dscr("MIX", [D, NQ], BF16, dbg)
    H2 = K.dscr("H2", [D, NQ], F32, dbg)
    H3 = K.dscr("H3", [D, NQ], F32, dbg)
    H4 = K.dscr("H4", [D, NQ], F32, dbg)
    U = K.dscr("U", [512, NQL], F32)
    QD = K.dscr("QD", [4, 128, NQL], BF16)
    KD = K.dscr("KD", [4, 128, NKP], BF16)
    KDC = K.dscr("KDC", [4, 128, 256], BF16)
    VD = K.dscr("VD", [NKP, 8 * 128], BF16)
    VDC = K.dscr("VDC", [256, 8 * 128], BF16)
    MIXD = K.dscr("MIXD", [D, NQL], BF16, dbg)
    H5 = K.dscr("H5", [D, NQL], F32, dbg)

    with contextlib.ExitStack() as top:
        uid = [0]

        def sbuf(st, name, shape, dt):
            uid[0] += 1
            return st.enter_context(nc.sbuf_tensor("%s_%d" % (name, uid[0]), list(shape), dt))

        PS = [top.enter_context(nc.psum_tensor("ps%d" % i, [128, 512], F32)) for i in range(8)]
        PSB = []
        for i in range(8):
            _b = P.buf("ps%d" % i)
            _b.excl = True
            PSB.append([_b, _b])

        MOD = sbuf(top, "MOD", [128, 2, 72, 2], F32)
        bMOD = P.buf("MOD")
        LNG = sbuf(top, "LNG", [128, 48], F32)
        LNB = sbuf(top, "LNB", [128, 48], F32)
        bLN = P.buf("LN")
        ONES = sbuf(top, "ONES", [128, 4, 128], BF16)
        bONES = P.buf("ONES")
        EPSC = sbuf(top, "EPSC", [128, 1], F32)

        with contextlib.ExitStack() as st:
            SC = sbuf(st, "SC", [128, 8, 2], F32)
            SG0 = sbuf(st, "SG0", [128, 8, 2], F32)
            ADB = sbuf(st, "ADB", [128, 2, 72, 2], F32)
            WS = [sbuf(st, "WS%d" % i, [128, 8, 1024], F32) for i in range(2)]
            bSC, bADB = P.buf("SC"), P.buf("ADB")
            bWS = [P.buf("WS0"), P.buf("WS1")]
            K.dma("sp", SC[:], ct[:, :, :], [K.db("ct")], [bSC])
            K.dma("sp", ADB[:], ada_b_t[:, :, :, :], [K.db("ada_b_t")], [bADB])
            K.dma("sp", LNG[:], ln_g_t[:, :], [], [bLN])
            K.dma("sp", LNB[:], ln_b_t[:, :], [], [bLN])
            bSG0 = P.buf("SG0")
            K.act(SG0[:], SC[:], AF.Silu, [bSC], [bSG0])
            K.copy("dve", SC[:], SG0[:], [bSG0], [bSC])
            K.memset("pool", ONES[:, 0, :], 1.0 / 1024.0, [bONES])
            K.memset("pool", ONES[:, 1, :], 1.0 / 128.0, [bONES])
            K.memset("pool", ONES[:, 2, :], 1.0 / 256.0, [bONES])
            K.memset("pool", ONES[:, 3, :], 1.0, [bONES])
            K.memset("pool", EPSC[:], 0.0, [bONES])
            n = 0
            for l in range(2):
                psA = PS[l]
                for s in range(9):
                    wsl = WS[n % 2]
                    bw = bWS[n % 2]
                    n += 1
                    src = ada_w[l, :, s * 1024:(s + 1) * 1024].rearrange("(c p) n -> p c n", p=128)
                    for kc in range(8):
                        K.dma("sp" if kc % 2 == 0 else "act", wsl[:, kc, :], src[:, kc, :], [], [bw])
                    for m in range(8):
                        col = (s * 8 + m) * 2
                        for kc in range(8):
                            K.mm(psA[:, col:col + 2], wsl[:, kc, m * 128:(m + 1) * 128], SC[:, kc, :],
                                 kc == 0, kc == 7, [bw, bSC], [PSB[l][0]])
                K.tt("dve", MOD[:, l, :, :], psA[:, 0:144].rearrange("p (a b) -> p a b", b=2), ADB[:, l, :, :],
                     ALU.add, [PSB[l][0], bADB], [bMOD])
                for i in (1, 4, 7):
                    K.ts("dve", MOD[:, l, i * 8:(i + 1) * 8, :], MOD[:, l, i * 8:(i + 1) * 8, :], 1.0, ALU.add,
                         [bMOD], [bMOD])
                for i, f in ((2, 0.5 / DN_ALPHA), (5, 1.0 / DN_ALPHA), (8, 0.5 / DN_ALPHA)):
                    K.ts("dve", MOD[:, l, i * 8:(i + 1) * 8, :], MOD[:, l, i * 8:(i + 1) * 8, :], f, ALU.mult,
                         [bMOD], [bMOD])
            P.flush()

        def modap(l, i, m, col):
            return MOD[:, l, i * 8 + m, col:col + 1]

        def load_w(st, name, src2d, kc, ncols, eng_q="pool"):
            t = sbuf(st, name, [128, kc, ncols], BF16)
            b = P.buf(name)
            src = src2d.rearrange("(c p) n -> p c n", p=128)
            for c in range(kc):
                K.dma(eng_q, t[:, c, :], src[:, c, :], [], [b])
            return t, b

        def layer_norm(Z, bZ, ZB, bZB, ZQ, bZQ, ST, bST, Hout, bH, gcol, width, psb_idx, eps):
            w_ = width
            for m in range(8):
                K.copy("pool", ZB[:, m, :w_], Z[:, m, :w_], [bZ], [bZB])
                K.act(ZQ[:, m, :w_], Z[:, m, :w_], AF.Square, [bZ], [bZQ])
            pm, bpm = PS[psb_idx][:, 0:w_], PSB[psb_idx][0]
            pq, bpq = PS[psb_idx + 1][:, 0:w_], PSB[psb_idx + 1][0]
            for m in range(8):
                K.mm(pm, ONES[:, 0, :], ZB[:, m, :w_], m == 0, m == 7, [bZB, bONES], [bpm])
            for m in range(8):
                K.mm(pq, ONES[:, 0, :], ZQ[:, m, :w_], m == 0, m == 7, [bZQ, bONES], [bpq])
            mean, m2, rstd = ST[:, 0, :w_], ST[:, 1, :w_], ST[:, 2, :w_]
            K.copy("act", mean, pm, [bpm], [bST])
            K.tt("pool", m2, mean, mean, ALU.mult, [bST], [bST])
            K.stt("dve", rstd, pq, eps, m2, ALU.add, ALU.subtract, [bpq, bST], [bST])
            K.act(rstd, rstd, AF.Sqrt, [bST], [bST])
            K.recip(rstd, rstd, [bST], [bST])
            for m in range(8):
                K.tt("dve", Z[:, m, :w_], Z[:, m, :w_], mean, ALU.subtract, [bZ, bST], [bZ])
                K.tt("pool", Z[:, m, :w_], Z[:, m, :w_], rstd, ALU.mult, [bZ, bST], [bZ])
                K.act(Hout[:, m, :w_], Z[:, m, :w_], AF.Identity, [bZ, bLN], [bH],
                      bias=LNB[:, gcol * 8 + m:gcol * 8 + m + 1], scale=LNG[:, gcol * 8 + m:gcol * 8 + m + 1])

        def ffn_phase(l, f, src, srcname, dst, dstname, tiles, src_col, dst_col):
            with contextlib.ExitStack() as st:
                WG, bWG = load_w(st, "WG", wg[l, f], 8, DFF)
                WU, bWU = load_w(st, "WU", wu[l, f], 8, DFF)
                WD, bWD = load_w(st, "WD", wd[l, f], NFF, D)
                Hs = [sbuf(st, "Hs%d" % i, [128, 8, T], F32) for i in range(2)]
                bHs = [P.buf("H0"), P.buf("H1")]
                XMs = [sbuf(st, "XM%d" % i, [128, 8, T], BF16) for i in range(2)]
                bXMs = [P.buf("XM0"), P.buf("XM1")]
                HID = sbuf(st, "HID", [128, NFF, T], BF16)
                bHID = P.buf("HID")
                Zs = [sbuf(st, "Z%d" % i, [128, 8, T], F32) for i in range(2)]
                bZs = [P.buf("Z0"), P.buf("Z1")]
                ZBs = [sbuf(st, "ZB%d" % i, [128, 8, T], BF16) for i in range(2)]
                bZBs = [P.buf("ZB0"), P.buf("ZB1")]
                ZQs = [sbuf(st, "ZQ%d" % i, [128, 8, T], BF16) for i in range(2)]
                bZQs = [P.buf("ZQ0"), P.buf("ZQ1")]
                SG = [sbuf(st, "SG%d" % i, [128, T], F32) for i in range(2)]
                bSG = [P.buf("SG0"), P.buf("SG1")]
                ST = sbuf(st, "ST", [128, 3, T], F32)
                bST = P.buf("ST")
                NH = sbuf(st, "NH", [128, T], F32)
                bNH = P.buf("NH")
                K.memset("pool", NH[:], -0.5, [bNH])
                srcv = src.rearrange("(c p) t -> p c t", p=128)
                dstv = dst.rearrange("(c p) t -> p c t", p=128)
                gcol = l * 3 + (0 if f == 0 else 2)
                mi = 0 if f == 0 else 6
                eps = LN_EPS / (DN_ALPHA ** 2)
                n = len(tiles)

                def load(idx):
                    c0 = src_col(tiles[idx])
                    K.dma("sp", Hs[idx % 2][:], srcv[:, :, c0:c0 + T], [K.db(srcname)], [bHs[idx % 2]])

                def s1a(idx):
                    ti = tiles[idx]
                    H, bH, XM, bXM = Hs[idx % 2], bHs[idx % 2], XMs[idx % 2], bXMs[idx % 2]
                    col = 1 if ti == 32 else 0
                    for kc in range(8):
                        K.act(XM[:, kc, :], H[:, kc, :], AF.Identity, [bH, bMOD], [bXM],
                              bias=modap(l, mi, kc, col), scale=modap(l, mi + 1, kc, col))

                def s1b(idx):
                    XM, bXM = XMs[idx % 2], bXMs[idx % 2]
                    for j in range(NFF):
                        pg, bpg = PS[j % 2][:, 0:T], PSB[j % 2][0]
                        pu, bpu = PS[2 + j % 2][:, 0:T], PSB[2 + j % 2][0]
                        for kc in range(8):
                            K.mm(pg, WG[:, kc, j * 128:(j + 1) * 128], XM[:, kc, :], kc == 0, kc == 7, [bWG, bXM], [bpg])
                        for kc in range(8):
                            K.mm(pu, WU[:, kc, j * 128:(j + 1) * 128], XM[:, kc, :], kc == 0, kc == 7, [bWU, bXM], [bpu])
                        sg, bsg = SG[j % 2], bSG[j % 2]
                        K.act(sg[:], pg, AF.Silu, [bpg], [bsg])
                        K.tt("dve", HID[:, j, :], sg[:], pu, ALU.mult, [bsg, bpu], [bHID])

                def s2(idx):
                    ti = tiles[idx]
                    H, bH, Z, bZ = Hs[idx % 2], bHs[idx % 2], Zs[idx % 2], bZs[idx % 2]
                    ZB, bZB, ZQ, bZQ = ZBs[idx % 2], bZBs[idx % 2], ZQs[idx % 2], bZQs[idx % 2]
                    col = 1 if ti == 32 else 0
                    for m in range(8):
                        pd, bpd = PS[4 + m % 2][:, 0:T], PSB[4 + m % 2][0]
                        for j in range(NFF):
                            K.mm(pd, WD[:, j, m * 128:(m + 1) * 128], HID[:, j, :], j == 0, j == NFF - 1, [bWD, bHID], [bpd])
                        K.stt("dve", Z[:, m, :], pd, modap(l, mi + 2, m, col), H[:, m, :], ALU.mult, ALU.add, [bpd, bH, bMOD], [bZ])
                        K.copy("pool", ZB[:, m, :], Z[:, m, :], [bZ], [bZB])
                        K.act(ZQ[:, m, :], Z[:, m, :], AF.Square, [bZ], [bZQ])

                def s3(idx):
                    ti = tiles[idx]
                    Z, bZ = Zs[idx % 2], bZs[idx % 2]
                    ZB, bZB, ZQ, bZQ = ZBs[idx % 2], bZBs[idx % 2], ZQs[idx % 2], bZQs[idx % 2]
                    pm, bpm = PS[6][:, 0:T], PSB[6][0]
                    pq, bpq = PS[7][:, 0:T], PSB[7][0]
                    for m in range(8):
                        K.mm(pm, ONES[:, 0, :], ZB[:, m, :], m == 0, m == 7, [bZB, bONES], [bpm])
                    for m in range(8):
                        K.mm(pq, ONES[:, 0, :], ZQ[:, m, :], m == 0, m == 7, [bZQ, bONES], [bpq])
                    mean, var, rstd = ST[:, 0, :], ST[:, 1, :], ST[:, 2, :]
                    K.copy("dve", mean, pm, [bpm], [bST])
                    K.copy("dve", var, pq, [bpq], [bST])
                    K.tt("pool", rstd, mean, mean, ALU.mult, [bST], [bST])
                    K.tt("pool", var, var, rstd, ALU.subtract, [bST], [bST])
                    K.ts("pool", var, var, eps, ALU.add, [bST], [bST])
                    K.tt("pool", rstd, var, NH[:], ALU.pow, [bST, bNH], [bST])
                    for m in range(8):
                        K.tt("pool", Z[:, m, :], Z[:, m, :], mean, ALU.subtract, [bZ, bST], [bZ])
                        K.tt("pool", Z[:, m, :], Z[:, m, :], rstd, ALU.mult, [bZ, bST], [bZ])
                        K.ts("pool", Z[:, m, :], Z[:, m, :], LNG[:, gcol * 8 + m:gcol * 8 + m + 1], ALU.mult, [bZ, bLN], [bZ],
                             s2=LNB[:, gcol * 8 + m:gcol * 8 + m + 1], op1=ALU.add)
                    c1 = dst_col(ti)
                    K.dma("sp", dstv[:, :, c1:c1 + T], Z[:], [bZ], [K.db(dstname)])

                load(0)
                if n > 1:
                    load(1)
                s1a(0)
                s1b(0)
                s2(0)
                if n > 1:
                    s1a(1)
                for idx in range(n):
                    if idx + 2 < n:
                        load(idx + 2)
                    if idx + 1 < n:
                        s1b(idx + 1)
                        s2(idx + 1)
                    if idx + 2 < n:
                        s1a(idx + 2)
                    s3(idx)
                P.flush()

        NTL = int(os.environ.get("NTILES", "33"))
        if NTL > 0:
            ffn_phase(0, 0, xt, "xt", H1, "H1", TILES_ALL[:NTL], lambda ti: ti * 256, lambda ti: ti * 256)

        class Rot:
            def __init__(self, st, name, n, shape, dt):
                self.t = [sbuf(st, "%s%d" % (name, i), shape, dt) for i in range(n)]
                self.b = [P.buf("%s%d" % (name, i)) for i in range(n)]
                self.i = 0

            def get(self):
                k = self.i % len(self.t)
                self.i += 1
                return self.t[k], self.b[k]

        bank_ctr = [0]

        def bank(lo=0, hi=8):
            k = lo + bank_ctr[0] % (hi - lo)
            bank_ctr[0] += 1
            return PS[k], PSB[k][0]

        PHASES = os.environ.get("PHASES", "ABCDEFGHIJK")

        def proj0_phase():
            with contextlib.ExitStack() as st:
                WIN, bWIN = load_w(st, "WIN", ev_w_in, 8, 1952)
                WPM, bWPM = load_w(st, "WPM", ev_w_in_perm, 8, 1024)
                WKPE, bWKPE = load_w(st, "WKPE", ev_w_kpe, 8, 192)
                WUQ, bWUQ = load_w(st, "WUQ", ev_w_uq, 2, 768)
                WUQP, bWUQP = load_w(st, "WUQP", ev_w_uq_perm, 2, 768)
                WUK, bWUK = load_w(st, "WUK", ev_w_ukv_k, 1, 768)
                WUV, bWUV = load_w(st, "WUV", ev_w_ukv_v, 1, 512)
                SM = sbuf(st, "SM", [128, 4], F32)
                bSM = P.buf("SM")
                K.dma("sp", SM[:], ev_small[:, :], [], [bSM])
                Hs = [sbuf(st, "Hp%d" % i, [128, 8, T], F32) for i in range(2)]
                bHs = [P.buf("Hp0"), P.buf("Hp1")]
                TAB = [sbuf(st, "TAB%d" % i, [128, 4, T], F32) for i in range(2)]
                bTAB = [P.buf("TAB0"), P.buf("TAB1")]
                XM = sbuf(st, "XMp", [128, 8, T], BF16)
                bXM = P.buf("XMp")
                TMP = Rot(st, "TMP", 4, [128, T], F32)
                OB = Rot(st, "OB", 6, [128, T], BF16)
                OV = Rot(st, "OV", 2, [128, 512], BF16)
                VAt = sbuf(st, "VAt", [128, 2, 8, 128], BF16)
                bVAt = [P.buf("VAt0"), P.buf("VAt1")]
                K.memset("pool", VAt[:, :, :, 64:128], 1.0, bVAt)
                KVL = sbuf(st, "KVL", [128, T], F32)
                SQ = sbuf(st, "SQ", [128, 2, T], BF16)
                RS = sbuf(st, "RS", [128, T], F32)
                KVN = sbuf(st, "KVN", [128, T], BF16)
                KPE = sbuf(st, "KPE", [96, T], F32)
                QL = sbuf(st, "QL", [128, 2, T], F32)
                QN = sbuf(st, "QN", [128, 2, T], BF16)
                bKVL, bSQ, bRS, bKVN, bKPE, bQL, bQN = [P.buf(n) for n in "KVL SQ RS KVN KPE QL QN".split()]
                srcv = H1.rearrange("(c p) t -> p c t", p=128)

                def load(idx):
                    ti = TILES_ALL[idx]
                    c0 = ti * 256
                    K.dma("sp", Hs[idx % 2][:], srcv[:, :, c0:c0 + T], [K.db("H1")], [bHs[idx % 2]])
                    tb_, btb = TAB[idx % 2], bTAB[idx % 2]
                    K.dma("sp", tb_[0:96, 0, :], ca[:, c0:c0 + T], [], [btb])
                    K.dma("sp", tb_[0:96, 1, :], sa[:, c0:c0 + T], [], [btb])
                    K.dma("sp", tb_[:, 2, :], cb[:, c0:c0 + T], [], [btb])
                    K.dma("sp", tb_[:, 3, :], sb_[:, c0:c0 + T], [], [btb])

                def rope_out(psa, bpa, psb, bpb, c_ap, s_ap, btb, rows, dst, dstname):
                    t1, b1 = TMP.get()
                    t2, b2 = TMP.get()
                    K.tt("dve", t1[0:rows, :], psa, c_ap, ALU.mult, [bpa, btb], [b1])
                    K.tt("dve", t2[0:rows, :], psb, s_ap, ALU.mult, [bpb, btb], [b2])
                    ob, bob = OB.get()
                    K.tt("pool", ob[0:rows, :], t1[0:rows, :], t2[0:rows, :], ALU.add, [b1, b2], [bob])
                    K.dma("sp", dst, ob[0:rows, :], [bob], [K.db(dstname)])

                load(0)
                for idx, ti in enumerate(TILES_ALL):
                    if idx + 1 < len(TILES_ALL):
                        load(idx + 1)
                    H, bH = Hs[idx % 2], bHs[idx % 2]
                    tb_, btb = TAB[idx % 2], bTAB[idx % 2]
                    col = 1 if ti == 32 else 0
                    isq = ti in TILES_Q
                    t0 = ti * 256
                    q0 = qcol(ti)
                    for kc in range(8):
                        K.act(XM[:, kc, :], H[:, kc, :], AF.Identity, [bH, bMOD], [bXM],
                              bias=modap(0, 3, kc, col), scale=modap(0, 4, kc, col))
                    for (isneeded, wc0, pc0, dst3, dname, dcol) in ((True, 928, 512, KBs, "KBs", t0), (isq, 256, 0, QB, "QB", q0)):
                        if not isneeded:
                            continue
                        for h in range(4):
                            p1, bp1 = bank()
                            p2, bp2 = bank()
                            for kc in range(8):
                                K.mm(p1[:, 0:T], WIN[:, kc, wc0 + h * 128:wc0 + (h + 1) * 128], XM[:, kc, :], kc == 0, kc == 7, [bWIN, bXM], [bp1])
                            for kc in range(8):
                                K.mm(p2[:, 0:T], WPM[:, kc, pc0 + h * 128:pc0 + (h + 1) * 128], XM[:, kc, :], kc == 0, kc == 7, [bWPM, bXM], [bp2])
                            rope_out(p1[:, 0:T], bp1, p2[:, 0:T], bp2, tb_[:, 2, :], tb_[:, 3, :], btb, 128,
                                     dst3[h, :, dcol:dcol + T], dname)
                    for tb in range(2):
                        pv, bpv = bank()
                        for kc in range(8):
                            K.mm(pv[:, :], XM[:, kc, tb * 128:(tb + 1) * 128], WIN[:, kc, 1440:1952], kc == 0, kc == 7, [bWIN, bXM], [bpv])
                        ov, bov = OV.get()
                        K.copy("act", ov[:], pv[:, :], [bpv], [bov])
                        K.dma("sp", VB[t0 + tb * 128:t0 + (tb + 1) * 128, :], ov[:], [bov], [K.db("VB")])
                    pk, bpk = bank()
                    for kc in range(8):
                        K.mm(pk[:, 0:T], WIN[:, kc, 768:896], XM[:, kc, :], kc == 0, kc == 7, [bWIN, bXM], [bpk])
                    K.copy("dve", KVL[:], pk[:, 0:T], [bpk], [bKVL])
                    K.act(SQ[:, 0, :], pk[:, 0:T], AF.Square, [bpk], [bSQ])
                    pss, bpss = bank()
                    K.mm(pss[:, 0:T], ONES[:, 1, :], SQ[:, 0, :], True, True, [bSQ, bONES], [bpss])
                    K.ts("dve", RS[:], pss[:, 0:T], 1e-6, ALU.add, [bpss], [bRS])
                    K.act(RS[:], RS[:], AF.Sqrt, [bRS], [bRS])
                    K.recip(RS[:], RS[:], [bRS], [bRS])
                    K.tt("dve", KVL[:], KVL[:], RS[:], ALU.mult, [bKVL, bRS], [bKVL])
                    K.act(KVN[:], KVL[:], AF.Identity, [bKVL, bSM], [bKVN], scale=SM[:, 2:3])
                    pp1, bpp1 = bank()
                    pp2, bpp2 = bank()
                    for kc in range(8):
                        K.mm(pp1[0:96, 0:T], WKPE[:, kc, 0:96], XM[:, kc, :], kc == 0, kc == 7, [bWKPE, bXM], [bpp1])
                    for kc in range(8):
                        K.mm(pp2[0:96, 0:T], WKPE[:, kc, 96:192], XM[:, kc, :], kc == 0, kc == 7, [bWKPE, bXM], [bpp2])
                    t1, b1 = TMP.get()
                    t2, b2 = TMP.get()
                    K.tt("dve", t1[0:96, :], pp1[0:96, 0:T], tb_[0:96, 0, :], ALU.mult, [bpp1, btb], [b1])
                    K.tt("dve", t2[0:96, :], pp2[0:96, 0:T], tb_[0:96, 1, :], ALU.mult, [bpp2, btb], [b2])
                    K.tt("pool", KPE[:], t1[0:96, :], t2[0:96, :], ALU.add, [b1, b2], [bKPE])
                    for h in range(8):
                        pkh, bpkh = bank()
                        K.mm(pkh[0:96, 0:T], WUK[:, 0, 96 * h:96 * h + 96], KVN[:], True, True, [bWUK, bKVN], [bpkh])
                        ob, bob = OB.get()
                        K.tt("dve", ob[0:96, :], pkh[0:96, 0:T], KPE[:], ALU.add, [bpkh, bKPE], [bob])
                        K.dma("sp", KA[h, :, t0:t0 + T], ob[0:96, :], [bob], [K.db("KA")])
                    for tb in range(2):
                        pv, bpv = bank()
                        K.mm(pv[:, :], KVN[:, tb * 128:(tb + 1) * 128], WUV[:, 0, :], True, True, [bWUV, bKVN], [bpv])
                        K.copy("act", VAt[:, tb, :, 0:64], pv[:, :].rearrange("p (h c) -> p h c", c=64), [bpv], [bVAt[tb]])
                        K.dma("sp", VA[t0 + tb * 128:t0 + (tb + 1) * 128, :].rearrange("p (h c) -> p h c", c=128),
                              VAt[:, tb, :, :], [bVAt[tb]], [K.db("VA")])
                    if isq:
                        for c in range(2):
                            pq_, bpq_ = bank()
                            for kc in range(8):
                                K.mm(pq_[:, 0:T], WIN[:, kc, c * 128:(c + 1) * 128], XM[:, kc, :], kc == 0, kc == 7, [bWIN, bXM], [bpq_])
                            K.copy("dve", QL[:, c, :], pq_[:, 0:T], [bpq_], [bQL])
                            K.act(SQ[:, c, :], pq_[:, 0:T], AF.Square, [bpq_], [bSQ])
                        pss, bpss = bank()
                        for c in range(2):
                            K.mm(pss[:, 0:T], ONES[:, 2, :], SQ[:, c, :], c == 0, c == 1, [bSQ, bONES], [bpss])
                        K.ts("dve", RS[:], pss[:, 0:T], 1e-6, ALU.add, [bpss], [bRS])
                        K.act(RS[:], RS[:], AF.Sqrt, [bRS], [bRS])
                        K.recip(RS[:], RS[:], [bRS], [bRS])
                        for c in range(2):
                            K.tt("dve", QL[:, c, :], QL[:, c, :], RS[:], ALU.mult, [bQL, bRS], [bQL])
                            K.act(QN[:, c, :], QL[:, c, :], AF.Identity, [bQL, bSM], [bQN], scale=SM[:, c:c + 1])
                        for h in range(8):
                            p1, bp1 = bank()
                            p2, bp2 = bank()
                            for c in range(2):
                                K.mm(p1[0:96, 0:T], WUQ[:, c, 96 * h:96 * h + 96], QN[:, c, :], c == 0, c == 1, [bWUQ, bQN], [bp1])
                            for c in range(2):
                                K.mm(p2[0:96, 0:T], WUQP[:, c, 96 * h:96 * h + 96], QN[:, c, :], c == 0, c == 1, [bWUQP, bQN], [bp2])
                            rope_out(p1[0:96, 0:T], bp1, p2[0:96, 0:T], bp2, tb_[0:96, 0, :], tb_[0:96, 1, :], btb, 96,
                                     QA[h, :, q0:q0+T], "QA")
                P.flush()

        if "C" in PHASES:
            proj0_phase()

        QTILES = [(i * 512, 512, 0, 66) for i in range(8)] + [(4096, 256, 0, 66), (4352, 256, 64, 66)]

        def att0_mla_phase(heads):
            with contextlib.ExitStack() as st:
                Kt = [sbuf(st, "Kt%d" % i, [96, NT], BF16) for i in range(2)]
                Vt = [sbuf(st, "Vt%d" % i, [128, 66, 128], BF16) for i in range(2)]
                Qt = [sbuf(st, "Qt%d" % i, [96, NQ], BF16) for i in range(2)]
                bKt = [P.buf("Kt%d" % i) for i in range(2)]
                bVt = [P.buf("Vt%d" % i) for i in range(2)]
                bQt = [P.buf("Qt%d" % i) for i in range(2)]
                PT = Rot(st, "PT", 6, [128, 512], BF16)
                RR = sbuf(st, "RR", [64, 512], F32)
                bRR = P.buf("RR")
                OO = Rot(st, "OO", 2, [64, 512], BF16)
                VAv = VA.rearrange("(kt p) (h c) -> p kt h c", p=128, h=8)

                def load(i):
                    h = heads[i]
                    K.dma("sp", Kt[i % 2][:], KA[h, :, :], [K.db("KA")], [bKt[i % 2]])
                    K.dma("sp", Qt[i % 2][:], QA[h, :, :], [K.db("QA")], [bQt[i % 2]])
                    for kq in range(6):
                        K.dma("act" if kq % 2 else "sp", Vt[i % 2][:, kq * 11:(kq + 1) * 11, :], VAv[:, kq * 11:(kq + 1) * 11, h, :], [K.db("VA")], [bVt[i % 2]])
                load(0)
                for i, h in enumerate(heads):
                    if i + 1 < len(heads):
                        load(i + 1)
                    kt_, vt_, qt_ = Kt[i % 2], Vt[i % 2], Qt[i % 2]
                    bk, bv, bq = bKt[i % 2], bVt[i % 2], bQt[i % 2]
                    steps = [(qi, q0, N, kt, kt == klo, kt == khi - 1) for qi, (q0, N, klo, khi) in enumerate(QTILES) for kt in range(klo, khi)]
                    Sq = {}

                    def emitS(s):
                        qi, q0, N, kt, first, last = steps[s]
                        S, bS = bank(0, 6)
                        K.mm(S[:, 0:N], kt_[:, kt * 128:(kt + 1) * 128], qt_[:, q0:q0 + N], True, True, [bk, bq], [bS])
                        Sq[s] = (S, bS)
                    DEPTH = 3
                    for s in range(min(DEPTH, len(steps))):
                        emitS(s)
                    for s in range(len(steps)):
                        if s + DEPTH < len(steps):
                            emitS(s + DEPTH)
                        qi, q0, N, kt, first, last = steps[s]
                        S, bS = Sq.pop(s)
                        O, bO = PS[6 + qi % 2], PSB[6 + qi % 2][0]
                        pt, bpt = PT.get()
                        K.act(pt[:, 0:N], S[:, 0:N], AF.Exp, [bS], [bpt], scale=A_SCALE)
                        K.mm(O[:, 0:N], vt_[:, kt, :], pt[:, 0:N], first, last, [bv, bpt], [bO])
                        if last:
                            K.recip(RR[:, 0:N], O[64:128, 0:N], [bO], [bRR])
                            oo, boo = OO.get()
                            K.tt("dve", oo[:, 0:N], O[0:64, 0:N], RR[:, 0:N], ALU.mult, [bO, bRR], [boo])
                            K.dma("sp", MIX[64 * h:64 * h + 64, q0:q0 + N], oo[:, 0:N], [boo], [K.db("MIX")])
                P.flush()

        def att0_diff_phase(heads):
            with contextlib.ExitStack() as st:
                Kt = [sbuf(st, "Kd%d" % i, [128, NT], BF16) for i in range(2)]
                Vt = [sbuf(st, "Vd%d" % i, [128, 66, 128], BF16) for i in range(2)]
                Qt = [sbuf(st, "Qd%d" % i, [128, 2, NQ], BF16) for i in range(2)]
                ACC = [sbuf(st, "ACC%d" % i, [128, 512], F32) for i in range(2)]
                bACC = [P.buf("ACC0"), P.buf("ACC1")]
                ONESF = sbuf(st, "ONESF", [128, 128], F32)
                bONESF = P.buf("ONESF")
                K.memset("pool", ONESF[:], 1.0, [bONESF])
                bKt = [P.buf("Kd%d" % i) for i in range(2)]
                bVt = [P.buf("Vd%d" % i) for i in range(2)]
                bQt = [P.buf("Qd%d" % i) for i in range(2)]
                PT = Rot(st, "PTd", 4, [128, 512], BF16)
                R1 = sbuf(st, "R1", [128, 512], F32)
                R2 = sbuf(st, "R2", [128, 512], F32)
                OA = sbuf(st, "OA", [128, 512], F32)
                OBd = sbuf(st, "OBd", [128, 512], F32)
                SQd = sbuf(st, "SQd", [128, 512], BF16)
                bR1, bR2, bOA, bOBd, bSQd = [P.buf(n) for n in "R1 R2 OA OBd SQd".split()]
                OO = Rot(st, "OOd", 2, [128, 512], BF16)
                LV = sbuf(st, "LV", [128, 256], F32)
                LP = sbuf(st, "LP", [128, 2, 64], F32)
                LS = sbuf(st, "LS", [128, 4], F32)
                GS = sbuf(st, "GS", [128, 4], F32)
                bLV, bLS, bGS = P.buf("LV"), P.buf("LS"), P.buf("GS")
                K.dma("sp", LV[:], ev_lam_bc[:, :], [], [bLV])
                K.dma("sp", GS[:], ev_small[:, :], [], [bGS])
                K.tt("dve", LP[:, 0, :], LV[:, 0:64], LV[:, 64:128], ALU.mult, [bLV], [bLV])
                K.tt("dve", LP[:, 1, :], LV[:, 128:192], LV[:, 192:256], ALU.mult, [bLV], [bLV])
                P.op("dve", lambda e: e.reduce_sum(out=LS[:, 0:2], in_=LP[:, :, :], axis=AX.X), [bLV], [bLS])
                K.act(LS[:, 0:2], LS[:, 0:2], AF.Exp, [bLS], [bLS])
                K.tt("dve", LS[:, 2:3], LS[:, 1:2], LS[:, 0:1], ALU.subtract, [bLS], [bLS])
                K.ts("dve", LS[:, 2:3], LS[:, 2:3], -LAM_INIT0, ALU.add, [bLS], [bLS])
                K.ts("dve", GS[:, 3:4], GS[:, 3:4], 1.0 - LAM_INIT0, ALU.mult, [bGS], [bGS])
                VBv = VB.rearrange("(kt p) (h c) -> p kt h c", p=128, h=4)

                def load(i):
                    h = heads[i]
                    K.dma("sp", Kt[i % 2][:], KBs[h, :, :], [K.db("KBs")], [bKt[i % 2]])
                    K.dma("sp", Qt[i % 2][0:64, 0, :], QB[h, 0:64, :], [K.db("QB")], [bQt[i % 2]])
                    K.dma("sp", Qt[i % 2][64:128, 1, :], QB[h, 64:128, :], [K.db("QB")], [bQt[i % 2]])
                    for kq in range(6):
                        K.dma("act" if kq % 2 else "sp", Vt[i % 2][:, kq * 11:(kq + 1) * 11, :], VBv[:, kq * 11:(kq + 1) * 11, h, :], [K.db("VB")], [bVt[i % 2]])
                for i_ in range(2):
                    K.memset("pool", Qt[i_][64:128, 0, :], 0.0, [bQt[i_]])
                    K.memset("pool", Qt[i_][0:64, 1, :], 0.0, [bQt[i_]])
                load(0)
                for i, h in enumerate(heads):
                    if i + 1 < len(heads):
                        load(i + 1)
                    kt_, vt_, qt_ = Kt[i % 2], Vt[i % 2], Qt[i % 2]
                    bk, bv, bq = bKt[i % 2], bVt[i % 2], bQt[i % 2]
                    O1, bO1 = PS[4], PSB[4][0]
                    L1, bL1 = PS[5], PSB[5][0]
                    O2, bO2 = PS[6], PSB[6][0]
                    L2, bL2 = PS[7], PSB[7][0]
                    steps = [(qi, q0, N, kt, mp, kt == klo, kt == khi - 1) for qi, (q0, N, klo, khi) in enumerate(QTILES)
                             for kt in range(klo, khi) for mp in range(2)]
                    Sq = {}

                    def emitS(s):
                        qi, q0, N, kt, mp, first, last = steps[s]
                        S, bS = bank(0, 4)
                        K.mm(S[:, 0:N], kt_[:, kt * 128:(kt + 1) * 128], qt_[:, mp, q0:q0 + N], True, True, [bk, bq], [bS])
                        Sq[s] = (S, bS)
                    DEPTH = 2
                    for s in range(min(DEPTH, len(steps))):
                        emitS(s)
                    for s in range(len(steps)):
                        if s + DEPTH < len(steps):
                            emitS(s + DEPTH)
                        qi, q0, N, kt, mp, first, last = steps[s]
                        S, bS = Sq.pop(s)
                        O_, bO_, L_, bL_ = ((O1, bO1, L1, bL1), (O2, bO2, L2, bL2))[mp]
                        pt, bpt = PT.get()
                        K.act(pt[:, 0:N], S[:, 0:N], AF.Exp, [bS], [bpt], scale=B_SCALE)
                        K.mm(O_[:, 0:N], vt_[:, kt, :], pt[:, 0:N], first, last, [bv, bpt], [bO_])
                        aeng = "dve" if mp == 0 else "pool"
                        if first:
                            K.copy(aeng, ACC[mp][:, 0:N], pt[:, 0:N], [bpt], [bACC[mp]])
                        else:
                            K.tt(aeng, ACC[mp][:, 0:N], ACC[mp][:, 0:N], pt[:, 0:N], ALU.add, [bpt, bACC[mp]], [bACC[mp]])
                        if not (last and mp == 1):
                            continue
                        K.mm(L1[:, 0:N], ONESF[:], ACC[0][:, 0:N], True, True, [bONESF, bACC[0]], [bL1])
                        K.mm(L2[:, 0:N], ONESF[:], ACC[1][:, 0:N], True, True, [bONESF, bACC[1]], [bL2])
                        K.recip(R1[:, 0:N], L1[:, 0:N], [bL1], [bR1])
                        K.recip(R2[:, 0:N], L2[:, 0:N], [bL2], [bR2])
                        K.tt("dve", OA[:, 0:N], O1[:, 0:N], R1[:, 0:N], ALU.mult, [bO1, bR1], [bOA])
                        K.tt("dve", OBd[:, 0:N], O2[:, 0:N], R2[:, 0:N], ALU.mult, [bO2, bR2], [bOBd])
                        K.stt("dve", OA[:, 0:N], OBd[:, 0:N], LS[:, 2:3], OA[:, 0:N], ALU.mult, ALU.add, [bOBd, bOA, bLS], [bOA])
                        K.act(SQd[:, 0:N], OA[:, 0:N], AF.Square, [bOA], [bSQd])
                        K.mm(L1[:, 0:N], ONES[:, 1, :], SQd[:, 0:N], True, True, [bONES, bSQd], [bL1])
                        K.ts("dve", R1[:, 0:N], L1[:, 0:N], 1e-5, ALU.add, [bL1], [bR1])
                        K.act(R1[:, 0:N], R1[:, 0:N], AF.Sqrt, [bR1], [bR1])
                        K.recip(R1[:, 0:N], R1[:, 0:N], [bR1], [bR1])
                        K.tt("pool", OA[:, 0:N], OA[:, 0:N], R1[:, 0:N], ALU.mult, [bOA, bR1], [bOA])
                        oo, boo = OO.get()
                        K.act(oo[:, 0:N], OA[:, 0:N], AF.Identity, [bOA, bGS], [boo], scale=GS[:, 3:4])
                        K.dma("sp", MIX[512 + 128 * h:512 + 128 * (h + 1), q0:q0 + N], oo[:, 0:N], [boo], [K.db("MIX")])
                P.flush()

        if "D" in PHASES:
            att0_mla_phase([0, 1, 2, 3])
            att0_mla_phase([4, 5, 6, 7])
            att0_diff_phase([0, 1])
            att0_diff_phase([2, 3])

        def out_phase(l, wout, mix, mixname, mixcol, hin, hinname, hincol, hout, houtname, houtcol, tiles):
            with contextlib.ExitStack() as st:
                WO, bWO = load_w(st, "WO", wout, 8, D)
                Hs = [sbuf(st, "Ho%d" % i, [128, 8, T], F32) for i in range(2)]
                MX = [sbuf(st, "MX%d" % i, [128, 8, T], BF16) for i in range(2)]
                bHs = [P.buf("Ho0"), P.buf("Ho1")]
                bMX = [P.buf("MX0"), P.buf("MX1")]
                Z = sbuf(st, "Zo", [128, 8, T], F32)
                ZB = sbuf(st, "ZBo", [128, 8, T], BF16)
                ZQ = sbuf(st, "ZQo", [128, 8, T], BF16)
                ST = sbuf(st, "STo", [128, 3, T], F32)
                HO = sbuf(st, "HOo", [128, 8, T], F32)
                bZ, bZB, bZQ, bST, bHO = [P.buf(n) for n in "Zo ZBo ZQo STo HOo".split()]
                hv = hin.rearrange("(c p) t -> p c t", p=128)
                mv = mix.rearrange("(c p) t -> p c t", p=128)
                ov = hout.rearrange("(c p) t -> p c t", p=128)

                def load(idx):
                    c0 = hincol(tiles[idx])
                    K.dma("sp", Hs[idx % 2][:], hv[:, :, c0:c0 + T], [K.db(hinname)], [bHs[idx % 2]])
                    c0 = mixcol(tiles[idx])
                    K.dma("sp", MX[idx % 2][:], mv[:, :, c0:c0 + T], [K.db(mixname)], [bMX[idx % 2]])
                load(0)
                for idx, ti in enumerate(tiles):
                    if idx + 1 < len(tiles):
                        load(idx + 1)
                    H, bH, M_, bM = Hs[idx % 2], bHs[idx % 2], MX[idx % 2], bMX[idx % 2]
                    col = 1 if ti == 32 else 0
                    for m in range(8):
                        py, bpy = bank(0, 6)
                        for kc in range(8):
                            K.mm(py[:, 0:T], WO[:, kc, m * 128:(m + 1) * 128], M_[:, kc, :], kc == 0, kc == 7, [bWO, bM], [bpy])
                        K.stt("dve", Z[:, m, :], py[:, 0:T], modap(l, 5, m, col), H[:, m, :], ALU.mult, ALU.add, [bpy, bH, bMOD], [bZ])
                    layer_norm(Z, bZ, ZB, bZB, ZQ, bZQ, ST, bST, HO, bHO, l * 3 + 1, T, 6, LN_EPS / (DN_ALPHA ** 2))
                    c1 = houtcol(ti)
                    K.dma("sp", ov[:, :, c1:c1 + T], HO[:], [bHO], [K.db(houtname)])
                P.flush()


        natcol = lambda ti: ti * 256
        if "E" in PHASES:
            out_phase(0, ev_w_out, MIX, "MIX", qcol, H1, "H1", natcol, H2, "H2", qcol, TILES_Q)
        if "F" in PHASES:
            ffn_phase(0, 1, H2, "H2", H3, "H3", TILES_Q, qcol, qcol)
        if "G" in PHASES:
            ffn_phase(1, 0, H3, "H3", H4, "H4", TILES_Q, qcol, qcol)

        def proj1_phase():
            with contextlib.ExitStack() as st:
                WIN, bWIN = load_w(st, "WIN1", od_w_in, 8, 2048)
                Hs = [sbuf(st, "Hq%d" % i, [128, 8, T], F32) for i in range(2)]
                bHs = [P.buf("Hq0"), P.buf("Hq1")]
                XM = sbuf(st, "XMq", [128, 8, T], BF16)
                bXM = P.buf("XMq")
                ZR = sbuf(st, "ZR", [128, 1024], BF16)
                bZR = P.buf("ZR")
                OU = Rot(st, "OU", 3, [128, T], F32)
                OB = Rot(st, "OB1", 4, [128, T], BF16)
                VDt = sbuf(st, "VDt", [128, 2, 8, 128], BF16)
                bVDt = [P.buf("VDt0"), P.buf("VDt1")]
                K.memset("pool", VDt[:, :, :, 64:128], 1.0, bVDt)
                K.memset("pool", ZR[:], 0.0, [bZR])
                for c in range(4):
                    K.dma("sp", KD[c, :, 0:256], ZR[:, 0:256], [bZR], [K.db("KD")])
                    K.dma("sp", KD[c, :, NKP - 256:NKP], ZR[:, 0:256], [bZR], [K.db("KD")])
                for r in range(2):
                    K.dma("sp", VD[r * 128:(r + 1) * 128, :], ZR[:], [bZR], [K.db("VD")])
                    K.dma("sp", VD[NKP - 256 + r * 128:NKP - 256 + (r + 1) * 128, :], ZR[:], [bZR], [K.db("VD")])
                srcv = H4.rearrange("(c p) t -> p c t", p=128)

                def load(idx):
                    c0 = qcol(TILES_Q[idx])
                    K.dma("sp", Hs[idx % 2][:], srcv[:, :, c0:c0 + T], [K.db("H4")], [bHs[idx % 2]])
                load(0)
                for idx, ti in enumerate(TILES_Q):
                    if idx + 1 < len(TILES_Q):
                        load(idx + 1)
                    H, bH = Hs[idx % 2], bHs[idx % 2]
                    col = 1 if ti == 32 else 0
                    lat = ti != 32
                    t0 = ti * 256
                    for kc in range(8):
                        K.act(XM[:, kc, :], H[:, kc, :], AF.Identity, [bH, bMOD], [bXM],
                              bias=modap(1, 3, kc, col), scale=modap(1, 4, kc, col))
                    if lat:
                        for c in range(4):
                            pu_, bpu_ = bank()
                            for kc in range(8):
                                K.mm(pu_[:, 0:T], WIN[:, kc, c * 128:(c + 1) * 128], XM[:, kc, :], kc == 0, kc == 7, [bWIN, bXM], [bpu_])
                            ou, bou = OU.get()
                            K.copy("act", ou[:], pu_[:, 0:T], [bpu_], [bou])
                            K.dma("sp", U[c * 128:(c + 1) * 128, t0:t0 + T], ou[:], [bou], [K.db("U")])
                        for c in range(4):
                            pq_, bpq_ = bank()
                            for kc in range(8):
                                K.mm(pq_[:, 0:T], WIN[:, kc, 512 + c * 128:512 + (c + 1) * 128], XM[:, kc, :], kc == 0, kc == 7, [bWIN, bXM], [bpq_])
                            ob, bob = OB.get()
                            K.copy("dve", ob[:], pq_[:, 0:T], [bpq_], [bob])
                            K.dma("sp", QD[c, :, t0:t0 + T], ob[:], [bob], [K.db("QD")])
                    for c in range(4):
                        pk_, bpk_ = bank()
                        for kc in range(8):
                            K.mm(pk_[:, 0:T], WIN[:, kc, 1024 + c * 128:1024 + (c + 1) * 128], XM[:, kc, :], kc == 0, kc == 7, [bWIN, bXM], [bpk_])
                        ob, bob = OB.get()
                        K.copy("dve", ob[:], pk_[:, 0:T], [bpk_], [bob])
                        if lat:
                            K.dma("sp", KD[c, :, 256 + t0:256 + t0 + T], ob[:], [bob], [K.db("KD")])
                        else:
                            K.dma("sp", KDC[c, :, :], ob[:], [bob], [K.db("KDC")])
                    for tb in range(2):
                        pv, bpv = bank()
                        for kc in range(8):
                            K.mm(pv[:, :], XM[:, kc, tb * 128:(tb + 1) * 128], WIN[:, kc, 1536:2048], kc == 0, kc == 7, [bWIN, bXM], [bpv])
                        K.copy("act", VDt[:, tb, :, 0:64], pv[:, :].rearrange("p (h c) -> p h c", c=64), [bpv], [bVDt[tb]])
                        if lat:
                            dst = VD[256 + t0 + tb * 128:256 + t0 + (tb + 1) * 128, :]
                            dn = "VD"
                        else:
                            dst = VDC[tb * 128:(tb + 1) * 128, :]
                            dn = "VDC"
                        K.dma("sp", dst.rearrange("p (h c) -> p h c", c=128), VDt[:, tb, :, :], [bVDt[tb]], [K.db(dn)])
                P.flush()

        if "H" in PHASES:
            proj1_phase()

        def pool_phase():
            with contextlib.ExitStack() as st:
                NP = NQL + 16
                UT = sbuf(st, "UT", [128, NP], F32)
                X1 = sbuf(st, "X1", [128, NP], F32)
                X2 = sbuf(st, "X2", [128, NP], F32)
                IC = sbuf(st, "IC", [128, NQL], F32)
                PMf = sbuf(st, "PMf", [128, NQL], F32)
                PMb = sbuf(st, "PMb", [128, NQL], BF16)
                WP = sbuf(st, "WP", [128, 128], BF16)
                PSC = sbuf(st, "PSC", [128, 4], F32)
                bUT, bX1, bX2, bIC, bPMf, bPMb, bWP, bPSC = [P.buf(n) for n in "UT X1 X2 IC PMf PMb WP PSC".split()]
                OO = Rot(st, "OOp", 3, [128, 512], BF16)
                K.dma("sp", PSC[:], od_ps_t[:, :], [], [bPSC])
                for g in range(4):
                    K.memset("pool", UT[:, 0:8], 0.0, [bUT])
                    K.memset("pool", UT[:, NP - 8:NP], 0.0, [bUT])
                    K.dma("sp", UT[:, 8:8 + NQL], U[g * 128:(g + 1) * 128, :], [K.db("U")], [bUT])
                    K.dma("sp", IC[:], icnt[:, g, :], [], [bIC])
                    K.dma("pool", WP[:], od_w_pool[g, :, :], [], [bWP])
                    A, bA = UT, bUT
                    outs = [(X1, bX1), (X2, bX2)]
                    for s_ in range(g + 1):
                        sh = 1 << s_
                        B, bB = outs[s_ % 2]
                        K.copy("pool", B[:, 0:sh], A[:, 0:sh], [bA], [bB])
                        K.tt("dve", B[:, sh:NP], A[:, sh:NP], A[:, 0:NP - sh], ALU.add, [bA], [bB])
                        A, bA = B, bB
                    right = (0, 1, 3, 7)[g]
                    K.tt("dve", PMf[:], A[:, 8 + right:8 + right + NQL], IC[:], ALU.mult, [bA, bIC], [bPMf])
                    K.tt("pool", PMb[:], PMf[:], UT[:, 8:8 + NQL], ALU.subtract, [bPMf, bUT], [bPMb])
                    for ch in range(9):
                        c0 = ch * 512
                        N = 512 if ch < 8 else 256
                        pp, bpp = bank()
                        K.mm(pp[:, 0:N], WP[:], PMb[:, c0:c0 + N], True, True, [bWP, bPMb], [bpp])
                        oo, boo = OO.get()
                        K.act(oo[:, 0:N], pp[:, 0:N], AF.Identity, [bpp, bPSC], [boo], scale=PSC[:, g:g + 1])
                        K.dma("sp", MIXD[g * 128:(g + 1) * 128, c0:c0 + N], oo[:, 0:N], [boo], [K.db("MIXD")])
                P.flush()

        def na_phase():
            with contextlib.ExitStack() as st:
                BIAS = sbuf(st, "BIAS", [128, 3, 6 * 8 * 256], F32)
                bBIAS = P.buf("BIAS")
                for t_ in range(3):
                    for hh in range(2):
                        K.dma("sp" if hh == 0 else "act", BIAS[:, t_, hh * 6144:(hh + 1) * 6144], na_b[t_, :, hh * 6144:(hh + 1) * 6144], [], [bBIAS])
                KDb = [sbuf(st, "KDb%d" % i, [128, 4, 768], BF16) for i in range(2)]
                VDb = [sbuf(st, "VDb%d" % i, [128, 6, 1024], BF16) for i in range(2)]
                QDb = [sbuf(st, "QDb%d" % i, [128, 4, 256], BF16) for i in range(2)]
                bKDb = [P.buf("KDb%d" % i) for i in range(2)]
                bVDb = [P.buf("VDb%d" % i) for i in range(2)]
                bQDb = [P.buf("QDb%d" % i) for i in range(2)]
                KC = sbuf(st, "KC", [128, 4, 256], BF16)
                VC = sbuf(st, "VC", [128, 2, 1024], BF16)
                bKC, bVC = P.buf("KC"), P.buf("VC")
                K.dma("sp", KC[:], KDC.rearrange("c p n -> p c n"), [K.db("KDC")], [bKC])
                K.dma("sp", VC[:], VDC.rearrange("(kt p) n -> p kt n", p=128), [K.db("VDC")], [bVC])
                SBt = Rot(st, "SBt", 4, [128, 256], F32)
                PT = Rot(st, "PTn", 6, [128, 256], BF16)
                RR = sbuf(st, "RRn", [64, 256], F32)
                bRR = P.buf("RRn")
                OO = Rot(st, "OOn", 3, [64, 256], BF16)
                KDv = KD.rearrange("c p n -> p c n")
                QDv = QD.rearrange("c p n -> p c n")
                VDv = VD.rearrange("(kt p) n -> p kt n", p=128)

                def load(bi):
                    i0 = 4 * bi
                    K.dma("sp", KDb[bi % 2][:], KDv[:, :, i0 * 64:i0 * 64 + 768], [K.db("KD")], [bKDb[bi % 2]])
                    K.dma("sp", VDb[bi % 2][:], VDv[:, i0 // 2:i0 // 2 + 6, :], [K.db("VD")], [bVDb[bi % 2]])
                    K.dma("sp", QDb[bi % 2][:], QDv[:, :, bi * 256:(bi + 1) * 256], [K.db("QD")], [bQDb[bi % 2]])
                load(0)
                for bi in range(17):
                    if bi + 1 < 17:
                        load(bi + 1)
                    kd, vd, qd = KDb[bi % 2], VDb[bi % 2], QDb[bi % 2]
                    bk, bv, bq = bKDb[bi % 2], bVDb[bi % 2], bQDb[bi % 2]
                    tab = 0 if bi == 0 else (2 if bi == 16 else 1)
                    steps = [(h, kt) for h in range(8) for kt in range(8)]
                    Sq = {}

                    def emitS(s):
                        h, kt = steps[s]
                        c, po = h // 2, (h % 2) * 64
                        S, bS = bank(0, 6)
                        if kt < 6:
                            K.mm(S[:, 0:256], kd[po:po + 64, c, kt * 128:(kt + 1) * 128], qd[po:po + 64, c, :], True, True, [bk, bq], [bS])
                        else:
                            kk = kt - 6
                            K.mm(S[:, 0:256], KC[po:po + 64, c, kk * 128:(kk + 1) * 128], qd[po:po + 64, c, :], True, True, [bKC, bq], [bS])
                        Sq[s] = (S, bS)
                    DEPTH = 3
                    for s in range(DEPTH):
                        emitS(s)
                    for s in range(len(steps)):
                        if s + DEPTH < len(steps):
                            emitS(s + DEPTH)
                        h, kt = steps[s]
                        S, bS = Sq.pop(s)
                        O, bO = PS[6 + h % 2], PSB[6 + h % 2][0]
                        pt, bpt = PT.get()
                        if kt < 6:
                            sb2, bsb2 = SBt.get()
                            boff = (kt * 8 + h) * 256
                            K.stt("dve", sb2[:], S[:, 0:256], D_SCALE, BIAS[:, tab, boff:boff + 256], ALU.mult, ALU.add, [bS, bBIAS], [bsb2])
                            K.act(pt[:], sb2[:], AF.Exp, [bsb2], [bpt])
                            K.mm(O[:, 0:256], vd[:, kt, h * 128:(h + 1) * 128], pt[:], kt == 0, False, [bv, bpt], [bO])
                        else:
                            kk = kt - 6
                            K.act(pt[:], S[:, 0:256], AF.Exp, [bS], [bpt], scale=D_SCALE)
                            K.mm(O[:, 0:256], VC[:, kk, h * 128:(h + 1) * 128], pt[:], False, kt == 7, [bVC, bpt], [bO])
                        if kt == 7:
                            K.recip(RR[:], O[64:128, 0:256], [bO], [bRR])
                            oo, boo = OO.get()
                            K.tt("dve", oo[:], O[0:64, 0:256], RR[:], ALU.mult, [bO, bRR], [boo])
                            K.dma("sp", MIXD[512 + 64 * h:512 + 64 * (h + 1), bi * 256:(bi + 1) * 256], oo[:], [boo], [K.db("MIXD")])
                P.flush()

        if "I" in PHASES:
            pool_phase()
            na_phase()
        TILES_L = list(range(17))
        if "J" in PHASES:
            out_phase(1, od_w_out, MIXD, "MIXD", natcol, H4, "H4", qcol, H5, "H5", natcol, TILES_L)
        if "K" in PHASES:
            ffn_phase(1, 1, H5, "H5", outT, "outT", TILES_L, natcol, natcol)
    return nc


def _perm_sign(rot_dim):
    q = rot_dim // 4
    perm = np.zeros(rot_dim, np.int64)
    sign = np.zeros(rot_dim, np.float32)
    for i in range(rot_dim):
        qq = i // q
        if qq % 2 == 0:
            perm[i] = i + q
            sign[i] = -1.0
        else:
            perm[i] = i - q
            sign[i] = 1.0
    return perm, sign


def _rope_tables(pos, rot_dim):
    q = rot_dim // 4
    row = (pos // 64).astype(np.float32)
    col = (pos % 64).astype(np.float32)
    inv = (np.float32(10000.0) ** (-np.arange(q, dtype=np.float32) / np.float32(q))).astype(np.float32)
    ang_r = row[:, None] * inv[None, :]
    ang_c = col[:, None] * inv[None, :]
    ang = np.concatenate([ang_r, ang_r, ang_c, ang_c], -1).astype(np.float32)
    return np.cos(ang).astype(np.float32), np.sin(ang).astype(np.float32)


def core_positions(half):
    s0 = 0 if half == 0 else 3840
    own = np.arange(s0, s0 + NQL)
    rest = np.concatenate([np.arange(0, s0), np.arange(s0 + NQL, NL)])
    return np.concatenate([own, rest]), s0


def prepare_inputs(inp):
    f = lambda a: np.ascontiguousarray(np.asarray(a, dtype=np.float32))
    x, c, ctx, c_ctx = f(inp["x"]), f(inp["c"]), f(inp["ctx"]), f(inp["c_ctx"])
    shared = {}
    shared["ada_w"] = f(inp["ada_w"])
    ada_b = f(inp["ada_b"])
    abt = ada_b.reshape(2, 72, 128).transpose(2, 0, 1)
    shared["ada_b_t"] = np.ascontiguousarray(np.stack([abt, abt], -1))
    shared["ln_g_t"] = np.ascontiguousarray(f(inp["ln_g"]).reshape(2, 3, 8, 128).transpose(3, 0, 1, 2).reshape(128, 48))
    shared["ln_b_t"] = np.ascontiguousarray(f(inp["ln_b"]).reshape(2, 3, 8, 128).transpose(3, 0, 1, 2).reshape(128, 48))
    shared["wg"] = f(inp["ffn_w_gate"])
    shared["wu"] = f(inp["ffn_w_up"])
    shared["wd"] = f(inp["ffn_w_down"])
    w_in = f(inp["ev_w_in"])[0]
    shared["ev_w_in"] = w_in
    pa, sga = _perm_sign(32)
    pb, sgb = _perm_sign(64)
    wperm = np.zeros((D, 1024), np.float32)
    for blk in range(8):
        wperm[:, blk * 64:(blk + 1) * 64] = w_in[:, 256 + blk * 64 + pb]
        wperm[:, 512 + blk * 64:512 + (blk + 1) * 64] = w_in[:, 928 + blk * 64 + pb]
    shared["ev_w_in_perm"] = wperm
    wkpe = np.zeros((D, 2, 96), np.float32)
    wkpe[:, 0, 64:] = w_in[:, 896:928]
    wkpe[:, 1, 64:] = w_in[:, 896 + pa]
    shared["ev_w_kpe"] = wkpe.reshape(D, 192)
    w_uq = f(inp["ev_w_uq"])[0]
    shared["ev_w_uq"] = w_uq
    wuqp = w_uq.copy()
    for h in range(8):
        wuqp[:, 96 * h + 64:96 * h + 96] = w_uq[:, 96 * h + 64 + pa]
    shared["ev_w_uq_perm"] = wuqp
    w_ukv = f(inp["ev_w_ukv"])[0].reshape(128, 8, 128)
    wk = np.zeros((128, 8, 96), np.float32)
    wk[:, :, :64] = w_ukv[:, :, :64]
    shared["ev_w_ukv_k"] = wk.reshape(128, 768)
    shared["ev_w_ukv_v"] = np.ascontiguousarray(w_ukv[:, :, 64:].reshape(128, 512))
    small = np.zeros((128, 4), np.float32)
    gq = f(inp["ev_g_qlat"])[0]
    small[:, 0] = gq[:128]
    small[:, 1] = gq[128:]
    small[:, 2] = f(inp["ev_g_kvlat"])[0]
    small[:, 3] = f(inp["ev_g_sub"])[0]
    shared["ev_small"] = small
    shared["ev_lam_bc"] = np.ascontiguousarray(np.broadcast_to(f(inp["ev_lam"])[0].reshape(1, 256), (128, 256)))
    shared["ev_w_out"] = f(inp["ev_w_out"])[0]
    shared["od_w_in"] = f(inp["od_w_in"])[0]
    shared["od_w_out"] = f(inp["od_w_out"])[0]
    shared["od_w_pool"] = f(inp["od_w_pool"])[0]
    shared["od_ps_t"] = np.ascontiguousarray(f(inp["od_pool_scale"])[0].reshape(4, 128).T)
    rpb = f(inp["od_rpb"])[0]

    per_half = []
    for half in range(2):
        d = {}
        pos, s0 = core_positions(half)
        cA, sA = _rope_tables(pos, 32)
        cB, sB = _rope_tables(pos, 64)
        ca = np.ones((96, NT), np.float32)
        sa = np.zeros((96, NT), np.float32)
        ca[64:, :NL] = cA.T
        sa[64:, :NL] = (sA * sga[None, :]).T
        cbt = np.ones((128, NT), np.float32)
        sbt = np.zeros((128, NT), np.float32)
        cbt[:64, :NL] = cB.T
        cbt[64:, :NL] = cB.T
        sbt[:64, :NL] = (sB * sgb[None, :]).T
        sbt[64:, :NL] = (sB * sgb[None, :]).T
        d["ca"], d["sa"], d["cb"], d["sb"] = ca, sa, cbt, sbt
        ic = np.ones((4, NQL), np.float32)
        tg = s0 + np.arange(NQL)
        for g, w in enumerate((2, 4, 8, 16)):
            left = w // 2
            right = w - 1 - left
            lo = np.clip(tg - left, 0, NL)
            hi = np.clip(tg + right + 1, 0, NL)
            ic[g] = 1.0 / (hi - lo).astype(np.float32)
        d["icnt"] = np.ascontiguousarray(np.broadcast_to(ic[None], (128, 4, NQL)))
        roff = 0 if half == 0 else 60
        gtab = np.zeros((3, 128, 6, 8, 256), np.float32)
        mtab = np.zeros((3, 128, 6, 8, 256), np.float32)
        qc = np.arange(64)
        kc = np.arange(64)
        cs = np.clip(qc - 8, 0, 48)
        colv = (kc[None, :] >= cs[:, None]) & (kc[None, :] < cs[:, None] + 16)
        cidx = np.clip(kc[None, :] - qc[:, None] + 15, 0, 30)
        for tix, bi in enumerate((0, 8, 16)):
            i0 = 4 * bi
            for a in range(4):
                R = i0 + a + roff
                rs = int(np.clip(R - 4, 0, 120))
                for cc in range(12):
                    kr = i0 - 4 + cc
                    KR = kr + roff
                    rowv = (rs <= KR < rs + 8) and (0 <= kr < 68)
                    kt, ph = cc // 2, (cc % 2) * 64
                    if rowv:
                        vals = rpb[:, KR - R + 7, :][:, cidx]
                        gtab[tix, ph:ph + 64, kt, :, a * 64:(a + 1) * 64] = vals.transpose(2, 0, 1)
                        mtab[tix, ph:ph + 64, kt, :, a * 64:(a + 1) * 64] = np.where(colv.T[:, None, :], 0.0, -1e30)
                    else:
                        mtab[tix, ph:ph + 64, kt, :, a * 64:(a + 1) * 64] = -1e30
        d["na_b"] = np.where(mtab < -1.0, np.float32(-1e30), gtab).reshape(3, 128, 6 * 8 * 256)
        d["pos"] = pos
        per_half.append(d)

    in_maps = []
    for cid in range(8):
        b, half = cid // 2, cid % 2
        d = per_half[half]
        m = dict(shared)
        xtc = np.empty((D, NT), np.float32)
        xtc[:, :NL] = x[b][d["pos"]].T
        xtc[:, NL:] = ctx[b].T
        m["xt"] = xtc
        ctt = np.empty((128, 8, 2), np.float32)
        ctt[:, :, 0] = c[b].reshape(8, 128).T
        ctt[:, :, 1] = c_ctx.reshape(8, 128).T
        m["ct"] = ctt
        for k in ("ca", "sa", "cb", "sb", "icnt", "na_b"):
            m[k] = d[k]
        in_maps.append(m)
    return in_maps


_NC_CACHE = {}


def kernel(**inputs):
    in_maps = prepare_inputs(inputs)
    if "nc" not in _NC_CACHE:
        _NC_CACHE["nc"] = build_program(False)
    nc = _NC_CACHE["nc"]
    res = run_bass_kernel_spmd(nc, in_maps, core_ids=list(range(8)))
    out = np.empty((4, NL, D), np.float32)
    for cid in range(8):
        b, half = cid // 2, cid % 2
        o = res.results[cid]["outT"]
        if half == 0:
            out[b, 0:4096] = o[:, 0:4096].T
        else:
            out[b, 4096:8192] = o[:, 256:4352].T
    return out
```

```python
import math
import os
import contextlib
import numpy as np
import concourse.bass as bass
import concourse.mybir as mybir
from concourse.bass_utils import run_bass_kernel_spmd

F32 = mybir.dt.float32
BF16 = mybir.dt.bfloat16
ALU = mybir.AluOpType
AF = mybir.ActivationFunctionType
AX = mybir.AxisListType

ENGS = ("pe", "act", "dve", "pool", "sp")

D = 1024
DFF = 2816
NFF = 22
T = 256
NL = 8192
NT = 8448
NQL = 4352
NQ = 4608
TILES_ALL = list(range(33))
TILES_Q = list(range(17)) + [32]
DN_ALPHA = float(4 ** 0.25)
LN_EPS = 1e-6
A_SCALE = float(96 ** -0.5)
B_SCALE = float(64 ** -0.5)
D_SCALE = float(64 ** -0.5)
LAM_INIT0 = 0.8 - 0.6 * math.exp(0.0)
NPADR = 76
NKP = NPADR * 64


def qcol(ti):
    return ti * 256 if ti < 17 else 4352


class Buf:
    __slots__ = ("name", "w", "r", "sem", "cnt", "excl")

    def __init__(self, name):
        self.name = name
        self.excl = False
        self.w = None
        self.r = []
        self.sem = None
        self.cnt = 0


class Prog:
    def __init__(self, nc):
        self.nc = nc
        self.streams = {e: [] for e in ENGS}
        self.count = {e: 0 for e in ENGS}
        self.known = {e: {} for e in ENGS}
        self.nsem_dma = 0
        self.dma_sems = []
        self.nflush = 0
        self.bufs = []

    def buf(self, name="b"):
        b = Buf(name)
        self.bufs.append(b)
        return b

    def _deps(self, eng, reads, writes):
        deps = {}

        def add(ev):
            if ev is None:
                return
            k, v = ev
            if deps.get(k, 0) < v:
                deps[k] = v
        for b in reads:
            add(b.w)
        for b in writes:
            add(b.w)
            for ev in b.r:
                add(ev)
        out = []
        kn = self.known[eng]
        for k, v in deps.items():
            if k == eng and eng == "pe":
                continue
            if kn.get(k, 0) >= v:
                continue
            kn[k] = v
            out.append((k, v))
        return out

    def _post(self, ev, reads, writes):
        for b in reads:
            if len(b.r) > 64:
                mx = {}
                for k, v in b.r:
                    if mx.get(k, 0) < v:
                        mx[k] = v
                b.r = list(mx.items())
            b.r.append(ev)
        for b in writes:
            b.w = ev
            b.r = []

    def op(self, eng, fn, reads=(), writes=()):
        if any(b.excl for b in reads):
            writes = list(writes) + [b for b in reads if b.excl]
            reads = [b for b in reads if not b.excl]
        waits = self._deps(eng, reads, writes)
        self.count[eng] += 1
        ev = (eng, self.count[eng])
        self.streams[eng].append((waits, fn, ev))
        self._post(ev, reads, writes)
        return ev

    def dma(self, q, fn, reads=(), writes=(), sembuf=None):
        sb = sembuf if sembuf is not None else writes[0]
        if sb.sem is None:
            sb.sem = "d%d" % self.nsem_dma
            self.nsem_dma += 1
            self.dma_sems.append(sb.sem)
        waits = self._deps(q, reads, writes)
        sb.cnt += 16
        ev = (sb.sem, sb.cnt)
        self.streams[q].append((waits, fn, ev))
        self._post(ev, reads, writes)
        return ev

    def flush(self):
        nc = self.nc
        with contextlib.ExitStack() as st:
            st.enter_context(nc.cleanup_on_exit())
            sems = {}
            for e in ENGS:
                sems[e] = nc.alloc_semaphore(name="s%d_%s" % (self.nflush, e))
            for k in self.dma_sems:
                sems[k] = nc.alloc_semaphore(name="s%d_%s" % (self.nflush, k))
            block = st.enter_context(nc.Block())
            final = {k: 0 for k in sems}
            for e in ENGS:
                for (_, _, ev) in self.streams[e]:
                    if ev is not None:
                        final[ev[0]] = max(final[ev[0]], ev[1])

            def run(eng_name):
                def body(eng):
                    for waits, fn, ev in self.streams[eng_name]:
                        for k, v in waits:
                            eng.wait_ge(sems[k], v)
                        ins = fn(eng)
                        if ev[0] in ENGS:
                            ins.then_inc(sems[ev[0]], 1)
                        else:
                            ins.then_inc(sems[ev[0]], 16)
                    if eng_name == "sp":
                        for k, v in final.items():
                            if v > 0:
                                eng.wait_ge(sems[k], v)
                return body
            block.tensor(run("pe"))
            block.scalar(run("act"))
            block.vector(run("dve"))
            block.gpsimd(run("pool"))
            block.sync(run("sp"))
        self.nflush += 1
        self.streams = {e: [] for e in ENGS}
        self.count = {e: 0 for e in ENGS}
        self.known = {e: {} for e in ENGS}
        self.nsem_dma = 0
        self.dma_sems = []
        for b in self.bufs:
            b.w = None
            b.r = []
            b.sem = None
            b.cnt = 0


class KB:
    def __init__(self, nc):
        self.nc = nc
        self.P = Prog(nc)
        self.dr = {}

    def din(self, name, shape, dt=F32):
        ap = self.nc.dram_tensor(name, list(shape), dt, kind="ExternalInput").ap()
        self.dr[name] = (ap, self.P.buf(name))
        return ap

    def dout(self, name, shape, dt=F32):
        ap = self.nc.dram_tensor(name, list(shape), dt, kind="ExternalOutput").ap()
        self.dr[name] = (ap, self.P.buf(name))
        return ap

    def dscr(self, name, shape, dt=F32, debug=False):
        if debug:
            return self.dout(name, shape, dt)
        ap = self.nc.dram_tensor(name, list(shape), dt).ap()
        self.dr[name] = (ap, self.P.buf(name))
        return ap

    def db(self, name):
        return self.dr[name][1]

    def mm(self, out, lhsT, rhs, start, stop, r, w):
        self.P.op("pe", lambda e: e.matmul(out, lhsT=lhsT, rhs=rhs, start=start, stop=stop), r, w)

    def act(self, out, in_, func, r, w, bias=None, scale=None):
        kw = {}
        if bias is not None:
            kw["bias"] = bias
        if scale is not None:
            kw["scale"] = scale
        self.P.op("act", lambda e: e.activation(out=out, in_=in_, func=func, **kw), r, w)

    def tt(self, eng, out, in0, in1, op, r, w):
        self.P.op(eng, lambda e: e.tensor_tensor(out=out, in0=in0, in1=in1, op=op), r, w)

    def ts(self, eng, out, in0, s1, op0, r, w, s2=None, op1=None):
        if op1 is None:
            self.P.op(eng, lambda e: e.tensor_scalar(out=out, in0=in0, scalar1=s1, scalar2=None, op0=op0), r, w)
        else:
            self.P.op(eng, lambda e: e.tensor_scalar(out=out, in0=in0, scalar1=s1, scalar2=s2, op0=op0, op1=op1), r, w)

    def stt(self, eng, out, in0, scalar, in1, op0, op1, r, w):
        self.P.op(eng, lambda e: e.scalar_tensor_tensor(out=out, in0=in0, scalar=scalar, in1=in1, op0=op0, op1=op1), r, w)

    def copy(self, eng, out, in_, r, w):
        if eng == "act":
            self.P.op("act", lambda e: e.copy(out=out, in_=in_), r, w)
        else:
            self.P.op(eng, lambda e: e.tensor_copy(out=out, in_=in_), r, w)

    def recip(self, out, in_, r, w):
        self.P.op("dve", lambda e: e.reciprocal(out=out, in_=in_), r, w)

    def memset(self, eng, ap, val, w):
        self.P.op(eng, lambda e: e.memset(ap, val), (), w)

    def dma(self, q, out, in_, r, w, sembuf=None):
        self.P.dma(q, lambda e: e.dma_start(out=out, in_=in_), r, w, sembuf)


def build_program(debug=False):
    nc = bass.Bass("TRN2", target_bir_lowering=False)
    K = KB(nc)
    P = K.P
    dbg = debug

    xt = K.din("xt", [D, NT])
    ct = K.din("ct", [128, 8, 2])
    ada_w = K.din("ada_w", [2, D, 9 * D])
    ada_b_t = K.din("ada_b_t", [128, 2, 72, 2])
    ln_g_t = K.din("ln_g_t", [128, 48])
    ln_b_t = K.din("ln_b_t", [128, 48])
    wg = K.din("wg", [2, 2, D, DFF])
    wu = K.din("wu", [2, 2, D, DFF])
    wd = K.din("wd", [2, 2, DFF, D])
    ev_w_in = K.din("ev_w_in", [D, 1952])
    ev_w_in_perm = K.din("ev_w_in_perm", [D, 1024])
    ev_w_kpe = K.din("ev_w_kpe", [D, 192])
    ev_w_uq = K.din("ev_w_uq", [256, 768])
    ev_w_uq_perm = K.din("ev_w_uq_perm", [256, 768])
    ev_w_ukv_k = K.din("ev_w_ukv_k", [128, 768])
    ev_w_ukv_v = K.din("ev_w_ukv_v", [128, 512])
    ev_small = K.din("ev_small", [128, 4])
    ev_lam_bc = K.din("ev_lam_bc", [128, 256])
    ev_w_out = K.din("ev_w_out", [D, D])
    ca = K.din("ca", [96, NT])
    sa = K.din("sa", [96, NT])
    cb = K.din("cb", [128, NT])
    sb_ = K.din("sb", [128, NT])
    od_w_in = K.din("od_w_in", [D, 2048])
    od_w_out = K.din("od_w_out", [D, D])
    od_w_pool = K.din("od_w_pool", [4, 128, 128])
    od_ps_t = K.din("od_ps_t", [128, 4])
    icnt = K.din("icnt", [128, 4, NQL])
    na_b = K.din("na_b", [3, 128, 6 * 8 * 256])
    ident = K.din("ident", [128, 128])

    outT = K.dout("outT", [D, NQL])

    H1 = K.dscr("H1", [D, NT], F32, dbg)
    QA = K.dscr("QA", [8, 96, NQ], BF16)
    KA = K.dscr("KA", [8, 96, NT], BF16)
    VA = K.dscr("VA", [NT, 8 * 128], BF16)
    QB = K.dscr("QB", [4, 128, NQ], BF16)
    KBs = K.dscr("KBs", [4, 128, NT], BF16)
    VB = K.dscr("VB", [NT, 512], BF16)
    MIX = K.dscr("MIX", [D, NQ], BF16, dbg)
    H2 = K.dscr("H2", [D, NQ], F32, dbg)
    H3 = K.dscr("H3", [D, NQ], F32, dbg)
    H4 = K.dscr("H4", [D, NQ], F32, dbg)
    U = K.dscr("U", [512, NQL], F32)
    QD = K.dscr("QD", [4, 128, NQL], BF16)
    KD = K.dscr("KD", [4, 128, NKP], BF16)
    KDC = K.dscr("KDC", [4, 128, 256], BF16)
    VD = K.dscr("VD", [NKP, 8 * 128], BF16)
    VDC = K.dscr("VDC", [256, 8 * 128], BF16)
    MIXD = K.dscr("MIXD", [D, NQL], BF16, dbg)
    H5 = K.dscr("H5", [D, NQL], F32, dbg)

    with contextlib.ExitStack() as top:
        uid = [0]

        def sbuf(st, name, shape, dt):
            uid[0] += 1
            return st.enter_context(nc.sbuf_tensor("%s_%d" % (name, uid[0]), list(shape), dt))

        PS = [top.enter_context(nc.psum_tensor("ps%d" % i, [128, 512], F32)) for i in range(8)]
        PSB = []
        for i in range(8):
            _b = P.buf("ps%d" % i)
            _b.excl = True
            PSB.append([_b, _b])

        MOD = sbuf(top, "MOD", [128, 2, 72, 2], F32)
        bMOD = P.buf("MOD")
        LNG = sbuf(top, "LNG", [128, 48], F32)
        LNB = sbuf(top, "LNB", [128, 48], F32)
        bLN = P.buf("LN")
        ONES = sbuf(top, "ONES", [128, 4, 128], BF16)
        bONES = P.buf("ONES")
        EPSC = sbuf(top, "EPSC", [128, 1], F32)

        with contextlib.ExitStack() as st:
            SC = sbuf(st, "SC", [128, 8, 2], F32)
            SG0 = sbuf(st, "SG0", [128, 8, 2], F32)
            ADB = sbuf(st, "ADB", [128, 2, 72, 2], F32)
            WS = [sbuf(st, "WS%d" % i, [128, 8, 1024], F32) for i in range(2)]
            bSC, bADB = P.buf("SC"), P.buf("ADB")
            bWS = [P.buf("WS0"), P.buf("WS1")]
            K.dma("sp", SC[:], ct[:, :, :], [K.db("ct")], [bSC])
            K.dma("sp", ADB[:], ada_b_t[:, :, :, :], [K.db("ada_b_t")], [bADB])
            K.dma("sp", LNG[:], ln_g_t[:, :], [], [bLN])
            K.dma("sp", LNB[:], ln_b_t[:, :], [], [bLN])
            bSG0 = P.buf("SG0")
            K.act(SG0[:], SC[:], AF.Silu, [bSC], [bSG0])
            K.copy("dve", SC[:], SG0[:], [bSG0], [bSC])
            K.memset("pool", ONES[:, 0, :], 1.0 / 1024.0, [bONES])
            K.memset("pool", ONES[:, 1, :], 1.0 / 128.0, [bONES])
            K.memset("pool", ONES[:, 2, :], 1.0 / 256.0, [bONES])
            K.memset("pool", ONES[:, 3, :], 1.0, [bONES])
            K.memset("pool", EPSC[:], 0.0, [bONES])
            n = 0
            for l in range(2):
                psA = PS[l]
                for s in range(9):
                    wsl = WS[n % 2]
                    bw = bWS[n % 2]
                    n += 1
                    src = ada_w[l, :, s * 1024:(s + 1) * 1024].rearrange("(c p) n -> p c n", p=128)
                    for kc in range(8):
                        K.dma("sp" if kc % 2 == 0 else "act", wsl[:, kc, :], src[:, kc, :], [], [bw])
                    for m in range(8):
                        col = (s * 8 + m) * 2
                        for kc in range(8):
                            K.mm(psA[:, col:col + 2], wsl[:, kc, m * 128:(m + 1) * 128], SC[:, kc, :],
                                 kc == 0, kc == 7, [bw, bSC], [PSB[l][0]])
                K.tt("dve", MOD[:, l, :, :], psA[:, 0:144].rearrange("p (a b) -> p a b", b=2), ADB[:, l, :, :],
                     ALU.add, [PSB[l][0], bADB], [bMOD])
                for i in (1, 4, 7):
                    K.ts("dve", MOD[:, l, i * 8:(i + 1) * 8, :], MOD[:, l, i * 8:(i + 1) * 8, :], 1.0, ALU.add,
                         [bMOD], [bMOD])
                for i, f in ((2, 0.5 / DN_ALPHA), (5, 1.0 / DN_ALPHA), (8, 0.5 / DN_ALPHA)):
                    K.ts("dve", MOD[:, l, i * 8:(i + 1) * 8, :], MOD[:, l, i * 8:(i + 1) * 8, :], f, ALU.mult,
                         [bMOD], [bMOD])
            P.flush()

        def modap(l, i, m, col):
            return MOD[:, l, i * 8 + m, col:col + 1]

        def load_w(st, name, src2d, kc, ncols, eng_q="pool"):
            t = sbuf(st, name, [128, kc, ncols], BF16)
            b = P.buf(name)
            src = src2d.rearrange("(c p) n -> p c n", p=128)
            for c in range(kc):
                K.dma(eng_q, t[:, c, :], src[:, c, :], [], [b])
            return t, b

        def layer_norm(Z, bZ, ZB, bZB, ZQ, bZQ, ST, bST, Hout, bH, gcol, width, psb_idx, eps):
            w_ = width
            for m in range(8):
                K.copy("pool", ZB[:, m, :w_], Z[:, m, :w_], [bZ], [bZB])
                K.act(ZQ[:, m, :w_], Z[:, m, :w_], AF.Square, [bZ], [bZQ])
            pm, bpm = PS[psb_idx][:, 0:w_], PSB[psb_idx][0]
            pq, bpq = PS[psb_idx + 1][:, 0:w_], PSB[psb_idx + 1][0]
            for m in range(8):
                K.mm(pm, ONES[:, 0, :], ZB[:, m, :w_], m == 0, m == 7, [bZB, bONES], [bpm])
            for m in range(8):
                K.mm(pq, ONES[:, 0, :], ZQ[:, m, :w_], m == 0, m == 7, [bZQ, bONES], [bpq])
            mean, m2, rstd = ST[:, 0, :w_], ST[:, 1, :w_], ST[:, 2, :w_]
            K.copy("act", mean, pm, [bpm], [bST])
            K.tt("pool", m2, mean, mean, ALU.mult, [bST], [bST])
            K.stt("dve", rstd, pq, eps, m2, ALU.add, ALU.subtract, [bpq, bST], [bST])
            K.act(rstd, rstd, AF.Sqrt, [bST], [bST])
            K.recip(rstd, rstd, [bST], [bST])
            for m in range(8):
                K.tt("dve", Z[:, m, :w_], Z[:, m, :w_], mean, ALU.subtract, [bZ, bST], [bZ])
                K.tt("pool", Z[:, m, :w_], Z[:, m, :w_], rstd, ALU.mult, [bZ, bST], [bZ])
                K.act(Hout[:, m, :w_], Z[:, m, :w_], AF.Identity, [bZ, bLN], [bH],
                      bias=LNB[:, gcol * 8 + m:gcol * 8 + m + 1], scale=LNG[:, gcol * 8 + m:gcol * 8 + m + 1])

        def ffn_phase(l, f, src, srcname, dst, dstname, tiles, src_col, dst_col):
            with contextlib.ExitStack() as st:
                WG, bWG = load_w(st, "WG", wg[l, f], 8, DFF)
                WU, bWU = load_w(st, "WU", wu[l, f], 8, DFF)
                WD, bWD = load_w(st, "WD", wd[l, f], NFF, D)
                Hs = [sbuf(st, "Hs%d" % i, [128, 8, T], F32) for i in range(2)]
                bHs = [P.buf("H0"), P.buf("H1")]
                XMs = [sbuf(st, "XM%d" % i, [128, 8, T], BF16) for i in range(2)]
                bXMs = [P.buf("XM0"), P.buf("XM1")]
                HID = sbuf(st, "HID", [128, NFF, T], BF16)
                bHID = P.buf("HID")
                Zs = [sbuf(st, "Z%d" % i, [128, 8, T], F32) for i in range(2)]
                bZs = [P.buf("Z0"), P.buf("Z1")]
                ZBs = [sbuf(st, "ZB%d" % i, [128, 8, T], BF16) for i in range(2)]
                bZBs = [P.buf("ZB0"), P.buf("ZB1")]
                ZQs = [sbuf(st, "ZQ%d" % i, [128, 8, T], BF16) for i in range(2)]
                bZQs = [P.buf("ZQ0"), P.buf("ZQ1")]
                SG = [sbuf(st, "SG%d" % i, [128, T], F32) for i in range(2)]
                bSG = [P.buf("SG0"), P.buf("SG1")]
                ST = sbuf(st, "ST", [128, 3, T], F32)
                bST = P.buf("ST")
                NH = sbuf(st, "NH", [128, T], F32)
                bNH = P.buf("NH")
                K.memset("pool", NH[:], -0.5, [bNH])
                srcv = src.rearrange("(c p) t -> p c t", p=128)
                dstv = dst.rearrange("(c p) t -> p c t", p=128)
                gcol = l * 3 + (0 if f == 0 else 2)
                mi = 0 if f == 0 else 6
                eps = LN_EPS / (DN_ALPHA ** 2)
                n = len(tiles)

                def load(idx):
                    c0 = src_col(tiles[idx])
                    K.dma("sp", Hs[idx % 2][:], srcv[:, :, c0:c0 + T], [K.db(srcname)], [bHs[idx % 2]])

                def s1a(idx):
                    ti = tiles[idx]
                    H, bH, XM, bXM = Hs[idx % 2], bHs[idx % 2], XMs[idx % 2], bXMs[idx % 2]
                    col = 1 if ti == 32 else 0
                    for kc in range(8):
                        K.act(XM[:, kc, :], H[:, kc, :], AF.Identity, [bH, bMOD], [bXM],
                              bias=modap(l, mi, kc, col), scale=modap(l, mi + 1, kc, col))

                def s1b(idx):
                    XM, bXM = XMs[idx % 2], bXMs[idx % 2]
                    for j in range(NFF):
                        pg, bpg = PS[j % 2][:, 0:T], PSB[j % 2][0]
                        pu, bpu = PS[2 + j % 2][:, 0:T], PSB[2 + j % 2][0]
                        for kc in range(8):
                            K.mm(pg, WG[:, kc, j * 128:(j + 1) * 128], XM[:, kc, :], kc == 0, kc == 7, [bWG, bXM], [bpg])
                        for kc in range(8):
                            K.mm(pu, WU[:, kc, j * 128:(j + 1) * 128], XM[:, kc, :], kc == 0, kc == 7, [bWU, bXM], [bpu])
                        sg, bsg = SG[j % 2], bSG[j % 2]
                        K.act(sg[:], pg, AF.Silu, [bpg], [bsg])
                        K.tt("dve", HID[:, j, :], sg[:], pu, ALU.mult, [bsg, bpu], [bHID])

                def s2(idx):
                    ti = tiles[idx]
                    H, bH, Z, bZ = Hs[idx % 2], bHs[idx % 2], Zs[idx % 2], bZs[idx % 2]
                    ZB, bZB, ZQ, bZQ = ZBs[idx % 2], bZBs[idx % 2], ZQs[idx % 2], bZQs[idx % 2]
                    col = 1 if ti == 32 else 0
                    for m in range(8):
                        pd, bpd = PS[4 + m % 2][:, 0:T], PSB[4 + m % 2][0]
                        for j in range(NFF):
                            K.mm(pd, WD[:, j, m * 128:(m + 1) * 128], HID[:, j, :], j == 0, j == NFF - 1, [bWD, bHID], [bpd])
                        K.stt("dve", Z[:, m, :], pd, modap(l, mi + 2, m, col), H[:, m, :], ALU.mult, ALU.add, [bpd, bH, bMOD], [bZ])
                        K.copy("pool", ZB[:, m, :], Z[:, m, :], [bZ], [bZB])
                        K.act(ZQ[:, m, :], Z[:, m, :], AF.Square, [bZ], [bZQ])

                def s3(idx):
                    ti = tiles[idx]
                    Z, bZ = Zs[idx % 2], bZs[idx % 2]
                    ZB, bZB, ZQ, bZQ = ZBs[idx % 2], bZBs[idx % 2], ZQs[idx % 2], bZQs[idx % 2]
                    pm, bpm = PS[6][:, 0:T], PSB[6][0]
                    pq, bpq = PS[7][:, 0:T], PSB[7][0]
                    for m in range(8):
                        K.mm(pm, ONES[:, 0, :], ZB[:, m, :], m == 0, m == 7, [bZB, bONES], [bpm])
                    for m in range(8):
                        K.mm(pq, ONES[:, 0, :], ZQ[:, m, :], m == 0, m == 7, [bZQ, bONES], [bpq])
                    mean, var, rstd = ST[:, 0, :], ST[:, 1, :], ST[:, 2, :]
                    K.copy("dve", mean, pm, [bpm], [bST])
                    K.tt("dve", rstd, mean, mean, ALU.mult, [bST], [bST])
                    K.stt("dve", var, pq, eps, rstd, ALU.add, ALU.subtract, [bpq, bST], [bST])
                    K.act(var, var, AF.Sqrt, [bST], [bST])
                    K.recip(rstd, var, [bST], [bST])
                    for m in range(8):
                        K.tt("pool", Z[:, m, :], Z[:, m, :], mean, ALU.subtract, [bZ, bST], [bZ])
                        K.tt("pool", Z[:, m, :], Z[:, m, :], rstd, ALU.mult, [bZ, bST], [bZ])
                        K.ts("pool", Z[:, m, :], Z[:, m, :], LNG[:, gcol * 8 + m:gcol * 8 + m + 1], ALU.mult, [bZ, bLN], [bZ],
                             s2=LNB[:, gcol * 8 + m:gcol * 8 + m + 1], op1=ALU.add)
                    c1 = dst_col(ti)
                    K.dma("sp", dstv[:, :, c1:c1 + T], Z[:], [bZ], [K.db(dstname)])

                load(0)
                if n > 1:
                    load(1)
                s1a(0)
                s1b(0)
                s2(0)
                if n > 1:
                    s1a(1)
                for idx in range(n):
                    if idx + 2 < n:
                        load(idx + 2)
                    if idx + 1 < n:
                        s1b(idx + 1)
                        s2(idx + 1)
                    if idx + 2 < n:
                        s1a(idx + 2)
                    s3(idx)
                P.flush()

        NTL = int(os.environ.get("NTILES", "33"))
        if NTL > 0:
            ffn_phase(0, 0, xt, "xt", H1, "H1", TILES_ALL[:NTL], lambda ti: ti * 256, lambda ti: ti * 256)

        class Rot:
            def __init__(self, st, name, n, shape, dt):
                self.t = [sbuf(st, "%s%d" % (name, i), shape, dt) for i in range(n)]
                self.b = [P.buf("%s%d" % (name, i)) for i in range(n)]
                self.i = 0

            def get(self):
                k = self.i % len(self.t)
                self.i += 1
                return self.t[k], self.b[k]

        bank_ctr = [0]

        def bank(lo=0, hi=8):
            k = lo + bank_ctr[0] % (hi - lo)
            bank_ctr[0] += 1
            return PS[k], PSB[k][0]

        PHASES = os.environ.get("PHASES", "ABCDEFGHIJK")

        def proj0_phase():
            with contextlib.ExitStack() as st:
                WIN, bWIN = load_w(st, "WIN", ev_w_in, 8, 1952)
                WPM, bWPM = load_w(st, "WPM", ev_w_in_perm, 8, 1024)
                WKPE, bWKPE = load_w(st, "WKPE", ev_w_kpe, 8, 192)
                WUQ, bWUQ = load_w(st, "WUQ", ev_w_uq, 2, 768)
                WUQP, bWUQP = load_w(st, "WUQP", ev_w_uq_perm, 2, 768)
                WUK, bWUK = load_w(st, "WUK", ev_w_ukv_k, 1, 768)
                WUV, bWUV = load_w(st, "WUV", ev_w_ukv_v, 1, 512)
                SM = sbuf(st, "SM", [128, 4], F32)
                bSM = P.buf("SM")
                K.dma("sp", SM[:], ev_small[:, :], [], [bSM])
                Hs = [sbuf(st, "Hp%d" % i, [128, 8, T], F32) for i in range(2)]
                bHs = [P.buf("Hp0"), P.buf("Hp1")]
                TAB = [sbuf(st, "TAB%d" % i, [128, 4, T], F32) for i in range(2)]
                bTAB = [P.buf("TAB0"), P.buf("TAB1")]
                XM = sbuf(st, "XMp", [128, 8, T], BF16)
                bXM = P.buf("XMp")
                TMP = Rot(st, "TMP", 4, [128, T], F32)
                OB = Rot(st, "OB", 6, [128, T], BF16)
                OV = Rot(st, "OV", 2, [128, 512], BF16)
                VAt = sbuf(st, "VAt", [128, 2, 8, 128], BF16)
                bVAt = [P.buf("VAt0"), P.buf("VAt1")]
                K.memset("pool", VAt[:, :, :, 64:128], 1.0, bVAt)
                KVL = sbuf(st, "KVL", [128, T], F32)
                SQ = sbuf(st, "SQ", [128, 2, T], BF16)
                RS = sbuf(st, "RS", [128, T], F32)
                KVN = sbuf(st, "KVN", [128, T], BF16)
                KPE = sbuf(st, "KPE", [96, T], F32)
                QL = sbuf(st, "QL", [128, 2, T], F32)
                QN = sbuf(st, "QN", [128, 2, T], BF16)
                bKVL, bSQ, bRS, bKVN, bKPE, bQL, bQN = [P.buf(n) for n in "KVL SQ RS KVN KPE QL QN".split()]
                srcv = H1.rearrange("(c p) t -> p c t", p=128)

                def load(idx):
                    ti = TILES_ALL[idx]
                    c0 = ti * 256
                    K.dma("sp", Hs[idx % 2][:], srcv[:, :, c0:c0 + T], [K.db("H1")], [bHs[idx % 2]])
                    tb_, btb = TAB[idx % 2], bTAB[idx % 2]
                    K.dma("sp", tb_[0:96, 0, :], ca[:, c0:c0 + T], [], [btb])
                    K.dma("sp", tb_[0:96, 1, :], sa[:, c0:c0 + T], [], [btb])
                    K.dma("sp", tb_[:, 2, :], cb[:, c0:c0 + T], [], [btb])
                    K.dma("sp", tb_[:, 3, :], sb_[:, c0:c0 + T], [], [btb])

                def rope_out(psa, bpa, psb, bpb, c_ap, s_ap, btb, rows, dst, dstname):
                    t1, b1 = TMP.get()
                    t2, b2 = TMP.get()
                    K.tt("dve", t1[0:rows, :], psa, c_ap, ALU.mult, [bpa, btb], [b1])
                    K.tt("dve", t2[0:rows, :], psb, s_ap, ALU.mult, [bpb, btb], [b2])
                    ob, bob = OB.get()
                    K.tt("pool", ob[0:rows, :], t1[0:rows, :], t2[0:rows, :], ALU.add, [b1, b2], [bob])
                    K.dma("sp", dst, ob[0:rows, :], [bob], [K.db(dstname)])

                load(0)
                for idx, ti in enumerate(TILES_ALL):
                    if idx + 1 < len(TILES_ALL):
                        load(idx + 1)
                    H, bH = Hs[idx % 2], bHs[idx % 2]
                    tb_, btb = TAB[idx % 2], bTAB[idx % 2]
                    col = 1 if ti == 32 else 0
                    isq = ti in TILES_Q
                    t0 = ti * 256
                    q0 = qcol(ti)
                    for kc in range(8):
                        K.act(XM[:, kc, :], H[:, kc, :], AF.Identity, [bH, bMOD], [bXM],
                              bias=modap(0, 3, kc, col), scale=modap(0, 4, kc, col))
                    for (isneeded, wc0, pc0, dst3, dname, dcol) in ((True, 928, 512, KBs, "KBs", t0), (isq, 256, 0, QB, "QB", q0)):
                        if not isneeded:
                            continue
                        for h in range(4):
                            p1, bp1 = bank()
                            p2, bp2 = bank()
                            for kc in range(8):
                                K.mm(p1[:, 0:T], WIN[:, kc, wc0 + h * 128:wc0 + (h + 1) * 128], XM[:, kc, :], kc == 0, kc == 7, [bWIN, bXM], [bp1])
                            for kc in range(8):
                                K.mm(p2[:, 0:T], WPM[:, kc, pc0 + h * 128:pc0 + (h + 1) * 128], XM[:, kc, :], kc == 0, kc == 7, [bWPM, bXM], [bp2])
                            rope_out(p1[:, 0:T], bp1, p2[:, 0:T], bp2, tb_[:, 2, :], tb_[:, 3, :], btb, 128,
                                     dst3[h, :, dcol:dcol + T], dname)
                    for tb in range(2):
                        pv, bpv = bank()
                        for kc in range(8):
                            K.mm(pv[:, :], XM[:, kc, tb * 128:(tb + 1) * 128], WIN[:, kc, 1440:1952], kc == 0, kc == 7, [bWIN, bXM], [bpv])
                        ov, bov = OV.get()
                        K.copy("act", ov[:], pv[:, :], [bpv], [bov])
                        K.dma("sp", VB[t0 + tb * 128:t0 + (tb + 1) * 128, :], ov[:], [bov], [K.db("VB")])
                    pk, bpk = bank()
                    for kc in range(8):
                        K.mm(pk[:, 0:T], WIN[:, kc, 768:896], XM[:, kc, :], kc == 0, kc == 7, [bWIN, bXM], [bpk])
                    K.copy("dve", KVL[:], pk[:, 0:T], [bpk], [bKVL])
                    K.act(SQ[:, 0, :], pk[:, 0:T], AF.Square, [bpk], [bSQ])
                    pss, bpss = bank()
                    K.mm(pss[:, 0:T], ONES[:, 1, :], SQ[:, 0, :], True, True, [bSQ, bONES], [bpss])
                    K.ts("dve", RS[:], pss[:, 0:T], 1e-6, ALU.add, [bpss], [bRS])
                    K.act(RS[:], RS[:], AF.Sqrt, [bRS], [bRS])
                    K.recip(RS[:], RS[:], [bRS], [bRS])
                    K.tt("dve", KVL[:], KVL[:], RS[:], ALU.mult, [bKVL, bRS], [bKVL])
                    K.act(KVN[:], KVL[:], AF.Identity, [bKVL, bSM], [bKVN], scale=SM[:, 2:3])
                    pp1, bpp1 = bank()
                    pp2, bpp2 = bank()
                    for kc in range(8):
                        K.mm(pp1[0:96, 0:T], WKPE[:, kc, 0:96], XM[:, kc, :], kc == 0, kc == 7, [bWKPE, bXM], [bpp1])
                    for kc in range(8):
                        K.mm(pp2[0:96, 0:T], WKPE[:, kc, 96:192], XM[:, kc, :], kc == 0, kc == 7, [bWKPE, bXM], [bpp2])
                    t1, b1 = TMP.get()
                    t2, b2 = TMP.get()
                    K.tt("dve", t1[0:96, :], pp1[0:96, 0:T], tb_[0:96, 0, :], ALU.mult, [bpp1, btb], [b1])
                    K.tt("dve", t2[0:96, :], pp2[0:96, 0:T], tb_[0:96, 1, :], ALU.mult, [bpp2, btb], [b2])
                    K.tt("pool", KPE[:], t1[0:96, :], t2[0:96, :], ALU.add, [b1, b2], [bKPE])
                    for h in range(8):
                        pkh, bpkh = bank()
                        K.mm(pkh[0:96, 0:T], WUK[:, 0, 96 * h:96 * h + 96], KVN[:], True, True, [bWUK, bKVN], [bpkh])
                        ob, bob = OB.get()
                        K.tt("dve", ob[0:96, :], pkh[0:96, 0:T], KPE[:], ALU.add, [bpkh, bKPE], [bob])
                        K.dma("sp", KA[h, :, t0:t0 + T], ob[0:96, :], [bob], [K.db("KA")])
                    for tb in range(2):
                        pv, bpv = bank()
                        K.mm(pv[:, :], KVN[:, tb * 128:(tb + 1) * 128], WUV[:, 0, :], True, True, [bWUV, bKVN], [bpv])
                        K.copy("act", VAt[:, tb, :, 0:64], pv[:, :].rearrange("p (h c) -> p h c", c=64), [bpv], [bVAt[tb]])
                        K.dma("sp", VA[t0 + tb * 128:t0 + (tb + 1) * 128, :].rearrange("p (h c) -> p h c", c=128),
                              VAt[:, tb, :, :], [bVAt[tb]], [K.db("VA")])
                    if isq:
                        for c in range(2):
                            pq_, bpq_ = bank()
                            for kc in range(8):
                                K.mm(pq_[:, 0:T], WIN[:, kc, c * 128:(c + 1) * 128], XM[:, kc, :], kc == 0, kc == 7, [bWIN, bXM], [bpq_])
                            K.copy("dve", QL[:, c, :], pq_[:, 0:T], [bpq_], [bQL])
                            K.act(SQ[:, c, :], pq_[:, 0:T], AF.Square, [bpq_], [bSQ])
                        pss, bpss = bank()
                        for c in range(2):
                            K.mm(pss[:, 0:T], ONES[:, 2, :], SQ[:, c, :], c == 0, c == 1, [bSQ, bONES], [bpss])
                        K.ts("dve", RS[:], pss[:, 0:T], 1e-6, ALU.add, [bpss], [bRS])
                        K.act(RS[:], RS[:], AF.Sqrt, [bRS], [bRS])
                        K.recip(RS[:], RS[:], [bRS], [bRS])
                        for c in range(2):
                            K.tt("dve", QL[:, c, :], QL[:, c, :], RS[:], ALU.mult, [bQL, bRS], [bQL])
                            K.act(QN[:, c, :], QL[:, c, :], AF.Identity, [bQL, bSM], [bQN], scale=SM[:, c:c + 1])
                        for h in range(8):
                            p1, bp1 = bank()
                            p2, bp2 = bank()
                            for c in range(2):
                                K.mm(p1[0:96, 0:T], WUQ[:, c, 96 * h:96 * h + 96], QN[:, c, :], c == 0, c == 1, [bWUQ, bQN], [bp1])
                            for c in range(2):
                                K.mm(p2[0:96, 0:T], WUQP[:, c, 96 * h:96 * h + 96], QN[:, c, :], c == 0, c == 1, [bWUQP, bQN], [bp2])
                            rope_out(p1[0:96, 0:T], bp1, p2[0:96, 0:T], bp2, tb_[0:96, 0, :], tb_[0:96, 1, :], btb, 96,
                                     QA[h, :, q0:q0+T], "QA")
                P.flush()

        if "C" in PHASES:
            proj0_phase()

        QTILES = [(i * 512, 512, 0, 66) for i in range(8)] + [(4096, 256, 0, 66), (4352, 256, 64, 66)]

        def att0_mla_phase(heads):
            with contextlib.ExitStack() as st:
                Kt = [sbuf(st, "Kt%d" % i, [96, NT], BF16) for i in range(2)]
                Vt = [sbuf(st, "Vt%d" % i, [128, 66, 128], BF16) for i in range(2)]
                Qt = [sbuf(st, "Qt%d" % i, [96, NQ], BF16) for i in range(2)]
                bKt = [P.buf("Kt%d" % i) for i in range(2)]
                bVt = [P.buf("Vt%d" % i) for i in range(2)]
                bQt = [P.buf("Qt%d" % i) for i in range(2)]
                PT = Rot(st, "PT", 6, [128, 512], BF16)
                RR = sbuf(st, "RR", [64, 512], F32)
                bRR = P.buf("RR")
                OO = Rot(st, "OO", 2, [64, 512], BF16)
                VAv = VA.rearrange("(kt p) (h c) -> p kt h c", p=128, h=8)

                def load(i):
                    h = heads[i]
                    K.dma("sp", Kt[i % 2][:], KA[h, :, :], [K.db("KA")], [bKt[i % 2]])
                    K.dma("sp", Qt[i % 2][:], QA[h, :, :], [K.db("QA")], [bQt[i % 2]])
                    for kq in range(6):
                        K.dma("act" if kq % 2 else "sp", Vt[i % 2][:, kq * 11:(kq + 1) * 11, :], VAv[:, kq * 11:(kq + 1) * 11, h, :], [K.db("VA")], [bVt[i % 2]])
                load(0)
                for i, h in enumerate(heads):
                    if i + 1 < len(heads):
                        load(i + 1)
                    kt_, vt_, qt_ = Kt[i % 2], Vt[i % 2], Qt[i % 2]
                    bk, bv, bq = bKt[i % 2], bVt[i % 2], bQt[i % 2]
                    steps = [(qi, q0, N, kt, kt == klo, kt == khi - 1) for qi, (q0, N, klo, khi) in enumerate(QTILES) for kt in range(klo, khi)]
                    Sq = {}

                    def emitS(s):
                        qi, q0, N, kt, first, last = steps[s]
                        S, bS = bank(0, 6)
                        K.mm(S[:, 0:N], kt_[:, kt * 128:(kt + 1) * 128], qt_[:, q0:q0 + N], True, True, [bk, bq], [bS])
                        Sq[s] = (S, bS)
                    DEPTH = 3
                    for s in range(min(DEPTH, len(steps))):
                        emitS(s)
                    for s in range(len(steps)):
                        if s + DEPTH < len(steps):
                            emitS(s + DEPTH)
                        qi, q0, N, kt, first, last = steps[s]
                        S, bS = Sq.pop(s)
                        O, bO = PS[6 + qi % 2], PSB[6 + qi % 2][0]
                        pt, bpt = PT.get()
                        K.act(pt[:, 0:N], S[:, 0:N], AF.Exp, [bS], [bpt], scale=A_SCALE)
                        K.mm(O[:, 0:N], vt_[:, kt, :], pt[:, 0:N], first, last, [bv, bpt], [bO])
                        if last:
                            K.recip(RR[:, 0:N], O[64:128, 0:N], [bO], [bRR])
                            oo, boo = OO.get()
                            K.tt("dve", oo[:, 0:N], O[0:64, 0:N], RR[:, 0:N], ALU.mult, [bO, bRR], [boo])
                            K.dma("sp", MIX[64 * h:64 * h + 64, q0:q0 + N], oo[:, 0:N], [boo], [K.db("MIX")])
                P.flush()

        def att0_diff_phase(heads):
            with contextlib.ExitStack() as st:
                Kt = [sbuf(st, "Kd%d" % i, [128, NT], BF16) for i in range(2)]
                Vt = [sbuf(st, "Vd%d" % i, [128, 66, 128], BF16) for i in range(2)]
                Qt = [sbuf(st, "Qd%d" % i, [128, 2, NQ], BF16) for i in range(2)]
                ACC = [sbuf(st, "ACC%d" % i, [128, 512], F32) for i in range(2)]
                bACC = [P.buf("ACC0"), P.buf("ACC1")]
                ONESF = sbuf(st, "ONESF", [128, 128], F32)
                bONESF = P.buf("ONESF")
                K.memset("pool", ONESF[:], 1.0, [bONESF])
                bKt = [P.buf("Kd%d" % i) for i in range(2)]
                bVt = [P.buf("Vd%d" % i) for i in range(2)]
                bQt = [P.buf("Qd%d" % i) for i in range(2)]
                PT = Rot(st, "PTd", 4, [128, 512], BF16)
                R1 = sbuf(st, "R1", [128, 512], F32)
                R2 = sbuf(st, "R2", [128, 512], F32)
                OA = sbuf(st, "OA", [128, 512], F32)
                OBd = sbuf(st, "OBd", [128, 512], F32)
                SQd = sbuf(st, "SQd", [128, 512], BF16)
                bR1, bR2, bOA, bOBd, bSQd = [P.buf(n) for n in "R1 R2 OA OBd SQd".split()]
                OO = Rot(st, "OOd", 2, [128, 512], BF16)
                LV = sbuf(st, "LV", [128, 256], F32)
                LP = sbuf(st, "LP", [128, 2, 64], F32)
                LS = sbuf(st, "LS", [128, 4], F32)
                GS = sbuf(st, "GS", [128, 4], F32)
                bLV, bLS, bGS = P.buf("LV"), P.buf("LS"), P.buf("GS")
                K.dma("sp", LV[:], ev_lam_bc[:, :], [], [bLV])
                K.dma("sp", GS[:], ev_small[:, :], [], [bGS])
                K.tt("dve", LP[:, 0, :], LV[:, 0:64], LV[:, 64:128], ALU.mult, [bLV], [bLV])
                K.tt("dve", LP[:, 1, :], LV[:, 128:192], LV[:, 192:256], ALU.mult, [bLV], [bLV])
                P.op("dve", lambda e: e.reduce_sum(out=LS[:, 0:2], in_=LP[:, :, :], axis=AX.X), [bLV], [bLS])
                K.act(LS[:, 0:2], LS[:, 0:2], AF.Exp, [bLS], [bLS])
                K.tt("dve", LS[:, 2:3], LS[:, 1:2], LS[:, 0:1], ALU.subtract, [bLS], [bLS])
                K.ts("dve", LS[:, 2:3], LS[:, 2:3], -LAM_INIT0, ALU.add, [bLS], [bLS])
                K.ts("dve", GS[:, 3:4], GS[:, 3:4], 1.0 - LAM_INIT0, ALU.mult, [bGS], [bGS])
                VBv = VB.rearrange("(kt p) (h c) -> p kt h c", p=128, h=4)

                def load(i):
                    h = heads[i]
                    K.dma("sp", Kt[i % 2][:], KBs[h, :, :], [K.db("KBs")], [bKt[i % 2]])
                    K.dma("sp", Qt[i % 2][0:64, 0, :], QB[h, 0:64, :], [K.db("QB")], [bQt[i % 2]])
                    K.dma("sp", Qt[i % 2][64:128, 1, :], QB[h, 64:128, :], [K.db("QB")], [bQt[i % 2]])
                    for kq in range(6):
                        K.dma("act" if kq % 2 else "sp", Vt[i % 2][:, kq * 11:(kq + 1) * 11, :], VBv[:, kq * 11:(kq + 1) * 11, h, :], [K.db("VB")], [bVt[i % 2]])
                for i_ in range(2):
                    K.memset("pool", Qt[i_][64:128, 0, :], 0.0, [bQt[i_]])
                    K.memset("pool", Qt[i_][0:64, 1, :], 0.0, [bQt[i_]])
                load(0)
                for i, h in enumerate(heads):
                    if i + 1 < len(heads):
                        load(i + 1)
                    kt_, vt_, qt_ = Kt[i % 2], Vt[i % 2], Qt[i % 2]
                    bk, bv, bq = bKt[i % 2], bVt[i % 2], bQt[i % 2]
                    O1, bO1 = PS[4], PSB[4][0]
                    L1, bL1 = PS[5], PSB[5][0]
                    O2, bO2 = PS[6], PSB[6][0]
                    L2, bL2 = PS[7], PSB[7][0]
                    steps = [(qi, q0, N, kt, mp, kt == klo, kt == khi - 1) for qi, (q0, N, klo, khi) in enumerate(QTILES)
                             for kt in range(klo, khi) for mp in range(2)]
                    Sq = {}

                    def emitS(s):
                        qi, q0, N, kt, mp, first, last = steps[s]
                        S, bS = bank(0, 4)
                        K.mm(S[:, 0:N], kt_[:, kt * 128:(kt + 1) * 128], qt_[:, mp, q0:q0 + N], True, True, [bk, bq], [bS])
                        Sq[s] = (S, bS)
                    DEPTH = 2
                    for s in range(min(DEPTH, len(steps))):
                        emitS(s)
                    for s in range(len(steps)):
                        if s + DEPTH < len(steps):
                            emitS(s + DEPTH)
                        qi, q0, N, kt, mp, first, last = steps[s]
                        S, bS = Sq.pop(s)
                        O_, bO_, L_, bL_ = ((O1, bO1, L1, bL1), (O2, bO2, L2, bL2))[mp]
                        pt, bpt = PT.get()
                        K.act(pt[:, 0:N], S[:, 0:N], AF.Exp, [bS], [bpt], scale=B_SCALE)
                        K.mm(O_[:, 0:N], vt_[:, kt, :], pt[:, 0:N], first, last, [bv, bpt], [bO_])
                        aeng = "dve" if mp == 0 else "pool"
                        if first:
                            K.copy(aeng, ACC[mp][:, 0:N], pt[:, 0:N], [bpt], [bACC[mp]])
                        else:
                            K.tt(aeng, ACC[mp][:, 0:N], ACC[mp][:, 0:N], pt[:, 0:N], ALU.add, [bpt, bACC[mp]], [bACC[mp]])
                        if not (last and mp == 1):
                            continue
                        K.mm(L1[:, 0:N], ONESF[:], ACC[0][:, 0:N], True, True, [bONESF, bACC[0]], [bL1])
                        K.mm(L2[:, 0:N], ONESF[:], ACC[1][:, 0:N], True, True, [bONESF, bACC[1]], [bL2])
                        K.recip(R1[:, 0:N], L1[:, 0:N], [bL1], [bR1])
                        K.recip(R2[:, 0:N], L2[:, 0:N], [bL2], [bR2])
                        K.tt("dve", OA[:, 0:N], O1[:, 0:N], R1[:, 0:N], ALU.mult, [bO1, bR1], [bOA])
                        K.tt("dve", OBd[:, 0:N], O2[:, 0:N], R2[:, 0:N], ALU.mult, [bO2, bR2], [bOBd])
                        K.stt("dve", OA[:, 0:N], OBd[:, 0:N], LS[:, 2:3], OA[:, 0:N], ALU.mult, ALU.add, [bOBd, bOA, bLS], [bOA])
                        K.act(SQd[:, 0:N], OA[:, 0:N], AF.Square, [bOA], [bSQd])
                        K.mm(L1[:, 0:N], ONES[:, 1, :], SQd[:, 0:N], True, True, [bONES, bSQd], [bL1])
                        K.ts("dve", R1[:, 0:N], L1[:, 0:N], 1e-5, ALU.add, [bL1], [bR1])
                        K.act(R1[:, 0:N], R1[:, 0:N], AF.Sqrt, [bR1], [bR1])
                        K.recip(R1[:, 0:N], R1[:, 0:N], [bR1], [bR1])
                        K.tt("pool", OA[:, 0:N], OA[:, 0:N], R1[:, 0:N], ALU.mult, [bOA, bR1], [bOA])
                        oo, boo = OO.get()
                        K.act(oo[:, 0:N], OA[:, 0:N], AF.Identity, [bOA, bGS], [boo], scale=GS[:, 3:4])
                        K.dma("sp", MIX[512 + 128 * h:512 + 128 * (h + 1), q0:q0 + N], oo[:, 0:N], [boo], [K.db("MIX")])
                P.flush()

        if "D" in PHASES:
            att0_mla_phase([0, 1, 2, 3])
            att0_mla_phase([4, 5, 6, 7])
            att0_diff_phase([0, 1])
            att0_diff_phase([2, 3])

        def out_phase(l, wout, mix, mixname, mixcol, hin, hinname, hincol, hout, houtname, houtcol, tiles):
            with contextlib.ExitStack() as st:
                WO, bWO = load_w(st, "WO", wout, 8, D)
                Hs = [sbuf(st, "Ho%d" % i, [128, 8, T], F32) for i in range(2)]
                MX = [sbuf(st, "MX%d" % i, [128, 8, T], BF16) for i in range(2)]
                bHs = [P.buf("Ho0"), P.buf("Ho1")]
                bMX = [P.buf("MX0"), P.buf("MX1")]
                Z = sbuf(st, "Zo", [128, 8, T], F32)
                ZB = sbuf(st, "ZBo", [128, 8, T], BF16)
                ZQ = sbuf(st, "ZQo", [128, 8, T], BF16)
                ST = sbuf(st, "STo", [128, 3, T], F32)
                HO = sbuf(st, "HOo", [128, 8, T], F32)
                bZ, bZB, bZQ, bST, bHO = [P.buf(n) for n in "Zo ZBo ZQo STo HOo".split()]
                hv = hin.rearrange("(c p) t -> p c t", p=128)
                mv = mix.rearrange("(c p) t -> p c t", p=128)
                ov = hout.rearrange("(c p) t -> p c t", p=128)

                def load(idx):
                    c0 = hincol(tiles[idx])
                    K.dma("sp", Hs[idx % 2][:], hv[:, :, c0:c0 + T], [K.db(hinname)], [bHs[idx % 2]])
                    c0 = mixcol(tiles[idx])
                    K.dma("sp", MX[idx % 2][:], mv[:, :, c0:c0 + T], [K.db(mixname)], [bMX[idx % 2]])
                load(0)
                for idx, ti in enumerate(tiles):
                    if idx + 1 < len(tiles):
                        load(idx + 1)
                    H, bH, M_, bM = Hs[idx % 2], bHs[idx % 2], MX[idx % 2], bMX[idx % 2]
                    col = 1 if ti == 32 else 0
                    for m in range(8):
                        py, bpy = bank(0, 6)
                        for kc in range(8):
                            K.mm(py[:, 0:T], WO[:, kc, m * 128:(m + 1) * 128], M_[:, kc, :], kc == 0, kc == 7, [bWO, bM], [bpy])
                        K.stt("dve", Z[:, m, :], py[:, 0:T], modap(l, 5, m, col), H[:, m, :], ALU.mult, ALU.add, [bpy, bH, bMOD], [bZ])
                    layer_norm(Z, bZ, ZB, bZB, ZQ, bZQ, ST, bST, HO, bHO, l * 3 + 1, T, 6, LN_EPS / (DN_ALPHA ** 2))
                    c1 = houtcol(ti)
                    K.dma("sp", ov[:, :, c1:c1 + T], HO[:], [bHO], [K.db(houtname)])
                P.flush()


        natcol = lambda ti: ti * 256
        if "E" in PHASES:
            out_phase(0, ev_w_out, MIX, "MIX", qcol, H1, "H1", natcol, H2, "H2", qcol, TILES_Q)
        if "F" in PHASES:
            ffn_phase(0, 1, H2, "H2", H3, "H3", TILES_Q, qcol, qcol)
        if "G" in PHASES:
            ffn_phase(1, 0, H3, "H3", H4, "H4", TILES_Q, qcol, qcol)

        def proj1_phase():
            with contextlib.ExitStack() as st:
                WIN, bWIN = load_w(st, "WIN1", od_w_in, 8, 2048)
                Hs = [sbuf(st, "Hq%d" % i, [128, 8, T], F32) for i in range(2)]
                bHs = [P.buf("Hq0"), P.buf("Hq1")]
                XM = sbuf(st, "XMq", [128, 8, T], BF16)
                bXM = P.buf("XMq")
                ZR = sbuf(st, "ZR", [128, 1024], BF16)
                bZR = P.buf("ZR")
                OU = Rot(st, "OU", 3, [128, T], F32)
                OB = Rot(st, "OB1", 4, [128, T], BF16)
                VDt = sbuf(st, "VDt", [128, 2, 8, 128], BF16)
                bVDt = [P.buf("VDt0"), P.buf("VDt1")]
                K.memset("pool", VDt[:, :, :, 64:128], 1.0, bVDt)
                K.memset("pool", ZR[:], 0.0, [bZR])
                for c in range(4):
                    K.dma("sp", KD[c, :, 0:256], ZR[:, 0:256], [bZR], [K.db("KD")])
                    K.dma("sp", KD[c, :, NKP - 256:NKP], ZR[:, 0:256], [bZR], [K.db("KD")])
                for r in range(2):
                    K.dma("sp", VD[r * 128:(r + 1) * 128, :], ZR[:], [bZR], [K.db("VD")])
                    K.dma("sp", VD[NKP - 256 + r * 128:NKP - 256 + (r + 1) * 128, :], ZR[:], [bZR], [K.db("VD")])
                srcv = H4.rearrange("(c p) t -> p c t", p=128)

                def load(idx):
                    c0 = qcol(TILES_Q[idx])
                    K.dma("sp", Hs[idx % 2][:], srcv[:, :, c0:c0 + T], [K.db("H4")], [bHs[idx % 2]])
                load(0)
                for idx, ti in enumerate(TILES_Q):
                    if idx + 1 < len(TILES_Q):
                        load(idx + 1)
                    H, bH = Hs[idx % 2], bHs[idx % 2]
                    col = 1 if ti == 32 else 0
                    lat = ti != 32
                    t0 = ti * 256
                    for kc in range(8):
                        K.act(XM[:, kc, :], H[:, kc, :], AF.Identity, [bH, bMOD], [bXM],
                              bias=modap(1, 3, kc, col), scale=modap(1, 4, kc, col))
                    if lat:
                        for c in range(4):
                            pu_, bpu_ = bank()
                            for kc in range(8):
                                K.mm(pu_[:, 0:T], WIN[:, kc, c * 128:(c + 1) * 128], XM[:, kc, :], kc == 0, kc == 7, [bWIN, bXM], [bpu_])
                            ou, bou = OU.get()
                            K.copy("act", ou[:], pu_[:, 0:T], [bpu_], [bou])
                            K.dma("sp", U[c * 128:(c + 1) * 128, t0:t0 + T], ou[:], [bou], [K.db("U")])
                        for c in range(4):
                            pq_, bpq_ = bank()
                            for kc in range(8):
                                K.mm(pq_[:, 0:T], WIN[:, kc, 512 + c * 128:512 + (c + 1) * 128], XM[:, kc, :], kc == 0, kc == 7, [bWIN, bXM], [bpq_])
                            ob, bob = OB.get()
                            K.copy("dve", ob[:], pq_[:, 0:T], [bpq_], [bob])
                            K.dma("sp", QD[c, :, t0:t0 + T], ob[:], [bob], [K.db("QD")])
                    for c in range(4):
                        pk_, bpk_ = bank()
                        for kc in range(8):
                            K.mm(pk_[:, 0:T], WIN[:, kc, 1024 + c * 128:1024 + (c + 1) * 128], XM[:, kc, :], kc == 0, kc == 7, [bWIN, bXM], [bpk_])
                        ob, bob = OB.get()
                        K.copy("dve", ob[:], pk_[:, 0:T], [bpk_], [bob])
                        if lat:
                            K.dma("sp", KD[c, :, 256 + t0:256 + t0 + T], ob[:], [bob], [K.db("KD")])
                        else:
                            K.dma("sp", KDC[c, :, :], ob[:], [bob], [K.db("KDC")])
                    for tb in range(2):
                        pv, bpv = bank()
                        for kc in range(8):
                            K.mm(pv[:, :], XM[:, kc, tb * 128:(tb + 1) * 128], WIN[:, kc, 1536:2048], kc == 0, kc == 7, [bWIN, bXM], [bpv])
                        K.copy("act", VDt[:, tb, :, 0:64], pv[:, :].rearrange("p (h c) -> p h c", c=64), [bpv], [bVDt[tb]])
                        if lat:
                            dst = VD[256 + t0 + tb * 128:256 + t0 + (tb + 1) * 128, :]
                            dn = "VD"
                        else:
                            dst = VDC[tb * 128:(tb + 1) * 128, :]
                            dn = "VDC"
                        K.dma("sp", dst.rearrange("p (h c) -> p h c", c=128), VDt[:, tb, :, :], [bVDt[tb]], [K.db(dn)])
                P.flush()

        if "H" in PHASES:
            proj1_phase()

        def pool_phase():
            with contextlib.ExitStack() as st:
                NP = NQL + 16
                UT = sbuf(st, "UT", [128, NP], F32)
                X1 = sbuf(st, "X1", [128, NP], F32)
                X2 = sbuf(st, "X2", [128, NP], F32)
                IC = sbuf(st, "IC", [128, NQL], F32)
                PMf = sbuf(st, "PMf", [128, NQL], F32)
                PMb = sbuf(st, "PMb", [128, NQL], BF16)
                WP = sbuf(st, "WP", [128, 128], BF16)
                PSC = sbuf(st, "PSC", [128, 4], F32)
                bUT, bX1, bX2, bIC, bPMf, bPMb, bWP, bPSC = [P.buf(n) for n in "UT X1 X2 IC PMf PMb WP PSC".split()]
                OO = Rot(st, "OOp", 3, [128, 512], BF16)
                K.dma("sp", PSC[:], od_ps_t[:, :], [], [bPSC])
                for g in range(4):
                    K.memset("pool", UT[:, 0:8], 0.0, [bUT])
                    K.memset("pool", UT[:, NP - 8:NP], 0.0, [bUT])
                    K.dma("sp", UT[:, 8:8 + NQL], U[g * 128:(g + 1) * 128, :], [K.db("U")], [bUT])
                    K.dma("sp", IC[:], icnt[:, g, :], [], [bIC])
                    K.dma("pool", WP[:], od_w_pool[g, :, :], [], [bWP])
                    A, bA = UT, bUT
                    outs = [(X1, bX1), (X2, bX2)]
                    for s_ in range(g + 1):
                        sh = 1 << s_
                        B, bB = outs[s_ % 2]
                        K.copy("pool", B[:, 0:sh], A[:, 0:sh], [bA], [bB])
                        K.tt("dve", B[:, sh:NP], A[:, sh:NP], A[:, 0:NP - sh], ALU.add, [bA], [bB])
                        A, bA = B, bB
                    right = (0, 1, 3, 7)[g]
                    K.tt("dve", PMf[:], A[:, 8 + right:8 + right + NQL], IC[:], ALU.mult, [bA, bIC], [bPMf])
                    K.tt("pool", PMb[:], PMf[:], UT[:, 8:8 + NQL], ALU.subtract, [bPMf, bUT], [bPMb])
                    for ch in range(9):
                        c0 = ch * 512
                        N = 512 if ch < 8 else 256
                        pp, bpp = bank()
                        K.mm(pp[:, 0:N], WP[:], PMb[:, c0:c0 + N], True, True, [bWP, bPMb], [bpp])
                        oo, boo = OO.get()
                        K.act(oo[:, 0:N], pp[:, 0:N], AF.Identity, [bpp, bPSC], [boo], scale=PSC[:, g:g + 1])
                        K.dma("sp", MIXD[g * 128:(g + 1) * 128, c0:c0 + N], oo[:, 0:N], [boo], [K.db("MIXD")])
                P.flush()

        def na_phase():
            with contextlib.ExitStack() as st:
                BIAS = sbuf(st, "BIAS", [128, 3, 6 * 8 * 256], BF16)
                bBIAS = P.buf("BIAS")
                IDN = sbuf(st, "IDN", [128, 128], BF16)
                bIDN = P.buf("IDN")
                K.dma("pool", IDN[:], ident[:, :], [], [bIDN])
                for t_ in range(3):
                    for hh in range(2):
                        K.dma("pool", BIAS[:, t_, hh * 6144:(hh + 1) * 6144], na_b[t_, :, hh * 6144:(hh + 1) * 6144], [], [bBIAS])
                for t_ in range(3):
                    K.ts("dve" if t_ % 2 == 0 else "pool", BIAS[:, t_, :], BIAS[:, t_, :], 1.0 / D_SCALE, ALU.mult, [bBIAS], [bBIAS])
                KDb = [sbuf(st, "KDb%d" % i, [128, 4, 768], BF16) for i in range(2)]
                VDb = [sbuf(st, "VDb%d" % i, [128, 6, 1024], BF16) for i in range(2)]
                QDb = [sbuf(st, "QDb%d" % i, [128, 4, 256], BF16) for i in range(2)]
                bKDb = [P.buf("KDb%d" % i) for i in range(2)]
                bVDb = [P.buf("VDb%d" % i) for i in range(2)]
                bQDb = [P.buf("QDb%d" % i) for i in range(2)]
                KC = sbuf(st, "KC", [128, 4, 256], BF16)
                VC = sbuf(st, "VC", [128, 2, 1024], BF16)
                bKC, bVC = P.buf("KC"), P.buf("VC")
                K.dma("sp", KC[:], KDC.rearrange("c p n -> p c n"), [K.db("KDC")], [bKC])
                K.dma("sp", VC[:], VDC.rearrange("(kt p) n -> p kt n", p=128), [K.db("VDC")], [bVC])
                SBt = Rot(st, "SBt", 4, [128, 256], F32)
                PT = Rot(st, "PTn", 6, [128, 256], BF16)
                RR = sbuf(st, "RRn", [64, 256], F32)
                bRR = P.buf("RRn")
                OO = Rot(st, "OOn", 3, [64, 256], BF16)
                KDv = KD.rearrange("c p n -> p c n")
                QDv = QD.rearrange("c p n -> p c n")
                VDv = VD.rearrange("(kt p) n -> p kt n", p=128)

                def load(bi):
                    i0 = 4 * bi
                    K.dma("sp", KDb[bi % 2][:], KDv[:, :, i0 * 64:i0 * 64 + 768], [K.db("KD")], [bKDb[bi % 2]])
                    K.dma("sp", VDb[bi % 2][:], VDv[:, i0 // 2:i0 // 2 + 6, :], [K.db("VD")], [bVDb[bi % 2]])
                    K.dma("sp", QDb[bi % 2][:], QDv[:, :, bi * 256:(bi + 1) * 256], [K.db("QD")], [bQDb[bi % 2]])
                load(0)
                for bi in range(17):
                    if bi + 1 < 17:
                        load(bi + 1)
                    kd, vd, qd = KDb[bi % 2], VDb[bi % 2], QDb[bi % 2]
                    bk, bv, bq = bKDb[bi % 2], bVDb[bi % 2], bQDb[bi % 2]
                    tab = 0 if bi == 0 else (2 if bi == 16 else 1)
                    steps = [(h, kt) for h in range(8) for kt in range(8)]
                    Sq = {}

                    def emitS(s):
                        h, kt = steps[s]
                        c, po = h // 2, (h % 2) * 64
                        S, bS = bank(0, 6)
                        if kt < 6:
                            boff = (kt * 8 + h) * 256
                            K.mm(S[:, 0:256], kd[po:po + 64, c, kt * 128:(kt + 1) * 128], qd[po:po + 64, c, :], True, False, [bk, bq], [bS])
                            K.mm(S[:, 0:256], IDN[:], BIAS[:, tab, boff:boff + 256], False, True, [bIDN, bBIAS], [bS])
                        else:
                            kk = kt - 6
                            K.mm(S[:, 0:256], KC[po:po + 64, c, kk * 128:(kk + 1) * 128], qd[po:po + 64, c, :], True, True, [bKC, bq], [bS])
                        Sq[s] = (S, bS)
                    DEPTH = 3
                    for s in range(DEPTH):
                        emitS(s)
                    for s in range(len(steps)):
                        if s + DEPTH < len(steps):
                            emitS(s + DEPTH)
                        h, kt = steps[s]
                        S, bS = Sq.pop(s)
                        O, bO = PS[6 + h % 2], PSB[6 + h % 2][0]
                        pt, bpt = PT.get()
                        if kt < 6:
                            K.act(pt[:], S[:, 0:256], AF.Exp, [bS], [bpt], scale=D_SCALE)
                            K.mm(O[:, 0:256], vd[:, kt, h * 128:(h + 1) * 128], pt[:], kt == 0, False, [bv, bpt], [bO])
                        else:
                            kk = kt - 6
                            K.act(pt[:], S[:, 0:256], AF.Exp, [bS], [bpt], scale=D_SCALE)
                            K.mm(O[:, 0:256], VC[:, kk, h * 128:(h + 1) * 128], pt[:], False, kt == 7, [bVC, bpt], [bO])
                        if kt == 7:
                            K.recip(RR[:], O[64:128, 0:256], [bO], [bRR])
                            oo, boo = OO.get()
                            K.tt("dve", oo[:], O[0:64, 0:256], RR[:], ALU.mult, [bO, bRR], [boo])
                            K.dma("sp", MIXD[512 + 64 * h:512 + 64 * (h + 1), bi * 256:(bi + 1) * 256], oo[:], [boo], [K.db("MIXD")])
                P.flush()

        if "I" in PHASES:
            pool_phase()
            na_phase()
        TILES_L = list(range(17))
        if "J" in PHASES:
            out_phase(1, od_w_out, MIXD, "MIXD", natcol, H4, "H4", qcol, H5, "H5", natcol, TILES_L)
        if "K" in PHASES:
            ffn_phase(1, 1, H5, "H5", outT, "outT", TILES_L, natcol, natcol)
    return nc


def _perm_sign(rot_dim):
    q = rot_dim // 4
    perm = np.zeros(rot_dim, np.int64)
    sign = np.zeros(rot_dim, np.float32)
    for i in range(rot_dim):
        qq = i // q
        if qq % 2 == 0:
            perm[i] = i + q
            sign[i] = -1.0
        else:
            perm[i] = i - q
            sign[i] = 1.0
    return perm, sign


def _rope_tables(pos, rot_dim):
    q = rot_dim // 4
    row = (pos // 64).astype(np.float32)
    col = (pos % 64).astype(np.float32)
    inv = (np.float32(10000.0) ** (-np.arange(q, dtype=np.float32) / np.float32(q))).astype(np.float32)
    ang_r = row[:, None] * inv[None, :]
    ang_c = col[:, None] * inv[None, :]
    ang = np.concatenate([ang_r, ang_r, ang_c, ang_c], -1).astype(np.float32)
    return np.cos(ang).astype(np.float32), np.sin(ang).astype(np.float32)


def core_positions(half):
    s0 = 0 if half == 0 else 3840
    own = np.arange(s0, s0 + NQL)
    rest = np.concatenate([np.arange(0, s0), np.arange(s0 + NQL, NL)])
    return np.concatenate([own, rest]), s0


def prepare_inputs(inp):
    f = lambda a: np.ascontiguousarray(np.asarray(a, dtype=np.float32))
    x, c, ctx, c_ctx = f(inp["x"]), f(inp["c"]), f(inp["ctx"]), f(inp["c_ctx"])
    shared = {}
    shared["ada_w"] = f(inp["ada_w"])
    ada_b = f(inp["ada_b"])
    abt = ada_b.reshape(2, 72, 128).transpose(2, 0, 1)
    shared["ada_b_t"] = np.ascontiguousarray(np.stack([abt, abt], -1))
    shared["ln_g_t"] = np.ascontiguousarray(f(inp["ln_g"]).reshape(2, 3, 8, 128).transpose(3, 0, 1, 2).reshape(128, 48))
    shared["ln_b_t"] = np.ascontiguousarray(f(inp["ln_b"]).reshape(2, 3, 8, 128).transpose(3, 0, 1, 2).reshape(128, 48))
    shared["wg"] = f(inp["ffn_w_gate"])
    shared["wu"] = f(inp["ffn_w_up"])
    shared["wd"] = f(inp["ffn_w_down"])
    w_in = f(inp["ev_w_in"])[0]
    shared["ev_w_in"] = w_in
    pa, sga = _perm_sign(32)
    pb, sgb = _perm_sign(64)
    wperm = np.zeros((D, 1024), np.float32)
    for blk in range(8):
        wperm[:, blk * 64:(blk + 1) * 64] = w_in[:, 256 + blk * 64 + pb]
        wperm[:, 512 + blk * 64:512 + (blk + 1) * 64] = w_in[:, 928 + blk * 64 + pb]
    shared["ev_w_in_perm"] = wperm
    wkpe = np.zeros((D, 2, 96), np.float32)
    wkpe[:, 0, 64:] = w_in[:, 896:928]
    wkpe[:, 1, 64:] = w_in[:, 896 + pa]
    shared["ev_w_kpe"] = wkpe.reshape(D, 192)
    w_uq = f(inp["ev_w_uq"])[0]
    shared["ev_w_uq"] = w_uq
    wuqp = w_uq.copy()
    for h in range(8):
        wuqp[:, 96 * h + 64:96 * h + 96] = w_uq[:, 96 * h + 64 + pa]
    shared["ev_w_uq_perm"] = wuqp
    w_ukv = f(inp["ev_w_ukv"])[0].reshape(128, 8, 128)
    wk = np.zeros((128, 8, 96), np.float32)
    wk[:, :, :64] = w_ukv[:, :, :64]
    shared["ev_w_ukv_k"] = wk.reshape(128, 768)
    shared["ev_w_ukv_v"] = np.ascontiguousarray(w_ukv[:, :, 64:].reshape(128, 512))
    small = np.zeros((128, 4), np.float32)
    gq = f(inp["ev_g_qlat"])[0]
    small[:, 0] = gq[:128]
    small[:, 1] = gq[128:]
    small[:, 2] = f(inp["ev_g_kvlat"])[0]
    small[:, 3] = f(inp["ev_g_sub"])[0]
    shared["ev_small"] = small
    shared["ev_lam_bc"] = np.ascontiguousarray(np.broadcast_to(f(inp["ev_lam"])[0].reshape(1, 256), (128, 256)))
    shared["ev_w_out"] = f(inp["ev_w_out"])[0]
    shared["od_w_in"] = f(inp["od_w_in"])[0]
    shared["od_w_out"] = f(inp["od_w_out"])[0]
    shared["od_w_pool"] = f(inp["od_w_pool"])[0]
    shared["od_ps_t"] = np.ascontiguousarray(f(inp["od_pool_scale"])[0].reshape(4, 128).T)
    rpb = f(inp["od_rpb"])[0]

    per_half = []
    for half in range(2):
        d = {}
        pos, s0 = core_positions(half)
        cA, sA = _rope_tables(pos, 32)
        cB, sB = _rope_tables(pos, 64)
        ca = np.ones((96, NT), np.float32)
        sa = np.zeros((96, NT), np.float32)
        ca[64:, :NL] = cA.T
        sa[64:, :NL] = (sA * sga[None, :]).T
        cbt = np.ones((128, NT), np.float32)
        sbt = np.zeros((128, NT), np.float32)
        cbt[:64, :NL] = cB.T
        cbt[64:, :NL] = cB.T
        sbt[:64, :NL] = (sB * sgb[None, :]).T
        sbt[64:, :NL] = (sB * sgb[None, :]).T
        d["ca"], d["sa"], d["cb"], d["sb"] = ca, sa, cbt, sbt
        ic = np.ones((4, NQL), np.float32)
        tg = s0 + np.arange(NQL)
        for g, w in enumerate((2, 4, 8, 16)):
            left = w // 2
            right = w - 1 - left
            lo = np.clip(tg - left, 0, NL)
            hi = np.clip(tg + right + 1, 0, NL)
            ic[g] = 1.0 / (hi - lo).astype(np.float32)
        d["icnt"] = np.ascontiguousarray(np.broadcast_to(ic[None], (128, 4, NQL)))
        roff = 0 if half == 0 else 60
        gtab = np.zeros((3, 128, 6, 8, 256), np.float32)
        mtab = np.zeros((3, 128, 6, 8, 256), np.float32)
        qc = np.arange(64)
        kc = np.arange(64)
        cs = np.clip(qc - 8, 0, 48)
        colv = (kc[None, :] >= cs[:, None]) & (kc[None, :] < cs[:, None] + 16)
        cidx = np.clip(kc[None, :] - qc[:, None] + 15, 0, 30)
        for tix, bi in enumerate((0, 8, 16)):
            i0 = 4 * bi
            for a in range(4):
                R = i0 + a + roff
                rs = int(np.clip(R - 4, 0, 120))
                for cc in range(12):
                    kr = i0 - 4 + cc
                    KR = kr + roff
                    rowv = (rs <= KR < rs + 8) and (0 <= kr < 68)
                    kt, ph = cc // 2, (cc % 2) * 64
                    if rowv:
                        vals = rpb[:, KR - R + 7, :][:, cidx]
                        gtab[tix, ph:ph + 64, kt, :, a * 64:(a + 1) * 64] = vals.transpose(2, 0, 1)
                        mtab[tix, ph:ph + 64, kt, :, a * 64:(a + 1) * 64] = np.where(colv.T[:, None, :], 0.0, -1e30)
                    else:
                        mtab[tix, ph:ph + 64, kt, :, a * 64:(a + 1) * 64] = -1e30
        d["na_b"] = np.where(mtab < -1.0, np.float32(-1e30), gtab).reshape(3, 128, 6 * 8 * 256)
        d["pos"] = pos
        per_half.append(d)

    in_maps = []
    for cid in range(8):
        b, half = cid // 2, cid % 2
        d = per_half[half]
        m = dict(shared)
        xtc = np.empty((D, NT), np.float32)
        xtc[:, :NL] = x[b][d["pos"]].T
        xtc[:, NL:] = ctx[b].T
        m["xt"] = xtc
        ctt = np.empty((128, 8, 2), np.float32)
        ctt[:, :, 0] = c[b].reshape(8, 128).T
        ctt[:, :, 1] = c_ctx.reshape(8, 128).T
        m["ct"] = ctt
        for k in ("ca", "sa", "cb", "sb", "icnt", "na_b"):
            m[k] = d[k]
        m["ident"] = np.eye(128, dtype=np.float32)
        in_maps.append(m)
    return in_maps


_NC_CACHE = {}


def kernel(**inputs):
    in_maps = prepare_inputs(inputs)
    if "nc" not in _NC_CACHE:
        _NC_CACHE["nc"] = build_program(False)
    nc = _NC_CACHE["nc"]
    res = run_bass_kernel_spmd(nc, in_maps, core_ids=list(range(8)))
    out = np.empty((4, NL, D), np.float32)
    for cid in range(8):
        b, half = cid // 2, cid % 2
        o = res.results[cid]["outT"]
        if half == 0:
            out[b, 0:4096] = o[:, 0:4096].T
        else:
            out[b, 4096:8192] = o[:, 256:4352].T
    return out
```

```python
import math
import os
import contextlib
import numpy as np
import concourse.bass as bass
import concourse.mybir as mybir
from concourse.bass_utils import run_bass_kernel_spmd

F32 = mybir.dt.float32
BF16 = mybir.dt.bfloat16
ALU = mybir.AluOpType
AF = mybir.ActivationFunctionType
AX = mybir.AxisListType

ENGS = ("pe", "act", "dve", "pool", "sp")

D = 1024
DFF = 2816
NFF = 22
T = 256
NL = 8192
NT = 8448
NQL = 4352
NQ = 4608
TILES_ALL = list(range(33))
TILES_Q = list(range(17)) + [32]
DN_ALPHA = float(4 ** 0.25)
LN_EPS = 1e-6
A_SCALE = float(96 ** -0.5)
B_SCALE = float(64 ** -0.5)
D_SCALE = float(64 ** -0.5)
LAM_INIT0 = 0.8 - 0.6 * math.exp(0.0)
NPADR = 76
NKP = NPADR * 64


def qcol(ti):
    return ti * 256 if ti < 17 else 4352


class Buf:
    __slots__ = ("name", "w", "r", "sem", "cnt", "excl")

    def __init__(self, name):
        self.name = name
        self.excl = False
        self.w = None
        self.r = []
        self.sem = None
        self.cnt = 0


class Prog:
    def __init__(self, nc):
        self.nc = nc
        self.streams = {e: [] for e in ENGS}
        self.count = {e: 0 for e in ENGS}
        self.known = {e: {} for e in ENGS}
        self.nsem_dma = 0
        self.dma_sems = []
        self.nflush = 0
        self.bufs = []

    def buf(self, name="b"):
        b = Buf(name)
        self.bufs.append(b)
        return b

    def _deps(self, eng, reads, writes):
        deps = {}

        def add(ev):
            if ev is None:
                return
            k, v = ev
            if deps.get(k, 0) < v:
                deps[k] = v
        for b in reads:
            add(b.w)
        for b in writes:
            add(b.w)
            for ev in b.r:
                add(ev)
        out = []
        kn = self.known[eng]
        for k, v in deps.items():
            if k == eng and eng == "pe":
                continue
            if kn.get(k, 0) >= v:
                continue
            kn[k] = v
            out.append((k, v))
        return out

    def _post(self, ev, reads, writes):
        for b in reads:
            if len(b.r) > 64:
                mx = {}
                for k, v in b.r:
                    if mx.get(k, 0) < v:
                        mx[k] = v
                b.r = list(mx.items())
            b.r.append(ev)
        for b in writes:
            b.w = ev
            b.r = []

    def op(self, eng, fn, reads=(), writes=()):
        if any(b.excl for b in reads):
            writes = list(writes) + [b for b in reads if b.excl]
            reads = [b for b in reads if not b.excl]
        waits = self._deps(eng, reads, writes)
        self.count[eng] += 1
        ev = (eng, self.count[eng])
        self.streams[eng].append((waits, fn, ev))
        self._post(ev, reads, writes)
        return ev

    def dma(self, q, fn, reads=(), writes=(), sembuf=None):
        sb = sembuf if sembuf is not None else writes[0]
        if sb.sem is None:
            sb.sem = "d%d" % self.nsem_dma
            self.nsem_dma += 1
            self.dma_sems.append(sb.sem)
        waits = self._deps(q, reads, writes)
        sb.cnt += 16
        ev = (sb.sem, sb.cnt)
        self.streams[q].append((waits, fn, ev))
        self._post(ev, reads, writes)
        return ev

    def flush(self):
        nc = self.nc
        with contextlib.ExitStack() as st:
            st.enter_context(nc.cleanup_on_exit())
            sems = {}
            for e in ENGS:
                sems[e] = nc.alloc_semaphore(name="s%d_%s" % (self.nflush, e))
            for k in self.dma_sems:
                sems[k] = nc.alloc_semaphore(name="s%d_%s" % (self.nflush, k))
            block = st.enter_context(nc.Block())
            final = {k: 0 for k in sems}
            for e in ENGS:
                for (_, _, ev) in self.streams[e]:
                    if ev is not None:
                        final[ev[0]] = max(final[ev[0]], ev[1])

            def run(eng_name):
                def body(eng):
                    for waits, fn, ev in self.streams[eng_name]:
                        for k, v in waits:
                            eng.wait_ge(sems[k], v)
                        ins = fn(eng)
                        if ev[0] in ENGS:
                            ins.then_inc(sems[ev[0]], 1)
                        else:
                            ins.then_inc(sems[ev[0]], 16)
                    if eng_name == "sp":
                        for k, v in final.items():
                            if v > 0:
                                eng.wait_ge(sems[k], v)
                return body
            block.tensor(run("pe"))
            block.scalar(run("act"))
            block.vector(run("dve"))
            block.gpsimd(run("pool"))
            block.sync(run("sp"))
        self.nflush += 1
        self.streams = {e: [] for e in ENGS}
        self.count = {e: 0 for e in ENGS}
        self.known = {e: {} for e in ENGS}
        self.nsem_dma = 0
        self.dma_sems = []
        for b in self.bufs:
            b.w = None
            b.r = []
            b.sem = None
            b.cnt = 0


class KB:
    def __init__(self, nc):
        self.nc = nc
        self.P = Prog(nc)
        self.dr = {}

    def din(self, name, shape, dt=F32):
        ap = self.nc.dram_tensor(name, list(shape), dt, kind="ExternalInput").ap()
        self.dr[name] = (ap, self.P.buf(name))
        return ap

    def dout(self, name, shape, dt=F32):
        ap = self.nc.dram_tensor(name, list(shape), dt, kind="ExternalOutput").ap()
        self.dr[name] = (ap, self.P.buf(name))
        return ap

    def dscr(self, name, shape, dt=F32, debug=False):
        if debug:
            return self.dout(name, shape, dt)
        ap = self.nc.dram_tensor(name, list(shape), dt).ap()
        self.dr[name] = (ap, self.P.buf(name))
        return ap

    def db(self, name):
        return self.dr[name][1]

    def mm(self, out, lhsT, rhs, start, stop, r, w):
        self.P.op("pe", lambda e: e.matmul(out, lhsT=lhsT, rhs=rhs, start=start, stop=stop), r, w)

    def act(self, out, in_, func, r, w, bias=None, scale=None):
        kw = {}
        if bias is not None:
            kw["bias"] = bias
        if scale is not None:
            kw["scale"] = scale
        self.P.op("act", lambda e: e.activation(out=out, in_=in_, func=func, **kw), r, w)

    def tt(self, eng, out, in0, in1, op, r, w):
        self.P.op(eng, lambda e: e.tensor_tensor(out=out, in0=in0, in1=in1, op=op), r, w)

    def ts(self, eng, out, in0, s1, op0, r, w, s2=None, op1=None):
        if op1 is None:
            self.P.op(eng, lambda e: e.tensor_scalar(out=out, in0=in0, scalar1=s1, scalar2=None, op0=op0), r, w)
        else:
            self.P.op(eng, lambda e: e.tensor_scalar(out=out, in0=in0, scalar1=s1, scalar2=s2, op0=op0, op1=op1), r, w)

    def stt(self, eng, out, in0, scalar, in1, op0, op1, r, w):
        self.P.op(eng, lambda e: e.scalar_tensor_tensor(out=out, in0=in0, scalar=scalar, in1=in1, op0=op0, op1=op1), r, w)

    def copy(self, eng, out, in_, r, w):
        if eng == "act":
            self.P.op("act", lambda e: e.copy(out=out, in_=in_), r, w)
        else:
            self.P.op(eng, lambda e: e.tensor_copy(out=out, in_=in_), r, w)

    def recip(self, out, in_, r, w):
        self.P.op("dve", lambda e: e.reciprocal(out=out, in_=in_), r, w)

    def memset(self, eng, ap, val, w):
        self.P.op(eng, lambda e: e.memset(ap, val), (), w)

    def dma(self, q, out, in_, r, w, sembuf=None):
        self.P.dma(q, lambda e: e.dma_start(out=out, in_=in_), r, w, sembuf)


def build_program(debug=False):
    nc = bass.Bass("TRN2", target_bir_lowering=False)
    K = KB(nc)
    P = K.P
    dbg = debug

    xt = K.din("xt", [D, NT])
    ct = K.din("ct", [128, 8, 2])
    ada_w = K.din("ada_w", [2, D, 9 * D])
    ada_b_t = K.din("ada_b_t", [128, 2, 72, 2])
    ln_g_t = K.din("ln_g_t", [128, 48])
    ln_b_t = K.din("ln_b_t", [128, 48])
    wg = K.din("wg", [2, 2, D, DFF])
    wu = K.din("wu", [2, 2, D, DFF])
    wd = K.din("wd", [2, 2, DFF, D])
    ev_w_in = K.din("ev_w_in", [D, 1952])
    ev_w_in_perm = K.din("ev_w_in_perm", [D, 1024])
    ev_w_kpe = K.din("ev_w_kpe", [D, 192])
    ev_w_uq = K.din("ev_w_uq", [256, 768])
    ev_w_uq_perm = K.din("ev_w_uq_perm", [256, 768])
    ev_w_ukv_k = K.din("ev_w_ukv_k", [128, 768])
    ev_w_ukv_v = K.din("ev_w_ukv_v", [128, 512])
    ev_small = K.din("ev_small", [128, 4])
    ev_lam_bc = K.din("ev_lam_bc", [128, 256])
    ev_w_out = K.din("ev_w_out", [D, D])
    ca = K.din("ca", [96, NT])
    sa = K.din("sa", [96, NT])
    cb = K.din("cb", [128, NT])
    sb_ = K.din("sb", [128, NT])
    od_w_in = K.din("od_w_in", [D, 2048])
    od_w_out = K.din("od_w_out", [D, D])
    od_w_pool = K.din("od_w_pool", [4, 128, 128])
    od_ps_t = K.din("od_ps_t", [128, 4])
    icnt = K.din("icnt", [128, 4, NQL])
    na_b = K.din("na_b", [3, 128, 6 * 8 * 256])
    ident = K.din("ident", [128, 128])

    outT = K.dout("outT", [D, NQL])

    H1 = K.dscr("H1", [D, NT], F32, dbg)
    QA = K.dscr("QA", [8, 96, NQ], BF16)
    KA = K.dscr("KA", [8, 96, NT], BF16)
    VA = K.dscr("VA", [NT, 8 * 128], BF16)
    QB = K.dscr("QB", [4, 128, NQ], BF16)
    KBs = K.dscr("KBs", [4, 128, NT], BF16)
    VB = K.dscr("VB", [NT, 512], BF16)
    MIX = K.dscr("MIX", [D, NQ], BF16, dbg)
    H2 = K.dscr("H2", [D, NQ], F32, dbg)
    H3 = K.dscr("H3", [D, NQ], F32, dbg)
    H4 = K.dscr("H4", [D, NQ], F32, dbg)
    U = K.dscr("U", [512, NQL], F32)
    QD = K.dscr("QD", [4, 128, NQL], BF16)
    KD = K.dscr("KD", [4, 128, NKP], BF16)
    KDC = K.dscr("KDC", [4, 128, 256], BF16)
    VD = K.dscr("VD", [NKP, 8 * 128], BF16)
    VDC = K.dscr("VDC", [256, 8 * 128], BF16)
    MIXD = K.dscr("MIXD", [D, NQL], BF16, dbg)
    H5 = K.dscr("H5", [D, NQL], F32, dbg)

    with contextlib.ExitStack() as top:
        uid = [0]

        def sbuf(st, name, shape, dt):
            uid[0] += 1
            return st.enter_context(nc.sbuf_tensor("%s_%d" % (name, uid[0]), list(shape), dt))

        PS = [top.enter_context(nc.psum_tensor("ps%d" % i, [128, 512], F32)) for i in range(8)]
        PSB = []
        for i in range(8):
            _b = P.buf("ps%d" % i)
            _b.excl = True
            PSB.append([_b, _b])

        MOD = sbuf(top, "MOD", [128, 2, 72, 2], F32)
        bMOD = P.buf("MOD")
        LNG = sbuf(top, "LNG", [128, 48], F32)
        LNB = sbuf(top, "LNB", [128, 48], F32)
        bLN = P.buf("LN")
        ONES = sbuf(top, "ONES", [128, 4, 128], BF16)
        bONES = P.buf("ONES")
        EPSC = sbuf(top, "EPSC", [128, 1], F32)

        with contextlib.ExitStack() as st:
            SC = sbuf(st, "SC", [128, 8, 2], F32)
            SG0 = sbuf(st, "SG0", [128, 8, 2], F32)
            ADB = sbuf(st, "ADB", [128, 2, 72, 2], F32)
            WS = [sbuf(st, "WS%d" % i, [128, 8, 1024], F32) for i in range(2)]
            bSC, bADB = P.buf("SC"), P.buf("ADB")
            bWS = [P.buf("WS0"), P.buf("WS1")]
            K.dma("sp", SC[:], ct[:, :, :], [K.db("ct")], [bSC])
            K.dma("sp", ADB[:], ada_b_t[:, :, :, :], [K.db("ada_b_t")], [bADB])
            K.dma("sp", LNG[:], ln_g_t[:, :], [], [bLN])
            K.dma("sp", LNB[:], ln_b_t[:, :], [], [bLN])
            bSG0 = P.buf("SG0")
            K.act(SG0[:], SC[:], AF.Silu, [bSC], [bSG0])
            K.copy("dve", SC[:], SG0[:], [bSG0], [bSC])
            K.memset("pool", ONES[:, 0, :], 1.0 / 1024.0, [bONES])
            K.memset("pool", ONES[:, 1, :], 1.0 / 128.0, [bONES])
            K.memset("pool", ONES[:, 2, :], 1.0 / 256.0, [bONES])
            K.memset("pool", ONES[:, 3, :], 1.0, [bONES])
            K.memset("pool", EPSC[:], 0.0, [bONES])
            n = 0
            for l in range(2):
                psA = PS[l]
                for s in range(9):
                    wsl = WS[n % 2]
                    bw = bWS[n % 2]
                    n += 1
                    src = ada_w[l, :, s * 1024:(s + 1) * 1024].rearrange("(c p) n -> p c n", p=128)
                    for kc in range(8):
                        K.dma("sp" if kc % 2 == 0 else "act", wsl[:, kc, :], src[:, kc, :], [], [bw])
                    for m in range(8):
                        col = (s * 8 + m) * 2
                        for kc in range(8):
                            K.mm(psA[:, col:col + 2], wsl[:, kc, m * 128:(m + 1) * 128], SC[:, kc, :],
                                 kc == 0, kc == 7, [bw, bSC], [PSB[l][0]])
                K.tt("dve", MOD[:, l, :, :], psA[:, 0:144].rearrange("p (a b) -> p a b", b=2), ADB[:, l, :, :],
                     ALU.add, [PSB[l][0], bADB], [bMOD])
                for i in (1, 4, 7):
                    K.ts("dve", MOD[:, l, i * 8:(i + 1) * 8, :], MOD[:, l, i * 8:(i + 1) * 8, :], 1.0, ALU.add,
                         [bMOD], [bMOD])
                for i, f in ((2, 0.5 / DN_ALPHA), (5, 1.0 / DN_ALPHA), (8, 0.5 / DN_ALPHA)):
                    K.ts("dve", MOD[:, l, i * 8:(i + 1) * 8, :], MOD[:, l, i * 8:(i + 1) * 8, :], f, ALU.mult,
                         [bMOD], [bMOD])
            P.flush()

        def modap(l, i, m, col):
            return MOD[:, l, i * 8 + m, col:col + 1]

        def load_w(st, name, src2d, kc, ncols, eng_q="pool"):
            t = sbuf(st, name, [128, kc, ncols], BF16)
            b = P.buf(name)
            src = src2d.rearrange("(c p) n -> p c n", p=128)
            for c in range(kc):
                K.dma(eng_q, t[:, c, :], src[:, c, :], [], [b])
            return t, b

        def layer_norm(Z, bZ, ZB, bZB, ZQ, bZQ, ST, bST, Hout, bH, gcol, width, psb_idx, eps):
            w_ = width
            for m in range(8):
                K.copy("pool", ZB[:, m, :w_], Z[:, m, :w_], [bZ], [bZB])
                K.act(ZQ[:, m, :w_], Z[:, m, :w_], AF.Square, [bZ], [bZQ])
            pm, bpm = PS[psb_idx][:, 0:w_], PSB[psb_idx][0]
            pq, bpq = PS[psb_idx + 1][:, 0:w_], PSB[psb_idx + 1][0]
            for m in range(8):
                K.mm(pm, ONES[:, 0, :], ZB[:, m, :w_], m == 0, m == 7, [bZB, bONES], [bpm])
            for m in range(8):
                K.mm(pq, ONES[:, 0, :], ZQ[:, m, :w_], m == 0, m == 7, [bZQ, bONES], [bpq])
            mean, m2, rstd = ST[:, 0, :w_], ST[:, 1, :w_], ST[:, 2, :w_]
            K.copy("act", mean, pm, [bpm], [bST])
            K.tt("pool", m2, mean, mean, ALU.mult, [bST], [bST])
            K.stt("dve", rstd, pq, eps, m2, ALU.add, ALU.subtract, [bpq, bST], [bST])
            K.act(rstd, rstd, AF.Sqrt, [bST], [bST])
            K.recip(rstd, rstd, [bST], [bST])
            for m in range(8):
                K.tt("dve", Z[:, m, :w_], Z[:, m, :w_], mean, ALU.subtract, [bZ, bST], [bZ])
                K.tt("pool", Z[:, m, :w_], Z[:, m, :w_], rstd, ALU.mult, [bZ, bST], [bZ])
                K.act(Hout[:, m, :w_], Z[:, m, :w_], AF.Identity, [bZ, bLN], [bH],
                      bias=LNB[:, gcol * 8 + m:gcol * 8 + m + 1], scale=LNG[:, gcol * 8 + m:gcol * 8 + m + 1])

        def ffn_phase(l, f, src, srcname, dst, dstname, tiles, src_col, dst_col):
            with contextlib.ExitStack() as st:
                WG, bWG = load_w(st, "WG", wg[l, f], 8, DFF)
                WU, bWU = load_w(st, "WU", wu[l, f], 8, DFF)
                WD, bWD = load_w(st, "WD", wd[l, f], NFF, D)
                Hs = [sbuf(st, "Hs%d" % i, [128, 8, T], F32) for i in range(2)]
                bHs = [P.buf("H0"), P.buf("H1")]
                XMs = [sbuf(st, "XM%d" % i, [128, 8, T], BF16) for i in range(2)]
                bXMs = [P.buf("XM0"), P.buf("XM1")]
                HID = sbuf(st, "HID", [128, NFF, T], BF16)
                bHID = P.buf("HID")
                Zs = [sbuf(st, "Z%d" % i, [128, 8, T], F32) for i in range(2)]
                bZs = [P.buf("Z0"), P.buf("Z1")]
                ZBs = [sbuf(st, "ZB%d" % i, [128, 8, T], BF16) for i in range(2)]
                bZBs = [P.buf("ZB0"), P.buf("ZB1")]
                ZQs = [sbuf(st, "ZQ%d" % i, [128, 8, T], BF16) for i in range(2)]
                bZQs = [P.buf("ZQ0"), P.buf("ZQ1")]
                SG = [sbuf(st, "SG%d" % i, [128, T], F32) for i in range(2)]
                bSG = [P.buf("SG0"), P.buf("SG1")]
                ST = sbuf(st, "ST", [128, 3, T], F32)
                bST = P.buf("ST")
                NH = sbuf(st, "NH", [128, T], F32)
                bNH = P.buf("NH")
                K.memset("pool", NH[:], -0.5, [bNH])
                srcv = src.rearrange("(c p) t -> p c t", p=128)
                dstv = dst.rearrange("(c p) t -> p c t", p=128)
                gcol = l * 3 + (0 if f == 0 else 2)
                mi = 0 if f == 0 else 6
                eps = LN_EPS / (DN_ALPHA ** 2)
                n = len(tiles)

                def load(idx):
                    c0 = src_col(tiles[idx])
                    K.dma("sp", Hs[idx % 2][:], srcv[:, :, c0:c0 + T], [K.db(srcname)], [bHs[idx % 2]])

                def s1a(idx):
                    ti = tiles[idx]
                    H, bH, XM, bXM = Hs[idx % 2], bHs[idx % 2], XMs[idx % 2], bXMs[idx % 2]
                    col = 1 if ti == 32 else 0
                    for kc in range(8):
                        K.act(XM[:, kc, :], H[:, kc, :], AF.Identity, [bH, bMOD], [bXM],
                              bias=modap(l, mi, kc, col), scale=modap(l, mi + 1, kc, col))

                def s1b(idx):
                    XM, bXM = XMs[idx % 2], bXMs[idx % 2]
                    for j in range(NFF):
                        pg, bpg = PS[j % 2][:, 0:T], PSB[j % 2][0]
                        pu, bpu = PS[2 + j % 2][:, 0:T], PSB[2 + j % 2][0]
                        for kc in range(8):
                            K.mm(pg, WG[:, kc, j * 128:(j + 1) * 128], XM[:, kc, :], kc == 0, kc == 7, [bWG, bXM], [bpg])
                        for kc in range(8):
                            K.mm(pu, WU[:, kc, j * 128:(j + 1) * 128], XM[:, kc, :], kc == 0, kc == 7, [bWU, bXM], [bpu])
                        sg, bsg = SG[j % 2], bSG[j % 2]
                        K.act(sg[:], pg, AF.Silu, [bpg], [bsg])
                        K.tt("dve", HID[:, j, :], sg[:], pu, ALU.mult, [bsg, bpu], [bHID])

                def s2(idx):
                    ti = tiles[idx]
                    H, bH, Z, bZ = Hs[idx % 2], bHs[idx % 2], Zs[idx % 2], bZs[idx % 2]
                    ZB, bZB, ZQ, bZQ = ZBs[idx % 2], bZBs[idx % 2], ZQs[idx % 2], bZQs[idx % 2]
                    col = 1 if ti == 32 else 0
                    for m in range(8):
                        pd, bpd = PS[4 + m % 2][:, 0:T], PSB[4 + m % 2][0]
                        for j in range(NFF):
                            K.mm(pd, WD[:, j, m * 128:(m + 1) * 128], HID[:, j, :], j == 0, j == NFF - 1, [bWD, bHID], [bpd])
                        K.stt("dve", Z[:, m, :], pd, modap(l, mi + 2, m, col), H[:, m, :], ALU.mult, ALU.add, [bpd, bH, bMOD], [bZ])
                        K.copy("pool", ZB[:, m, :], Z[:, m, :], [bZ], [bZB])
                        K.act(ZQ[:, m, :], Z[:, m, :], AF.Square, [bZ], [bZQ])

                def s3(idx):
                    ti = tiles[idx]
                    Z, bZ = Zs[idx % 2], bZs[idx % 2]
                    ZB, bZB, ZQ, bZQ = ZBs[idx % 2], bZBs[idx % 2], ZQs[idx % 2], bZQs[idx % 2]
                    pm, bpm = PS[6][:, 0:T], PSB[6][0]
                    pq, bpq = PS[7][:, 0:T], PSB[7][0]
                    for m in range(8):
                        K.mm(pm, ONES[:, 0, :], ZB[:, m, :], m == 0, m == 7, [bZB, bONES], [bpm])
                    for m in range(8):
                        K.mm(pq, ONES[:, 0, :], ZQ[:, m, :], m == 0, m == 7, [bZQ, bONES], [bpq])
                    mean, var, rstd = ST[:, 0, :], ST[:, 1, :], ST[:, 2, :]
                    K.copy("dve", mean, pm, [bpm], [bST])
                    K.tt("dve", rstd, mean, mean, ALU.mult, [bST], [bST])
                    K.stt("dve", var, pq, eps, rstd, ALU.add, ALU.subtract, [bpq, bST], [bST])
                    K.act(var, var, AF.Sqrt, [bST], [bST])
                    K.recip(rstd, var, [bST], [bST])
                    for m in range(8):
                        K.tt("pool", Z[:, m, :], Z[:, m, :], mean, ALU.subtract, [bZ, bST], [bZ])
                        K.tt("pool", Z[:, m, :], Z[:, m, :], rstd, ALU.mult, [bZ, bST], [bZ])
                        K.ts("pool", Z[:, m, :], Z[:, m, :], LNG[:, gcol * 8 + m:gcol * 8 + m + 1], ALU.mult, [bZ, bLN], [bZ],
                             s2=LNB[:, gcol * 8 + m:gcol * 8 + m + 1], op1=ALU.add)
                    c1 = dst_col(ti)
                    K.dma("sp", dstv[:, :, c1:c1 + T], Z[:], [bZ], [K.db(dstname)])

                load(0)
                if n > 1:
                    load(1)
                s1a(0)
                s1b(0)
                s2(0)
                if n > 1:
                    s1a(1)
                for idx in range(n):
                    if idx + 2 < n:
                        load(idx + 2)
                    if idx + 1 < n:
                        s1b(idx + 1)
                        s2(idx + 1)
                    if idx + 2 < n:
                        s1a(idx + 2)
                    s3(idx)
                P.flush()

        NTL = int(os.environ.get("NTILES", "33"))
        if NTL > 0:
            ffn_phase(0, 0, xt, "xt", H1, "H1", TILES_ALL[:NTL], lambda ti: ti * 256, lambda ti: ti * 256)

        class Rot:
            def __init__(self, st, name, n, shape, dt):
                self.t = [sbuf(st, "%s%d" % (name, i), shape, dt) for i in range(n)]
                self.b = [P.buf("%s%d" % (name, i)) for i in range(n)]
                self.i = 0

            def get(self):
                k = self.i % len(self.t)
                self.i += 1
                return self.t[k], self.b[k]

        bank_ctr = [0]

        def bank(lo=0, hi=8):
            k = lo + bank_ctr[0] % (hi - lo)
            bank_ctr[0] += 1
            return PS[k], PSB[k][0]

        PHASES = os.environ.get("PHASES", "ABCDEFGHIJK")

        def proj0_phase():
            with contextlib.ExitStack() as st:
                WIN, bWIN = load_w(st, "WIN", ev_w_in, 8, 1952)
                WPM, bWPM = load_w(st, "WPM", ev_w_in_perm, 8, 1024)
                WKPE, bWKPE = load_w(st, "WKPE", ev_w_kpe, 8, 192)
                WUQ, bWUQ = load_w(st, "WUQ", ev_w_uq, 2, 768)
                WUQP, bWUQP = load_w(st, "WUQP", ev_w_uq_perm, 2, 768)
                WUK, bWUK = load_w(st, "WUK", ev_w_ukv_k, 1, 768)
                WUV, bWUV = load_w(st, "WUV", ev_w_ukv_v, 1, 512)
                SM = sbuf(st, "SM", [128, 4], F32)
                bSM = P.buf("SM")
                K.dma("sp", SM[:], ev_small[:, :], [], [bSM])
                Hs = [sbuf(st, "Hp%d" % i, [128, 8, T], F32) for i in range(2)]
                bHs = [P.buf("Hp0"), P.buf("Hp1")]
                TAB = [sbuf(st, "TAB%d" % i, [128, 4, T], F32) for i in range(2)]
                bTAB = [P.buf("TAB0"), P.buf("TAB1")]
                XM = sbuf(st, "XMp", [128, 8, T], BF16)
                bXM = P.buf("XMp")
                TMP = Rot(st, "TMP", 4, [128, T], F32)
                OB = Rot(st, "OB", 6, [128, T], BF16)
                OV = Rot(st, "OV", 2, [128, 512], BF16)
                VAt = sbuf(st, "VAt", [128, 2, 8, 128], BF16)
                bVAt = [P.buf("VAt0"), P.buf("VAt1")]
                K.memset("pool", VAt[:, :, :, 64:128], 1.0, bVAt)
                KVL = sbuf(st, "KVL", [128, T], F32)
                SQ = sbuf(st, "SQ", [128, 2, T], BF16)
                RS = sbuf(st, "RS", [128, T], F32)
                KVN = sbuf(st, "KVN", [128, T], BF16)
                KPE = sbuf(st, "KPE", [96, T], F32)
                QL = sbuf(st, "QL", [128, 2, T], F32)
                QN = sbuf(st, "QN", [128, 2, T], BF16)
                bKVL, bSQ, bRS, bKVN, bKPE, bQL, bQN = [P.buf(n) for n in "KVL SQ RS KVN KPE QL QN".split()]
                srcv = H1.rearrange("(c p) t -> p c t", p=128)

                def load(idx):
                    ti = TILES_ALL[idx]
                    c0 = ti * 256
                    K.dma("sp", Hs[idx % 2][:], srcv[:, :, c0:c0 + T], [K.db("H1")], [bHs[idx % 2]])
                    tb_, btb = TAB[idx % 2], bTAB[idx % 2]
                    K.dma("sp", tb_[0:96, 0, :], ca[:, c0:c0 + T], [], [btb])
                    K.dma("sp", tb_[0:96, 1, :], sa[:, c0:c0 + T], [], [btb])
                    K.dma("sp", tb_[:, 2, :], cb[:, c0:c0 + T], [], [btb])
                    K.dma("sp", tb_[:, 3, :], sb_[:, c0:c0 + T], [], [btb])

                def rope_out(psa, bpa, psb, bpb, c_ap, s_ap, btb, rows, dst, dstname):
                    t1, b1 = TMP.get()
                    t2, b2 = TMP.get()
                    K.tt("dve", t1[0:rows, :], psa, c_ap, ALU.mult, [bpa, btb], [b1])
                    K.tt("dve", t2[0:rows, :], psb, s_ap, ALU.mult, [bpb, btb], [b2])
                    ob, bob = OB.get()
                    K.tt("pool", ob[0:rows, :], t1[0:rows, :], t2[0:rows, :], ALU.add, [b1, b2], [bob])
                    K.dma("sp", dst, ob[0:rows, :], [bob], [K.db(dstname)])

                load(0)
                for idx, ti in enumerate(TILES_ALL):
                    if idx + 1 < len(TILES_ALL):
                        load(idx + 1)
                    H, bH = Hs[idx % 2], bHs[idx % 2]
                    tb_, btb = TAB[idx % 2], bTAB[idx % 2]
                    col = 1 if ti == 32 else 0
                    isq = ti in TILES_Q
                    t0 = ti * 256
                    q0 = qcol(ti)
                    for kc in range(8):
                        K.act(XM[:, kc, :], H[:, kc, :], AF.Identity, [bH, bMOD], [bXM],
                              bias=modap(0, 3, kc, col), scale=modap(0, 4, kc, col))
                    for (isneeded, wc0, pc0, dst3, dname, dcol) in ((True, 928, 512, KBs, "KBs", t0), (isq, 256, 0, QB, "QB", q0)):
                        if not isneeded:
                            continue
                        for h in range(4):
                            p1, bp1 = bank()
                            p2, bp2 = bank()
                            for kc in range(8):
                                K.mm(p1[:, 0:T], WIN[:, kc, wc0 + h * 128:wc0 + (h + 1) * 128], XM[:, kc, :], kc == 0, kc == 7, [bWIN, bXM], [bp1])
                            for kc in range(8):
                                K.mm(p2[:, 0:T], WPM[:, kc, pc0 + h * 128:pc0 + (h + 1) * 128], XM[:, kc, :], kc == 0, kc == 7, [bWPM, bXM], [bp2])
                            rope_out(p1[:, 0:T], bp1, p2[:, 0:T], bp2, tb_[:, 2, :], tb_[:, 3, :], btb, 128,
                                     dst3[h, :, dcol:dcol + T], dname)
                    for tb in range(2):
                        pv, bpv = bank()
                        for kc in range(8):
                            K.mm(pv[:, :], XM[:, kc, tb * 128:(tb + 1) * 128], WIN[:, kc, 1440:1952], kc == 0, kc == 7, [bWIN, bXM], [bpv])
                        ov, bov = OV.get()
                        K.copy("act", ov[:], pv[:, :], [bpv], [bov])
                        K.dma("sp", VB[t0 + tb * 128:t0 + (tb + 1) * 128, :], ov[:], [bov], [K.db("VB")])
                    pk, bpk = bank()
                    for kc in range(8):
                        K.mm(pk[:, 0:T], WIN[:, kc, 768:896], XM[:, kc, :], kc == 0, kc == 7, [bWIN, bXM], [bpk])
                    K.copy("dve", KVL[:], pk[:, 0:T], [bpk], [bKVL])
                    K.act(SQ[:, 0, :], pk[:, 0:T], AF.Square, [bpk], [bSQ])
                    pss, bpss = bank()
                    K.mm(pss[:, 0:T], ONES[:, 1, :], SQ[:, 0, :], True, True, [bSQ, bONES], [bpss])
                    K.ts("dve", RS[:], pss[:, 0:T], 1e-6, ALU.add, [bpss], [bRS])
                    K.act(RS[:], RS[:], AF.Sqrt, [bRS], [bRS])
                    K.recip(RS[:], RS[:], [bRS], [bRS])
                    K.tt("dve", KVL[:], KVL[:], RS[:], ALU.mult, [bKVL, bRS], [bKVL])
                    K.act(KVN[:], KVL[:], AF.Identity, [bKVL, bSM], [bKVN], scale=SM[:, 2:3])
                    pp1, bpp1 = bank()
                    pp2, bpp2 = bank()
                    for kc in range(8):
                        K.mm(pp1[0:96, 0:T], WKPE[:, kc, 0:96], XM[:, kc, :], kc == 0, kc == 7, [bWKPE, bXM], [bpp1])
                    for kc in range(8):
                        K.mm(pp2[0:96, 0:T], WKPE[:, kc, 96:192], XM[:, kc, :], kc == 0, kc == 7, [bWKPE, bXM], [bpp2])
                    t1, b1 = TMP.get()
                    t2, b2 = TMP.get()
                    K.tt("dve", t1[0:96, :], pp1[0:96, 0:T], tb_[0:96, 0, :], ALU.mult, [bpp1, btb], [b1])
                    K.tt("dve", t2[0:96, :], pp2[0:96, 0:T], tb_[0:96, 1, :], ALU.mult, [bpp2, btb], [b2])
                    K.tt("pool", KPE[:], t1[0:96, :], t2[0:96, :], ALU.add, [b1, b2], [bKPE])
                    for h in range(8):
                        pkh, bpkh = bank()
                        K.mm(pkh[0:96, 0:T], WUK[:, 0, 96 * h:96 * h + 96], KVN[:], True, True, [bWUK, bKVN], [bpkh])
                        ob, bob = OB.get()
                        K.tt("dve", ob[0:96, :], pkh[0:96, 0:T], KPE[:], ALU.add, [bpkh, bKPE], [bob])
                        K.dma("sp", KA[h, :, t0:t0 + T], ob[0:96, :], [bob], [K.db("KA")])
                    for tb in range(2):
                        pv, bpv = bank()
                        K.mm(pv[:, :], KVN[:, tb * 128:(tb + 1) * 128], WUV[:, 0, :], True, True, [bWUV, bKVN], [bpv])
                        K.copy("act", VAt[:, tb, :, 0:64], pv[:, :].rearrange("p (h c) -> p h c", c=64), [bpv], [bVAt[tb]])
                        K.dma("sp", VA[t0 + tb * 128:t0 + (tb + 1) * 128, :].rearrange("p (h c) -> p h c", c=128),
                              VAt[:, tb, :, :], [bVAt[tb]], [K.db("VA")])
                    if isq:
                        for c in range(2):
                            pq_, bpq_ = bank()
                            for kc in range(8):
                                K.mm(pq_[:, 0:T], WIN[:, kc, c * 128:(c + 1) * 128], XM[:, kc, :], kc == 0, kc == 7, [bWIN, bXM], [bpq_])
                            K.copy("dve", QL[:, c, :], pq_[:, 0:T], [bpq_], [bQL])
                            K.act(SQ[:, c, :], pq_[:, 0:T], AF.Square, [bpq_], [bSQ])
                        pss, bpss = bank()
                        for c in range(2):
                            K.mm(pss[:, 0:T], ONES[:, 2, :], SQ[:, c, :], c == 0, c == 1, [bSQ, bONES], [bpss])
                        K.ts("dve", RS[:], pss[:, 0:T], 1e-6, ALU.add, [bpss], [bRS])
                        K.act(RS[:], RS[:], AF.Sqrt, [bRS], [bRS])
                        K.recip(RS[:], RS[:], [bRS], [bRS])
                        for c in range(2):
                            K.tt("dve", QL[:, c, :], QL[:, c, :], RS[:], ALU.mult, [bQL, bRS], [bQL])
                            K.act(QN[:, c, :], QL[:, c, :], AF.Identity, [bQL, bSM], [bQN], scale=SM[:, c:c + 1])
                        for h in range(8):
                            p1, bp1 = bank()
                            p2, bp2 = bank()
                            for c in range(2):
                                K.mm(p1[0:96, 0:T], WUQ[:, c, 96 * h:96 * h + 96], QN[:, c, :], c == 0, c == 1, [bWUQ, bQN], [bp1])
                            for c in range(2):
                                K.mm(p2[0:96, 0:T], WUQP[:, c, 96 * h:96 * h + 96], QN[:, c, :], c == 0, c == 1, [bWUQP, bQN], [bp2])
                            rope_out(p1[0:96, 0:T], bp1, p2[0:96, 0:T], bp2, tb_[0:96, 0, :], tb_[0:96, 1, :], btb, 96,
                                     QA[h, :, q0:q0+T], "QA")
                P.flush()

        if "C" in PHASES:
            proj0_phase()

        QTILES = [(i * 512, 512, 0, 66) for i in range(8)] + [(4096, 256, 0, 66), (4352, 256, 64, 66)]

        def att0_mla_phase(heads):
            with contextlib.ExitStack() as st:
                Kt = [sbuf(st, "Kt%d" % i, [96, NT], BF16) for i in range(2)]
                Vt = [sbuf(st, "Vt%d" % i, [128, 66, 128], BF16) for i in range(2)]
                Qt = [sbuf(st, "Qt%d" % i, [96, NQ], BF16) for i in range(2)]
                bKt = [P.buf("Kt%d" % i) for i in range(2)]
                bVt = [P.buf("Vt%d" % i) for i in range(2)]
                bQt = [P.buf("Qt%d" % i) for i in range(2)]
                PT = Rot(st, "PT", 6, [128, 512], BF16)
                RR = sbuf(st, "RR", [64, 512], F32)
                bRR = P.buf("RR")
                OO = Rot(st, "OO", 2, [64, 512], BF16)
                VAv = VA.rearrange("(kt p) (h c) -> p kt h c", p=128, h=8)

                def load(i):
                    h = heads[i]
                    K.dma("sp", Kt[i % 2][:], KA[h, :, :], [K.db("KA")], [bKt[i % 2]])
                    K.dma("sp", Qt[i % 2][:], QA[h, :, :], [K.db("QA")], [bQt[i % 2]])
                    for kq in range(6):
                        K.dma("act" if kq % 2 else "sp", Vt[i % 2][:, kq * 11:(kq + 1) * 11, :], VAv[:, kq * 11:(kq + 1) * 11, h, :], [K.db("VA")], [bVt[i % 2]])
                load(0)
                for i, h in enumerate(heads):
                    if i + 1 < len(heads):
                        load(i + 1)
                    kt_, vt_, qt_ = Kt[i % 2], Vt[i % 2], Qt[i % 2]
                    bk, bv, bq = bKt[i % 2], bVt[i % 2], bQt[i % 2]
                    steps = [(qi, q0, N, kt, kt == klo, kt == khi - 1) for qi, (q0, N, klo, khi) in enumerate(QTILES) for kt in range(klo, khi)]
                    Sq = {}

                    def emitS(s):
                        qi, q0, N, kt, first, last = steps[s]
                        S, bS = bank(0, 6)
                        K.mm(S[:, 0:N], kt_[:, kt * 128:(kt + 1) * 128], qt_[:, q0:q0 + N], True, True, [bk, bq], [bS])
                        Sq[s] = (S, bS)
                    DEPTH = 3
                    for s in range(min(DEPTH, len(steps))):
                        emitS(s)
                    for s in range(len(steps)):
                        if s + DEPTH < len(steps):
                            emitS(s + DEPTH)
                        qi, q0, N, kt, first, last = steps[s]
                        S, bS = Sq.pop(s)
                        O, bO = PS[6 + qi % 2], PSB[6 + qi % 2][0]
                        pt, bpt = PT.get()
                        K.act(pt[:, 0:N], S[:, 0:N], AF.Exp, [bS], [bpt], scale=A_SCALE)
                        K.mm(O[:, 0:N], vt_[:, kt, :], pt[:, 0:N], first, last, [bv, bpt], [bO])
                        if last:
                            K.recip(RR[:, 0:N], O[64:128, 0:N], [bO], [bRR])
                            oo, boo = OO.get()
                            K.tt("dve", oo[:, 0:N], O[0:64, 0:N], RR[:, 0:N], ALU.mult, [bO, bRR], [boo])
                            K.dma("sp", MIX[64 * h:64 * h + 64, q0:q0 + N], oo[:, 0:N], [boo], [K.db("MIX")])
                P.flush()

        def att0_diff_phase(heads):
            with contextlib.ExitStack() as st:
                Kt = [sbuf(st, "Kd%d" % i, [128, NT], BF16) for i in range(2)]
                Vt = [sbuf(st, "Vd%d" % i, [128, 66, 128], BF16) for i in range(2)]
                Qt = [sbuf(st, "Qd%d" % i, [128, 2, NQ], BF16) for i in range(2)]
                ACC = [sbuf(st, "ACC%d" % i, [128, 512], F32) for i in range(2)]
                bACC = [P.buf("ACC0"), P.buf("ACC1")]
                ONESF = sbuf(st, "ONESF", [128, 128], F32)
                bONESF = P.buf("ONESF")
                K.memset("pool", ONESF[:], 1.0, [bONESF])
                bKt = [P.buf("Kd%d" % i) for i in range(2)]
                bVt = [P.buf("Vd%d" % i) for i in range(2)]
                bQt = [P.buf("Qd%d" % i) for i in range(2)]
                PT = Rot(st, "PTd", 4, [128, 512], BF16)
                R1 = sbuf(st, "R1", [128, 512], F32)
                R2 = sbuf(st, "R2", [128, 512], F32)
                OA = sbuf(st, "OA", [128, 512], F32)
                OBd = sbuf(st, "OBd", [128, 512], F32)
                SQd = sbuf(st, "SQd", [128, 512], BF16)
                bR1, bR2, bOA, bOBd, bSQd = [P.buf(n) for n in "R1 R2 OA OBd SQd".split()]
                OO = Rot(st, "OOd", 2, [128, 512], BF16)
                LV = sbuf(st, "LV", [128, 256], F32)
                LP = sbuf(st, "LP", [128, 2, 64], F32)
                LS = sbuf(st, "LS", [128, 4], F32)
                GS = sbuf(st, "GS", [128, 4], F32)
                bLV, bLS, bGS = P.buf("LV"), P.buf("LS"), P.buf("GS")
                K.dma("sp", LV[:], ev_lam_bc[:, :], [], [bLV])
                K.dma("sp", GS[:], ev_small[:, :], [], [bGS])
                K.tt("dve", LP[:, 0, :], LV[:, 0:64], LV[:, 64:128], ALU.mult, [bLV], [bLV])
                K.tt("dve", LP[:, 1, :], LV[:, 128:192], LV[:, 192:256], ALU.mult, [bLV], [bLV])
                P.op("dve", lambda e: e.reduce_sum(out=LS[:, 0:2], in_=LP[:, :, :], axis=AX.X), [bLV], [bLS])
                K.act(LS[:, 0:2], LS[:, 0:2], AF.Exp, [bLS], [bLS])
                K.tt("dve", LS[:, 2:3], LS[:, 1:2], LS[:, 0:1], ALU.subtract, [bLS], [bLS])
                K.ts("dve", LS[:, 2:3], LS[:, 2:3], -LAM_INIT0, ALU.add, [bLS], [bLS])
                K.ts("dve", GS[:, 3:4], GS[:, 3:4], 1.0 - LAM_INIT0, ALU.mult, [bGS], [bGS])
                VBv = VB.rearrange("(kt p) (h c) -> p kt h c", p=128, h=4)

                def load(i):
                    h = heads[i]
                    K.dma("sp", Kt[i % 2][:], KBs[h, :, :], [K.db("KBs")], [bKt[i % 2]])
                    K.dma("sp", Qt[i % 2][0:64, 0, :], QB[h, 0:64, :], [K.db("QB")], [bQt[i % 2]])
                    K.dma("sp", Qt[i % 2][64:128, 1, :], QB[h, 64:128, :], [K.db("QB")], [bQt[i % 2]])
                    for kq in range(6):
                        K.dma("act" if kq % 2 else "sp", Vt[i % 2][:, kq * 11:(kq + 1) * 11, :], VBv[:, kq * 11:(kq + 1) * 11, h, :], [K.db("VB")], [bVt[i % 2]])
                for i_ in range(2):
                    K.memset("pool", Qt[i_][64:128, 0, :], 0.0, [bQt[i_]])
                    K.memset("pool", Qt[i_][0:64, 1, :], 0.0, [bQt[i_]])
                load(0)
                for i, h in enumerate(heads):
                    if i + 1 < len(heads):
                        load(i + 1)
                    kt_, vt_, qt_ = Kt[i % 2], Vt[i % 2], Qt[i % 2]
                    bk, bv, bq = bKt[i % 2], bVt[i % 2], bQt[i % 2]
                    O1, bO1 = PS[4], PSB[4][0]
                    L1, bL1 = PS[5], PSB[5][0]
                    O2, bO2 = PS[6], PSB[6][0]
                    L2, bL2 = PS[7], PSB[7][0]
                    steps = [(qi, q0, N, kt, mp, kt == klo, kt == khi - 1) for qi, (q0, N, klo, khi) in enumerate(QTILES)
                             for kt in range(klo, khi) for mp in range(2)]
                    Sq = {}

                    def emitS(s):
                        qi, q0, N, kt, mp, first, last = steps[s]
                        S, bS = bank(0, 4)
                        K.mm(S[:, 0:N], kt_[:, kt * 128:(kt + 1) * 128], qt_[:, mp, q0:q0 + N], True, True, [bk, bq], [bS])
                        Sq[s] = (S, bS)
                    DEPTH = 2
                    for s in range(min(DEPTH, len(steps))):
                        emitS(s)
                    for s in range(len(steps)):
                        if s + DEPTH < len(steps):
                            emitS(s + DEPTH)
                        qi, q0, N, kt, mp, first, last = steps[s]
                        S, bS = Sq.pop(s)
                        O_, bO_, L_, bL_ = ((O1, bO1, L1, bL1), (O2, bO2, L2, bL2))[mp]
                        pt, bpt = PT.get()
                        K.act(pt[:, 0:N], S[:, 0:N], AF.Exp, [bS], [bpt], scale=B_SCALE)
                        K.mm(O_[:, 0:N], vt_[:, kt, :], pt[:, 0:N], first, last, [bv, bpt], [bO_])
                        if mp == 0:
                            K.mm(L1[:, 0:N], ONES[:, 3, :], pt[:, 0:N], first, last, [bONES, bpt], [bL1])
                        else:
                            klo_ = QTILES[qi][2]
                            par = (kt - klo_) % 2
                            aeng = "dve" if par == 0 else "pool"
                            if kt - klo_ < 2:
                                K.copy(aeng, ACC[par][:, 0:N], pt[:, 0:N], [bpt], [bACC[par]])
                            else:
                                K.tt(aeng, ACC[par][:, 0:N], ACC[par][:, 0:N], pt[:, 0:N], ALU.add, [bpt, bACC[par]], [bACC[par]])
                        if not (last and mp == 1):
                            continue
                        K.mm(L2[:, 0:N], ONESF[:], ACC[0][:, 0:N], True, False, [bONESF, bACC[0]], [bL2])
                        K.mm(L2[:, 0:N], ONESF[:], ACC[1][:, 0:N], False, True, [bONESF, bACC[1]], [bL2])
                        K.recip(R1[:, 0:N], L1[:, 0:N], [bL1], [bR1])
                        K.recip(R2[:, 0:N], L2[:, 0:N], [bL2], [bR2])
                        K.tt("dve", OA[:, 0:N], O1[:, 0:N], R1[:, 0:N], ALU.mult, [bO1, bR1], [bOA])
                        K.tt("dve", OBd[:, 0:N], O2[:, 0:N], R2[:, 0:N], ALU.mult, [bO2, bR2], [bOBd])
                        K.stt("dve", OA[:, 0:N], OBd[:, 0:N], LS[:, 2:3], OA[:, 0:N], ALU.mult, ALU.add, [bOBd, bOA, bLS], [bOA])
                        K.act(SQd[:, 0:N], OA[:, 0:N], AF.Square, [bOA], [bSQd])
                        K.mm(L1[:, 0:N], ONES[:, 1, :], SQd[:, 0:N], True, True, [bONES, bSQd], [bL1])
                        K.ts("dve", R1[:, 0:N], L1[:, 0:N], 1e-5, ALU.add, [bL1], [bR1])
                        K.act(R1[:, 0:N], R1[:, 0:N], AF.Sqrt, [bR1], [bR1])
                        K.recip(R1[:, 0:N], R1[:, 0:N], [bR1], [bR1])
                        K.tt("pool", OA[:, 0:N], OA[:, 0:N], R1[:, 0:N], ALU.mult, [bOA, bR1], [bOA])
                        oo, boo = OO.get()
                        K.act(oo[:, 0:N], OA[:, 0:N], AF.Identity, [bOA, bGS], [boo], scale=GS[:, 3:4])
                        K.dma("sp", MIX[512 + 128 * h:512 + 128 * (h + 1), q0:q0 + N], oo[:, 0:N], [boo], [K.db("MIX")])
                P.flush()

        if "D" in PHASES:
            att0_mla_phase([0, 1, 2, 3])
            att0_mla_phase([4, 5, 6, 7])
            att0_diff_phase([0, 1])
            att0_diff_phase([2, 3])

        def out_phase(l, wout, mix, mixname, mixcol, hin, hinname, hincol, hout, houtname, houtcol, tiles):
            with contextlib.ExitStack() as st:
                WO, bWO = load_w(st, "WO", wout, 8, D)
                Hs = [sbuf(st, "Ho%d" % i, [128, 8, T], F32) for i in range(3)]
                MX = [sbuf(st, "MX%d" % i, [128, 8, T], BF16) for i in range(3)]
                bHs = [P.buf("Ho%d" % i) for i in range(3)]
                bMX = [P.buf("MX%d" % i) for i in range(3)]
                Zs = [sbuf(st, "Zo%d" % i, [128, 8, T], F32) for i in range(2)]
                bZs = [P.buf("Zo0"), P.buf("Zo1")]
                ZB = sbuf(st, "ZBo", [128, 8, T], BF16)
                ZQ = sbuf(st, "ZQo", [128, 8, T], BF16)
                ST = sbuf(st, "STo", [128, 3, T], F32)
                HO = sbuf(st, "HOo", [128, 8, T], F32)
                bZB, bZQ, bST, bHO = [P.buf(n) for n in "ZBo ZQo STo HOo".split()]
                hv = hin.rearrange("(c p) t -> p c t", p=128)
                mv = mix.rearrange("(c p) t -> p c t", p=128)
                ov = hout.rearrange("(c p) t -> p c t", p=128)
                n = len(tiles)

                def load(idx):
                    c0 = hincol(tiles[idx])
                    K.dma("sp", Hs[idx % 3][:], hv[:, :, c0:c0 + T], [K.db(hinname)], [bHs[idx % 3]])
                    c0 = mixcol(tiles[idx])
                    K.dma("sp", MX[idx % 3][:], mv[:, :, c0:c0 + T], [K.db(mixname)], [bMX[idx % 3]])

                def sa(idx):
                    ti = tiles[idx]
                    H, bH, M_, bM = Hs[idx % 3], bHs[idx % 3], MX[idx % 3], bMX[idx % 3]
                    Z, bZ = Zs[idx % 2], bZs[idx % 2]
                    col = 1 if ti == 32 else 0
                    for m in range(8):
                        py, bpy = bank(0, 6)
                        for kc in range(8):
                            K.mm(py[:, 0:T], WO[:, kc, m * 128:(m + 1) * 128], M_[:, kc, :], kc == 0, kc == 7, [bWO, bM], [bpy])
                        K.stt("dve", Z[:, m, :], py[:, 0:T], modap(l, 5, m, col), H[:, m, :], ALU.mult, ALU.add, [bpy, bH, bMOD], [bZ])

                def sb(idx):
                    ti = tiles[idx]
                    Z, bZ = Zs[idx % 2], bZs[idx % 2]
                    layer_norm(Z, bZ, ZB, bZB, ZQ, bZQ, ST, bST, HO, bHO, l * 3 + 1, T, 6, LN_EPS / (DN_ALPHA ** 2))
                    c1 = houtcol(ti)
                    K.dma("sp", ov[:, :, c1:c1 + T], HO[:], [bHO], [K.db(houtname)])

                load(0)
                if n > 1:
                    load(1)
                sa(0)
                for idx in range(n):
                    if idx + 2 < n:
                        load(idx + 2)
                    if idx + 1 < n:
                        sa(idx + 1)
                    sb(idx)
                P.flush()

        natcol = lambda ti: ti * 256
        if "E" in PHASES:
            out_phase(0, ev_w_out, MIX, "MIX", qcol, H1, "H1", natcol, H2, "H2", qcol, TILES_Q)
        if "F" in PHASES:
            ffn_phase(0, 1, H2, "H2", H3, "H3", TILES_Q, qcol, qcol)
        if "G" in PHASES:
            ffn_phase(1, 0, H3, "H3", H4, "H4", TILES_Q, qcol, qcol)

        def proj1_phase():
            with contextlib.ExitStack() as st:
                WIN, bWIN = load_w(st, "WIN1", od_w_in, 8, 2048)
                Hs = [sbuf(st, "Hq%d" % i, [128, 8, T], F32) for i in range(2)]
                bHs = [P.buf("Hq0"), P.buf("Hq1")]
                XM = sbuf(st, "XMq", [128, 8, T], BF16)
                bXM = P.buf("XMq")
                ZR = sbuf(st, "ZR", [128, 1024], BF16)
                bZR = P.buf("ZR")
                OU = Rot(st, "OU", 3, [128, T], F32)
                OB = Rot(st, "OB1", 4, [128, T], BF16)
                VDt = sbuf(st, "VDt", [128, 2, 8, 128], BF16)
                bVDt = [P.buf("VDt0"), P.buf("VDt1")]
                K.memset("pool", VDt[:, :, :, 64:128], 1.0, bVDt)
                K.memset("pool", ZR[:], 0.0, [bZR])
                for c in range(4):
                    K.dma("sp", KD[c, :, 0:256], ZR[:, 0:256], [bZR], [K.db("KD")])
                    K.dma("sp", KD[c, :, NKP - 256:NKP], ZR[:, 0:256], [bZR], [K.db("KD")])
                for r in range(2):
                    K.dma("sp", VD[r * 128:(r + 1) * 128, :], ZR[:], [bZR], [K.db("VD")])
                    K.dma("sp", VD[NKP - 256 + r * 128:NKP - 256 + (r + 1) * 128, :], ZR[:], [bZR], [K.db("VD")])
                srcv = H4.rearrange("(c p) t -> p c t", p=128)

                def load(idx):
                    c0 = qcol(TILES_Q[idx])
                    K.dma("sp", Hs[idx % 2][:], srcv[:, :, c0:c0 + T], [K.db("H4")], [bHs[idx % 2]])
                load(0)
                for idx, ti in enumerate(TILES_Q):
                    if idx + 1 < len(TILES_Q):
                        load(idx + 1)
                    H, bH = Hs[idx % 2], bHs[idx % 2]
                    col = 1 if ti == 32 else 0
                    lat = ti != 32
                    t0 = ti * 256
                    for kc in range(8):
                        K.act(XM[:, kc, :], H[:, kc, :], AF.Identity, [bH, bMOD], [bXM],
                              bias=modap(1, 3, kc, col), scale=modap(1, 4, kc, col))
                    if lat:
                        for c in range(4):
                            pu_, bpu_ = bank()
                            for kc in range(8):
                                K.mm(pu_[:, 0:T], WIN[:, kc, c * 128:(c + 1) * 128], XM[:, kc, :], kc == 0, kc == 7, [bWIN, bXM], [bpu_])
                            ou, bou = OU.get()
                            K.copy("act", ou[:], pu_[:, 0:T], [bpu_], [bou])
                            K.dma("sp", U[c * 128:(c + 1) * 128, t0:t0 + T], ou[:], [bou], [K.db("U")])
                        for c in range(4):
                            pq_, bpq_ = bank()
                            for kc in range(8):
                                K.mm(pq_[:, 0:T], WIN[:, kc, 512 + c * 128:512 + (c + 1) * 128], XM[:, kc, :], kc == 0, kc == 7, [bWIN, bXM], [bpq_])
                            ob, bob = OB.get()
                            K.copy("dve", ob[:], pq_[:, 0:T], [bpq_], [bob])
                            K.dma("sp", QD[c, :, t0:t0 + T], ob[:], [bob], [K.db("QD")])
                    for c in range(4):
                        pk_, bpk_ = bank()
                        for kc in range(8):
                            K.mm(pk_[:, 0:T], WIN[:, kc, 1024 + c * 128:1024 + (c + 1) * 128], XM[:, kc, :], kc == 0, kc == 7, [bWIN, bXM], [bpk_])
                        ob, bob = OB.get()
                        K.copy("dve", ob[:], pk_[:, 0:T], [bpk_], [bob])
                        if lat:
                            K.dma("sp", KD[c, :, 256 + t0:256 + t0 + T], ob[:], [bob], [K.db("KD")])
                        else:
                            K.dma("sp", KDC[c, :, :], ob[:], [bob], [K.db("KDC")])
                    for tb in range(2):
                        pv, bpv = bank()
                        for kc in range(8):
                            K.mm(pv[:, :], XM[:, kc, tb * 128:(tb + 1) * 128], WIN[:, kc, 1536:2048], kc == 0, kc == 7, [bWIN, bXM], [bpv])
                        K.copy("act", VDt[:, tb, :, 0:64], pv[:, :].rearrange("p (h c) -> p h c", c=64), [bpv], [bVDt[tb]])
                        if lat:
                            dst = VD[256 + t0 + tb * 128:256 + t0 + (tb + 1) * 128, :]
                            dn = "VD"
                        else:
                            dst = VDC[tb * 128:(tb + 1) * 128, :]
                            dn = "VDC"
                        K.dma("sp", dst.rearrange("p (h c) -> p h c", c=128), VDt[:, tb, :, :], [bVDt[tb]], [K.db(dn)])
                P.flush()

        if "H" in PHASES:
            proj1_phase()

        def pool_phase():
            with contextlib.ExitStack() as st:
                NP = NQL + 16
                UT = sbuf(st, "UT", [128, NP], F32)
                X1 = sbuf(st, "X1", [128, NP], F32)
                X2 = sbuf(st, "X2", [128, NP], F32)
                IC = sbuf(st, "IC", [128, NQL], F32)
                PMf = sbuf(st, "PMf", [128, NQL], F32)
                PMb = sbuf(st, "PMb", [128, NQL], BF16)
                WP = sbuf(st, "WP", [128, 128], BF16)
                PSC = sbuf(st, "PSC", [128, 4], F32)
                bUT, bX1, bX2, bIC, bPMf, bPMb, bWP, bPSC = [P.buf(n) for n in "UT X1 X2 IC PMf PMb WP PSC".split()]
                OO = Rot(st, "OOp", 3, [128, 512], BF16)
                K.dma("sp", PSC[:], od_ps_t[:, :], [], [bPSC])
                for g in range(4):
                    K.memset("pool", UT[:, 0:8], 0.0, [bUT])
                    K.memset("pool", UT[:, NP - 8:NP], 0.0, [bUT])
                    K.dma("sp", UT[:, 8:8 + NQL], U[g * 128:(g + 1) * 128, :], [K.db("U")], [bUT])
                    K.dma("sp", IC[:], icnt[:, g, :], [], [bIC])
                    K.dma("pool", WP[:], od_w_pool[g, :, :], [], [bWP])
                    A, bA = UT, bUT
                    outs = [(X1, bX1), (X2, bX2)]
                    for s_ in range(g + 1):
                        sh = 1 << s_
                        B, bB = outs[s_ % 2]
                        K.copy("pool", B[:, 0:sh], A[:, 0:sh], [bA], [bB])
                        K.tt("dve", B[:, sh:NP], A[:, sh:NP], A[:, 0:NP - sh], ALU.add, [bA], [bB])
                        A, bA = B, bB
                    right = (0, 1, 3, 7)[g]
                    K.tt("dve", PMf[:], A[:, 8 + right:8 + right + NQL], IC[:], ALU.mult, [bA, bIC], [bPMf])
                    K.tt("pool", PMb[:], PMf[:], UT[:, 8:8 + NQL], ALU.subtract, [bPMf, bUT], [bPMb])
                    for ch in range(9):
                        c0 = ch * 512
                        N = 512 if ch < 8 else 256
                        pp, bpp = bank()
                        K.mm(pp[:, 0:N], WP[:], PMb[:, c0:c0 + N], True, True, [bWP, bPMb], [bpp])
                        oo, boo = OO.get()
                        K.act(oo[:, 0:N], pp[:, 0:N], AF.Identity, [bpp, bPSC], [boo], scale=PSC[:, g:g + 1])
                        K.dma("sp", MIXD[g * 128:(g + 1) * 128, c0:c0 + N], oo[:, 0:N], [boo], [K.db("MIXD")])
                P.flush()

        def na_phase():
            with contextlib.ExitStack() as st:
                BIAS = sbuf(st, "BIAS", [128, 3, 6 * 8 * 256], BF16)
                bBIAS = P.buf("BIAS")
                IDN = sbuf(st, "IDN", [128, 128], BF16)
                bIDN = P.buf("IDN")
                K.dma("pool", IDN[:], ident[:, :], [], [bIDN])
                for t_ in range(3):
                    for hh in range(2):
                        K.dma("pool", BIAS[:, t_, hh * 6144:(hh + 1) * 6144], na_b[t_, :, hh * 6144:(hh + 1) * 6144], [], [bBIAS])
                for t_ in range(3):
                    K.ts("dve" if t_ % 2 == 0 else "pool", BIAS[:, t_, :], BIAS[:, t_, :], 1.0 / D_SCALE, ALU.mult, [bBIAS], [bBIAS])
                KDb = [sbuf(st, "KDb%d" % i, [128, 4, 768], BF16) for i in range(2)]
                VDb = [sbuf(st, "VDb%d" % i, [128, 6, 1024], BF16) for i in range(2)]
                QDb = [sbuf(st, "QDb%d" % i, [128, 4, 256], BF16) for i in range(2)]
                bKDb = [P.buf("KDb%d" % i) for i in range(2)]
                bVDb = [P.buf("VDb%d" % i) for i in range(2)]
                bQDb = [P.buf("QDb%d" % i) for i in range(2)]
                KC = sbuf(st, "KC", [128, 4, 256], BF16)
                VC = sbuf(st, "VC", [128, 2, 1024], BF16)
                bKC, bVC = P.buf("KC"), P.buf("VC")
                K.dma("sp", KC[:], KDC.rearrange("c p n -> p c n"), [K.db("KDC")], [bKC])
                K.dma("sp", VC[:], VDC.rearrange("(kt p) n -> p kt n", p=128), [K.db("VDC")], [bVC])
                SBt = Rot(st, "SBt", 4, [128, 256], F32)
                PT = Rot(st, "PTn", 6, [128, 256], BF16)
                RR = sbuf(st, "RRn", [64, 256], F32)
                bRR = P.buf("RRn")
                OO = Rot(st, "OOn", 3, [64, 256], BF16)
                KDv = KD.rearrange("c p n -> p c n")
                QDv = QD.rearrange("c p n -> p c n")
                VDv = VD.rearrange("(kt p) n -> p kt n", p=128)

                def load(bi):
                    i0 = 4 * bi
                    K.dma("sp", KDb[bi % 2][:], KDv[:, :, i0 * 64:i0 * 64 + 768], [K.db("KD")], [bKDb[bi % 2]])
                    K.dma("sp", VDb[bi % 2][:], VDv[:, i0 // 2:i0 // 2 + 6, :], [K.db("VD")], [bVDb[bi % 2]])
                    K.dma("sp", QDb[bi % 2][:], QDv[:, :, bi * 256:(bi + 1) * 256], [K.db("QD")], [bQDb[bi % 2]])
                load(0)
                for bi in range(17):
                    if bi + 1 < 17:
                        load(bi + 1)
                    kd, vd, qd = KDb[bi % 2], VDb[bi % 2], QDb[bi % 2]
                    bk, bv, bq = bKDb[bi % 2], bVDb[bi % 2], bQDb[bi % 2]
                    tab = 0 if bi == 0 else (2 if bi == 16 else 1)
                    steps = [(h, kt) for h in range(8) for kt in range(8)]
                    Sq = {}

                    def emitS(s):
                        h, kt = steps[s]
                        c, po = h // 2, (h % 2) * 64
                        S, bS = bank(0, 6)
                        if kt < 6:
                            boff = (kt * 8 + h) * 256
                            K.mm(S[:, 0:256], kd[po:po + 64, c, kt * 128:(kt + 1) * 128], qd[po:po + 64, c, :], True, False, [bk, bq], [bS])
                            K.mm(S[:, 0:256], IDN[:], BIAS[:, tab, boff:boff + 256], False, True, [bIDN, bBIAS], [bS])
                        else:
                            kk = kt - 6
                            K.mm(S[:, 0:256], KC[po:po + 64, c, kk * 128:(kk + 1) * 128], qd[po:po + 64, c, :], True, True, [bKC, bq], [bS])
                        Sq[s] = (S, bS)
                    DEPTH = 3
                    for s in range(DEPTH):
                        emitS(s)
                    for s in range(len(steps)):
                        if s + DEPTH < len(steps):
                            emitS(s + DEPTH)
                        h, kt = steps[s]
                        S, bS = Sq.pop(s)
                        O, bO = PS[6 + h % 2], PSB[6 + h % 2][0]
                        pt, bpt = PT.get()
                        if kt < 6:
                            K.act(pt[:], S[:, 0:256], AF.Exp, [bS], [bpt], scale=D_SCALE)
                            K.mm(O[:, 0:256], vd[:, kt, h * 128:(h + 1) * 128], pt[:], kt == 0, False, [bv, bpt], [bO])
                        else:
                            kk = kt - 6
                            K.act(pt[:], S[:, 0:256], AF.Exp, [bS], [bpt], scale=D_SCALE)
                            K.mm(O[:, 0:256], VC[:, kk, h * 128:(h + 1) * 128], pt[:], False, kt == 7, [bVC, bpt], [bO])
                        if kt == 7:
                            K.recip(RR[:], O[64:128, 0:256], [bO], [bRR])
                            oo, boo = OO.get()
                            K.tt("dve", oo[:], O[0:64, 0:256], RR[:], ALU.mult, [bO, bRR], [boo])
                            K.dma("sp", MIXD[512 + 64 * h:512 + 64 * (h + 1), bi * 256:(bi + 1) * 256], oo[:], [boo], [K.db("MIXD")])
                P.flush()

        if "I" in PHASES:
            pool_phase()
            na_phase()
        TILES_L = list(range(17))
        if "J" in PHASES:
            out_phase(1, od_w_out, MIXD, "MIXD", natcol, H4, "H4", qcol, H5, "H5", natcol, TILES_L)
        if "K" in PHASES:
            ffn_phase(1, 1, H5, "H5", outT, "outT", TILES_L, natcol, natcol)
    return nc


def _perm_sign(rot_dim):
    q = rot_dim // 4
    perm = np.zeros(rot_dim, np.int64)
    sign = np.zeros(rot_dim, np.float32)
    for i in range(rot_dim):
        qq = i // q
        if qq % 2 == 0:
            perm[i] = i + q
            sign[i] = -1.0
        else:
            perm[i] = i - q
            sign[i] = 1.0
    return perm, sign


def _rope_tables(pos, rot_dim):
    q = rot_dim // 4
    row = (pos // 64).astype(np.float32)
    col = (pos % 64).astype(np.float32)
    inv = (np.float32(10000.0) ** (-np.arange(q, dtype=np.float32) / np.float32(q))).astype(np.float32)
    ang_r = row[:, None] * inv[None, :]
    ang_c = col[:, None] * inv[None, :]
    ang = np.concatenate([ang_r, ang_r, ang_c, ang_c], -1).astype(np.float32)
    return np.cos(ang).astype(np.float32), np.sin(ang).astype(np.float32)


def core_positions(half):
    s0 = 0 if half == 0 else 3840
    own = np.arange(s0, s0 + NQL)
    rest = np.concatenate([np.arange(0, s0), np.arange(s0 + NQL, NL)])
    return np.concatenate([own, rest]), s0


def prepare_inputs(inp):
    f = lambda a: np.ascontiguousarray(np.asarray(a, dtype=np.float32))
    x, c, ctx, c_ctx = f(inp["x"]), f(inp["c"]), f(inp["ctx"]), f(inp["c_ctx"])
    shared = {}
    shared["ada_w"] = f(inp["ada_w"])
    ada_b = f(inp["ada_b"])
    abt = ada_b.reshape(2, 72, 128).transpose(2, 0, 1)
    shared["ada_b_t"] = np.ascontiguousarray(np.stack([abt, abt], -1))
    shared["ln_g_t"] = np.ascontiguousarray(f(inp["ln_g"]).reshape(2, 3, 8, 128).transpose(3, 0, 1, 2).reshape(128, 48))
    shared["ln_b_t"] = np.ascontiguousarray(f(inp["ln_b"]).reshape(2, 3, 8, 128).transpose(3, 0, 1, 2).reshape(128, 48))
    shared["wg"] = f(inp["ffn_w_gate"])
    shared["wu"] = f(inp["ffn_w_up"])
    shared["wd"] = f(inp["ffn_w_down"])
    w_in = f(inp["ev_w_in"])[0]
    shared["ev_w_in"] = w_in
    pa, sga = _perm_sign(32)
    pb, sgb = _perm_sign(64)
    wperm = np.zeros((D, 1024), np.float32)
    for blk in range(8):
        wperm[:, blk * 64:(blk + 1) * 64] = w_in[:, 256 + blk * 64 + pb]
        wperm[:, 512 + blk * 64:512 + (blk + 1) * 64] = w_in[:, 928 + blk * 64 + pb]
    shared["ev_w_in_perm"] = wperm
    wkpe = np.zeros((D, 2, 96), np.float32)
    wkpe[:, 0, 64:] = w_in[:, 896:928]
    wkpe[:, 1, 64:] = w_in[:, 896 + pa]
    shared["ev_w_kpe"] = wkpe.reshape(D, 192)
    w_uq = f(inp["ev_w_uq"])[0]
    shared["ev_w_uq"] = w_uq
    wuqp = w_uq.copy()
    for h in range(8):
        wuqp[:, 96 * h + 64:96 * h + 96] = w_uq[:, 96 * h + 64 + pa]
    shared["ev_w_uq_perm"] = wuqp
    w_ukv = f(inp["ev_w_ukv"])[0].reshape(128, 8, 128)
    wk = np.zeros((128, 8, 96), np.float32)
    wk[:, :, :64] = w_ukv[:, :, :64]
    shared["ev_w_ukv_k"] = wk.reshape(128, 768)
    shared["ev_w_ukv_v"] = np.ascontiguousarray(w_ukv[:, :, 64:].reshape(128, 512))
    small = np.zeros((128, 4), np.float32)
    gq = f(inp["ev_g_qlat"])[0]
    small[:, 0] = gq[:128]
    small[:, 1] = gq[128:]
    small[:, 2] = f(inp["ev_g_kvlat"])[0]
    small[:, 3] = f(inp["ev_g_sub"])[0]
    shared["ev_small"] = small
    shared["ev_lam_bc"] = np.ascontiguousarray(np.broadcast_to(f(inp["ev_lam"])[0].reshape(1, 256), (128, 256)))
    shared["ev_w_out"] = f(inp["ev_w_out"])[0]
    shared["od_w_in"] = f(inp["od_w_in"])[0]
    shared["od_w_out"] = f(inp["od_w_out"])[0]
    shared["od_w_pool"] = f(inp["od_w_pool"])[0]
    shared["od_ps_t"] = np.ascontiguousarray(f(inp["od_pool_scale"])[0].reshape(4, 128).T)
    rpb = f(inp["od_rpb"])[0]

    per_half = []
    for half in range(2):
        d = {}
        pos, s0 = core_positions(half)
        cA, sA = _rope_tables(pos, 32)
        cB, sB = _rope_tables(pos, 64)
        ca = np.ones((96, NT), np.float32)
        sa = np.zeros((96, NT), np.float32)
        ca[64:, :NL] = cA.T
        sa[64:, :NL] = (sA * sga[None, :]).T
        cbt = np.ones((128, NT), np.float32)
        sbt = np.zeros((128, NT), np.float32)
        cbt[:64, :NL] = cB.T
        cbt[64:, :NL] = cB.T
        sbt[:64, :NL] = (sB * sgb[None, :]).T
        sbt[64:, :NL] = (sB * sgb[None, :]).T
        d["ca"], d["sa"], d["cb"], d["sb"] = ca, sa, cbt, sbt
        ic = np.ones((4, NQL), np.float32)
        tg = s0 + np.arange(NQL)
        for g, w in enumerate((2, 4, 8, 16)):
            left = w // 2
            right = w - 1 - left
            lo = np.clip(tg - left, 0, NL)
            hi = np.clip(tg + right + 1, 0, NL)
            ic[g] = 1.0 / (hi - lo).astype(np.float32)
        d["icnt"] = np.ascontiguousarray(np.broadcast_to(ic[None], (128, 4, NQL)))
        roff = 0 if half == 0 else 60
        gtab = np.zeros((3, 128, 6, 8, 256), np.float32)
        mtab = np.zeros((3, 128, 6, 8, 256), np.float32)
        qc = np.arange(64)
        kc = np.arange(64)
        cs = np.clip(qc - 8, 0, 48)
        colv = (kc[None, :] >= cs[:, None]) & (kc[None, :] < cs[:, None] + 16)
        cidx = np.clip(kc[None, :] - qc[:, None] + 15, 0, 30)
        for tix, bi in enumerate((0, 8, 16)):
            i0 = 4 * bi
            for a in range(4):
                R = i0 + a + roff
                rs = int(np.clip(R - 4, 0, 120))
                for cc in range(12):
                    kr = i0 - 4 + cc
                    KR = kr + roff
                    rowv = (rs <= KR < rs + 8) and (0 <= kr < 68)
                    kt, ph = cc // 2, (cc % 2) * 64
                    if rowv:
                        vals = rpb[:, KR - R + 7, :][:, cidx]
                        gtab[tix, ph:ph + 64, kt, :, a * 64:(a + 1) * 64] = vals.transpose(2, 0, 1)
                        mtab[tix, ph:ph + 64, kt, :, a * 64:(a + 1) * 64] = np.where(colv.T[:, None, :], 0.0, -1e30)
                    else:
                        mtab[tix, ph:ph + 64, kt, :, a * 64:(a + 1) * 64] = -1e30
        d["na_b"] = np.where(mtab < -1.0, np.float32(-1e30), gtab).reshape(3, 128, 6 * 8 * 256)
        d["pos"] = pos
        per_half.append(d)

    in_maps = []
    for cid in range(8):
        b, half = cid // 2, cid % 2
        d = per_half[half]
        m = dict(shared)
        xtc = np.empty((D, NT), np.float32)
        xtc[:, :NL] = x[b][d["pos"]].T
        xtc[:, NL:] = ctx[b].T
        m["xt"] = xtc
        ctt = np.empty((128, 8, 2), np.float32)
        ctt[:, :, 0] = c[b].reshape(8, 128).T
        ctt[:, :, 1] = c_ctx.reshape(8, 128).T
        m["ct"] = ctt
        for k in ("ca", "sa", "cb", "sb", "icnt", "na_b"):
            m[k] = d[k]
        m["ident"] = np.eye(128, dtype=np.float32)
        in_maps.append(m)
    return in_maps


_NC_CACHE = {}


def kernel(**inputs):
    in_maps = prepare_inputs(inputs)
    if "nc" not in _NC_CACHE:
        _NC_CACHE["nc"] = build_program(False)
    nc = _NC_CACHE["nc"]
    res = run_bass_kernel_spmd(nc, in_maps, core_ids=list(range(8)))
    out = np.empty((4, NL, D), np.float32)
    for cid in range(8):
        b, half = cid // 2, cid % 2
        o = res.results[cid]["outT"]
        if half == 0:
            out[b, 0:4096] = o[:, 0:4096].T
        else:
            out[b, 4096:8192] = o[:, 256:4352].T
    return out
```

```python
import math
import os
import contextlib
import numpy as np
import concourse.bass as bass
import concourse.mybir as mybir
from concourse.bass_utils import run_bass_kernel_spmd

F32 = mybir.dt.float32
BF16 = mybir.dt.bfloat16
ALU = mybir.AluOpType
AF = mybir.ActivationFunctionType
AX = mybir.AxisListType

ENGS = ("pe", "act", "dve", "pool", "sp")

D = 1024
DFF = 2816
NFF = 22
T = 256
NL = 8192
NT = 8448
NQL = 4352
NQ = 4608
TILES_ALL = list(range(33))
TILES_Q = list(range(17)) + [32]
DN_ALPHA = float(4 ** 0.25)
LN_EPS = 1e-6
A_SCALE = float(96 ** -0.5)
B_SCALE = float(64 ** -0.5)
D_SCALE = float(64 ** -0.5)
LAM_INIT0 = 0.8 - 0.6 * math.exp(0.0)
NPADR = 76
NKP = NPADR * 64


def qcol(ti):
    return ti * 256 if ti < 17 else 4352


class Buf:
    __slots__ = ("name", "w", "r", "sem", "cnt", "excl")

    def __init__(self, name):
        self.name = name
        self.excl = False
        self.w = None
        self.r = []
        self.sem = None
        self.cnt = 0


class Prog:
    def __init__(self, nc):
        self.nc = nc
        self.streams = {e: [] for e in ENGS}
        self.count = {e: 0 for e in ENGS}
        self.known = {e: {} for e in ENGS}
        self.nsem_dma = 0
        self.dma_sems = []
        self.nflush = 0
        self.bufs = []

    def buf(self, name="b"):
        b = Buf(name)
        self.bufs.append(b)
        return b

    def _deps(self, eng, reads, writes):
        deps = {}

        def add(ev):
            if ev is None:
                return
            k, v = ev
            if deps.get(k, 0) < v:
                deps[k] = v
        for b in reads:
            add(b.w)
        for b in writes:
            add(b.w)
            for ev in b.r:
                add(ev)
        out = []
        kn = self.known[eng]
        for k, v in deps.items():
            if k == eng and eng == "pe":
                continue
            if kn.get(k, 0) >= v:
                continue
            kn[k] = v
            out.append((k, v))
        return out

    def _post(self, ev, reads, writes):
        for b in reads:
            if len(b.r) > 64:
                mx = {}
                for k, v in b.r:
                    if mx.get(k, 0) < v:
                        mx[k] = v
                b.r = list(mx.items())
            b.r.append(ev)
        for b in writes:
            b.w = ev
            b.r = []

    def op(self, eng, fn, reads=(), writes=()):
        if any(b.excl for b in reads):
            writes = list(writes) + [b for b in reads if b.excl]
            reads = [b for b in reads if not b.excl]
        waits = self._deps(eng, reads, writes)
        self.count[eng] += 1
        ev = (eng, self.count[eng])
        self.streams[eng].append((waits, fn, ev))
        self._post(ev, reads, writes)
        return ev

    def dma(self, q, fn, reads=(), writes=(), sembuf=None):
        sb = sembuf if sembuf is not None else writes[0]
        if sb.sem is None:
            sb.sem = "d%d" % self.nsem_dma
            self.nsem_dma += 1
            self.dma_sems.append(sb.sem)
        waits = self._deps(q, reads, writes)
        sb.cnt += 16
        ev = (sb.sem, sb.cnt)
        self.streams[q].append((waits, fn, ev))
        self._post(ev, reads, writes)
        return ev

    def flush(self):
        nc = self.nc
        with contextlib.ExitStack() as st:
            st.enter_context(nc.cleanup_on_exit())
            sems = {}
            for e in ENGS:
                sems[e] = nc.alloc_semaphore(name="s%d_%s" % (self.nflush, e))
            for k in self.dma_sems:
                sems[k] = nc.alloc_semaphore(name="s%d_%s" % (self.nflush, k))
            block = st.enter_context(nc.Block())
            final = {k: 0 for k in sems}
            for e in ENGS:
                for (_, _, ev) in self.streams[e]:
                    if ev is not None:
                        final[ev[0]] = max(final[ev[0]], ev[1])

            def run(eng_name):
                def body(eng):
                    for waits, fn, ev in self.streams[eng_name]:
                        for k, v in waits:
                            eng.wait_ge(sems[k], v)
                        ins = fn(eng)
                        if ev[0] in ENGS:
                            ins.then_inc(sems[ev[0]], 1)
                        else:
                            ins.then_inc(sems[ev[0]], 16)
                    if eng_name == "sp":
                        for k, v in final.items():
                            if v > 0:
                                eng.wait_ge(sems[k], v)
                return body
            block.tensor(run("pe"))
            block.scalar(run("act"))
            block.vector(run("dve"))
            block.gpsimd(run("pool"))
            block.sync(run("sp"))
        self.nflush += 1
        self.streams = {e: [] for e in ENGS}
        self.count = {e: 0 for e in ENGS}
        self.known = {e: {} for e in ENGS}
        self.nsem_dma = 0
        self.dma_sems = []
        for b in self.bufs:
            b.w = None
            b.r = []
            b.sem = None
            b.cnt = 0


class KB:
    def __init__(self, nc):
        self.nc = nc
        self.P = Prog(nc)
        self.dr = {}

    def din(self, name, shape, dt=F32):
        ap = self.nc.dram_tensor(name, list(shape), dt, kind="ExternalInput").ap()
        self.dr[name] = (ap, self.P.buf(name))
        return ap

    def dout(self, name, shape, dt=F32):
        ap = self.nc.dram_tensor(name, list(shape), dt, kind="ExternalOutput").ap()
        self.dr[name] = (ap, self.P.buf(name))
        return ap

    def dscr(self, name, shape, dt=F32, debug=False):
        if debug:
            return self.dout(name, shape, dt)
        ap = self.nc.dram_tensor(name, list(shape), dt).ap()
        self.dr[name] = (ap, self.P.buf(name))
        return ap

    def db(self, name):
        return self.dr[name][1]

    def mm(self, out, lhsT, rhs, start, stop, r, w):
        self.P.op("pe", lambda e: e.matmul(out, lhsT=lhsT, rhs=rhs, start=start, stop=stop), r, w)

    def act(self, out, in_, func, r, w, bias=None, scale=None):
        kw = {}
        if bias is not None:
            kw["bias"] = bias
        if scale is not None:
            kw["scale"] = scale
        self.P.op("act", lambda e: e.activation(out=out, in_=in_, func=func, **kw), r, w)

    def tt(self, eng, out, in0, in1, op, r, w):
        self.P.op(eng, lambda e: e.tensor_tensor(out=out, in0=in0, in1=in1, op=op), r, w)

    def ts(self, eng, out, in0, s1, op0, r, w, s2=None, op1=None):
        if op1 is None:
            self.P.op(eng, lambda e: e.tensor_scalar(out=out, in0=in0, scalar1=s1, scalar2=None, op0=op0), r, w)
        else:
            self.P.op(eng, lambda e: e.tensor_scalar(out=out, in0=in0, scalar1=s1, scalar2=s2, op0=op0, op1=op1), r, w)

    def stt(self, eng, out, in0, scalar, in1, op0, op1, r, w):
        self.P.op(eng, lambda e: e.scalar_tensor_tensor(out=out, in0=in0, scalar=scalar, in1=in1, op0=op0, op1=op1), r, w)

    def copy(self, eng, out, in_, r, w):
        if eng == "act":
            self.P.op("act", lambda e: e.copy(out=out, in_=in_), r, w)
        else:
            self.P.op(eng, lambda e: e.tensor_copy(out=out, in_=in_), r, w)

    def recip(self, out, in_, r, w):
        self.P.op("dve", lambda e: e.reciprocal(out=out, in_=in_), r, w)

    def memset(self, eng, ap, val, w):
        self.P.op(eng, lambda e: e.memset(ap, val), (), w)

    def dma(self, q, out, in_, r, w, sembuf=None):
        self.P.dma(q, lambda e: e.dma_start(out=out, in_=in_), r, w, sembuf)


def build_program(debug=False):
    nc = bass.Bass("TRN2", target_bir_lowering=False)
    K = KB(nc)
    P = K.P
    dbg = debug

    xt = K.din("xt", [D, NT])
    ct = K.din("ct", [128, 8, 2])
    ada_w = K.din("ada_w", [2, D, 9 * D])
    ada_b_t = K.din("ada_b_t", [128, 2, 72, 2])
    ln_g_t = K.din("ln_g_t", [128, 48])
    ln_b_t = K.din("ln_b_t", [128, 48])
    wg = K.din("wg", [2, 2, D, DFF])
    wu = K.din("wu", [2, 2, D, DFF])
    wd = K.din("wd", [2, 2, DFF, D])
    ev_w_in = K.din("ev_w_in", [D, 1952])
    ev_w_in_perm = K.din("ev_w_in_perm", [D, 1024])
    ev_w_kpe = K.din("ev_w_kpe", [D, 192])
    ev_w_uq = K.din("ev_w_uq", [256, 768])
    ev_w_uq_perm = K.din("ev_w_uq_perm", [256, 768])
    ev_w_ukv_k = K.din("ev_w_ukv_k", [128, 768])
    ev_w_ukv_v = K.din("ev_w_ukv_v", [128, 512])
    ev_small = K.din("ev_small", [128, 4])
    ev_lam_bc = K.din("ev_lam_bc", [128, 256])
    ev_w_out = K.din("ev_w_out", [D, D])
    ca = K.din("ca", [96, NT])
    sa = K.din("sa", [96, NT])
    cb = K.din("cb", [128, NT])
    sb_ = K.din("sb", [128, NT])
    od_w_in = K.din("od_w_in", [D, 2048])
    od_w_out = K.din("od_w_out", [D, D])
    od_w_pool = K.din("od_w_pool", [4, 128, 128])
    od_ps_t = K.din("od_ps_t", [128, 4])
    icnt = K.din("icnt", [128, 4, NQL])
    na_b = K.din("na_b", [3, 128, 6 * 8 * 256])
    ident = K.din("ident", [128, 128])

    outT = K.dout("outT", [D, NQL])

    H1 = K.dscr("H1", [D, NT], F32, dbg)
    QA = K.dscr("QA", [8, 96, NQ], BF16)
    KA = K.dscr("KA", [8, 96, NT], BF16)
    VA = K.dscr("VA", [NT, 8 * 128], BF16)
    QB = K.dscr("QB", [4, 128, NQ], BF16)
    KBs = K.dscr("KBs", [4, 128, NT], BF16)
    VB = K.dscr("VB", [NT, 512], BF16)
    MIX = K.dscr("MIX", [D, NQ], BF16, dbg)
    H2 = K.dscr("H2", [D, NQ], F32, dbg)
    H3 = K.dscr("H3", [D, NQ], F32, dbg)
    H4 = K.dscr("H4", [D, NQ], F32, dbg)
    U = K.dscr("U", [512, NQL], F32)
    QD = K.dscr("QD", [4, 128, NQL], BF16)
    KD = K.dscr("KD", [4, 128, NKP], BF16)
    KDC = K.dscr("KDC", [4, 128, 256], BF16)
    VD = K.dscr("VD", [NKP, 8 * 128], BF16)
    VDC = K.dscr("VDC", [256, 8 * 128], BF16)
    MIXD = K.dscr("MIXD", [D, NQL], BF16, dbg)
    H5 = K.dscr("H5", [D, NQL], F32, dbg)

    with contextlib.ExitStack() as top:
        uid = [0]

        def sbuf(st, name, shape, dt):
            uid[0] += 1
            return st.enter_context(nc.sbuf_tensor("%s_%d" % (name, uid[0]), list(shape), dt))

        PS = [top.enter_context(nc.psum_tensor("ps%d" % i, [128, 512], F32)) for i in range(8)]
        PSB = []
        for i in range(8):
            _b = P.buf("ps%d" % i)
            _b.excl = True
            PSB.append([_b, _b])

        MOD = sbuf(top, "MOD", [128, 2, 72, 2], F32)
        bMOD = P.buf("MOD")
        LNG = sbuf(top, "LNG", [128, 48], F32)
        LNB = sbuf(top, "LNB", [128, 48], F32)
        bLN = P.buf("LN")
        ONES = sbuf(top, "ONES", [128, 4, 128], BF16)
        bONES = P.buf("ONES")
        EPSC = sbuf(top, "EPSC", [128, 1], F32)

        with contextlib.ExitStack() as st:
            SC = sbuf(st, "SC", [128, 8, 2], F32)
            SG0 = sbuf(st, "SG0", [128, 8, 2], F32)
            ADB = sbuf(st, "ADB", [128, 2, 72, 2], F32)
            WS = [sbuf(st, "WS%d" % i, [128, 8, 1024], F32) for i in range(2)]
            bSC, bADB = P.buf("SC"), P.buf("ADB")
            bWS = [P.buf("WS0"), P.buf("WS1")]
            K.dma("sp", SC[:], ct[:, :, :], [K.db("ct")], [bSC])
            K.dma("sp", ADB[:], ada_b_t[:, :, :, :], [K.db("ada_b_t")], [bADB])
            K.dma("sp", LNG[:], ln_g_t[:, :], [], [bLN])
            K.dma("sp", LNB[:], ln_b_t[:, :], [], [bLN])
            bSG0 = P.buf("SG0")
            K.act(SG0[:], SC[:], AF.Silu, [bSC], [bSG0])
            K.copy("dve", SC[:], SG0[:], [bSG0], [bSC])
            K.memset("pool", ONES[:, 0, :], 1.0 / 1024.0, [bONES])
            K.memset("pool", ONES[:, 1, :], 1.0 / 128.0, [bONES])
            K.memset("pool", ONES[:, 2, :], 1.0 / 256.0, [bONES])
            K.memset("pool", ONES[:, 3, :], 1.0, [bONES])
            K.memset("pool", EPSC[:], 0.0, [bONES])
            n = 0
            for l in range(2):
                psA = PS[l]
                for s in range(9):
                    wsl = WS[n % 2]
                    bw = bWS[n % 2]
                    n += 1
                    src = ada_w[l, :, s * 1024:(s + 1) * 1024].rearrange("(c p) n -> p c n", p=128)
                    for kc in range(8):
                        K.dma("sp" if kc % 2 == 0 else "act", wsl[:, kc, :], src[:, kc, :], [], [bw])
                    for m in range(8):
                        col = (s * 8 + m) * 2
                        for kc in range(8):
                            K.mm(psA[:, col:col + 2], wsl[:, kc, m * 128:(m + 1) * 128], SC[:, kc, :],
                                 kc == 0, kc == 7, [bw, bSC], [PSB[l][0]])
                K.tt("dve", MOD[:, l, :, :], psA[:, 0:144].rearrange("p (a b) -> p a b", b=2), ADB[:, l, :, :],
                     ALU.add, [PSB[l][0], bADB], [bMOD])
                for i in (1, 4, 7):
                    K.ts("dve", MOD[:, l, i * 8:(i + 1) * 8, :], MOD[:, l, i * 8:(i + 1) * 8, :], 1.0, ALU.add,
                         [bMOD], [bMOD])
                for i, f in ((2, 0.5 / DN_ALPHA), (5, 1.0 / DN_ALPHA), (8, 0.5 / DN_ALPHA)):
                    K.ts("dve", MOD[:, l, i * 8:(i + 1) * 8, :], MOD[:, l, i * 8:(i + 1) * 8, :], f, ALU.mult,
                         [bMOD], [bMOD])
            P.flush()

        def modap(l, i, m, col):
            return MOD[:, l, i * 8 + m, col:col + 1]

        def load_w(st, name, src2d, kc, ncols, eng_q="pool"):
            t = sbuf(st, name, [128, kc, ncols], BF16)
            b = P.buf(name)
            src = src2d.rearrange("(c p) n -> p c n", p=128)
            for c in range(kc):
                K.dma(eng_q, t[:, c, :], src[:, c, :], [], [b])
            return t, b

        def layer_norm(Z, bZ, ZB, bZB, ZQ, bZQ, ST, bST, Hout, bH, gcol, width, psb_idx, eps):
            w_ = width
            for m in range(8):
                K.copy("pool", ZB[:, m, :w_], Z[:, m, :w_], [bZ], [bZB])
                K.act(ZQ[:, m, :w_], Z[:, m, :w_], AF.Square, [bZ], [bZQ])
            pm, bpm = PS[psb_idx][:, 0:w_], PSB[psb_idx][0]
            pq, bpq = PS[psb_idx + 1][:, 0:w_], PSB[psb_idx + 1][0]
            for m in range(8):
                K.mm(pm, ONES[:, 0, :], ZB[:, m, :w_], m == 0, m == 7, [bZB, bONES], [bpm])
            for m in range(8):
                K.mm(pq, ONES[:, 0, :], ZQ[:, m, :w_], m == 0, m == 7, [bZQ, bONES], [bpq])
            mean, m2, rstd = ST[:, 0, :w_], ST[:, 1, :w_], ST[:, 2, :w_]
            K.copy("act", mean, pm, [bpm], [bST])
            K.tt("pool", m2, mean, mean, ALU.mult, [bST], [bST])
            K.stt("dve", rstd, pq, eps, m2, ALU.add, ALU.subtract, [bpq, bST], [bST])
            K.act(rstd, rstd, AF.Sqrt, [bST], [bST])
            K.recip(rstd, rstd, [bST], [bST])
            for m in range(8):
                K.tt("dve", Z[:, m, :w_], Z[:, m, :w_], mean, ALU.subtract, [bZ, bST], [bZ])
                K.tt("pool", Z[:, m, :w_], Z[:, m, :w_], rstd, ALU.mult, [bZ, bST], [bZ])
                K.act(Hout[:, m, :w_], Z[:, m, :w_], AF.Identity, [bZ, bLN], [bH],
                      bias=LNB[:, gcol * 8 + m:gcol * 8 + m + 1], scale=LNG[:, gcol * 8 + m:gcol * 8 + m + 1])

        def ffn_phase(l, f, src, srcname, dst, dstname, tiles, src_col, dst_col):
            with contextlib.ExitStack() as st:
                WG, bWG = load_w(st, "WG", wg[l, f], 8, DFF)
                WU, bWU = load_w(st, "WU", wu[l, f], 8, DFF)
                WD, bWD = load_w(st, "WD", wd[l, f], NFF, D)
                Hs = [sbuf(st, "Hs%d" % i, [128, 8, T], F32) for i in range(2)]
                bHs = [P.buf("H0"), P.buf("H1")]
                XMs = [sbuf(st, "XM%d" % i, [128, 8, T], BF16) for i in range(2)]
                bXMs = [P.buf("XM0"), P.buf("XM1")]
                HID = sbuf(st, "HID", [128, NFF, T], BF16)
                bHID = P.buf("HID")
                Zs = [sbuf(st, "Z%d" % i, [128, 8, T], F32) for i in range(2)]
                bZs = [P.buf("Z0"), P.buf("Z1")]
                ZBs = [sbuf(st, "ZB%d" % i, [128, 8, T], BF16) for i in range(2)]
                bZBs = [P.buf("ZB0"), P.buf("ZB1")]
                ZQs = [sbuf(st, "ZQ%d" % i, [128, 8, T], BF16) for i in range(2)]
                bZQs = [P.buf("ZQ0"), P.buf("ZQ1")]
                SG = [sbuf(st, "SG%d" % i, [128, T], F32) for i in range(2)]
                bSG = [P.buf("SG0"), P.buf("SG1")]
                ST = sbuf(st, "ST", [128, 3, T], F32)
                bST = P.buf("ST")
                NH = sbuf(st, "NH", [128, T], F32)
                bNH = P.buf("NH")
                K.memset("pool", NH[:], -0.5, [bNH])
                srcv = src.rearrange("(c p) t -> p c t", p=128)
                dstv = dst.rearrange("(c p) t -> p c t", p=128)
                gcol = l * 3 + (0 if f == 0 else 2)
                mi = 0 if f == 0 else 6
                eps = LN_EPS / (DN_ALPHA ** 2)
                n = len(tiles)

                def load(idx):
                    c0 = src_col(tiles[idx])
                    K.dma("sp", Hs[idx % 2][:], srcv[:, :, c0:c0 + T], [K.db(srcname)], [bHs[idx % 2]])

                def s1a(idx):
                    ti = tiles[idx]
                    H, bH, XM, bXM = Hs[idx % 2], bHs[idx % 2], XMs[idx % 2], bXMs[idx % 2]
                    col = 1 if ti == 32 else 0
                    for kc in range(8):
                        K.act(XM[:, kc, :], H[:, kc, :], AF.Identity, [bH, bMOD], [bXM],
                              bias=modap(l, mi, kc, col), scale=modap(l, mi + 1, kc, col))

                def s1b(idx):
                    XM, bXM = XMs[idx % 2], bXMs[idx % 2]
                    for j in range(NFF):
                        pg, bpg = PS[j % 2][:, 0:T], PSB[j % 2][0]
                        pu, bpu = PS[2 + j % 2][:, 0:T], PSB[2 + j % 2][0]
                        for kc in range(8):
                            K.mm(pg, WG[:, kc, j * 128:(j + 1) * 128], XM[:, kc, :], kc == 0, kc == 7, [bWG, bXM], [bpg])
                        for kc in range(8):
                            K.mm(pu, WU[:, kc, j * 128:(j + 1) * 128], XM[:, kc, :], kc == 0, kc == 7, [bWU, bXM], [bpu])
                        sg, bsg = SG[j % 2], bSG[j % 2]
                        K.act(sg[:], pg, AF.Silu, [bpg], [bsg])
                        K.tt("dve", HID[:, j, :], sg[:], pu, ALU.mult, [bsg, bpu], [bHID])

                def s2(idx):
                    ti = tiles[idx]
                    H, bH, Z, bZ = Hs[idx % 2], bHs[idx % 2], Zs[idx % 2], bZs[idx % 2]
                    ZB, bZB, ZQ, bZQ = ZBs[idx % 2], bZBs[idx % 2], ZQs[idx % 2], bZQs[idx % 2]
                    col = 1 if ti == 32 else 0
                    for m in range(8):
                        pd, bpd = PS[4 + m % 2][:, 0:T], PSB[4 + m % 2][0]
                        for j in range(NFF):
                            K.mm(pd, WD[:, j, m * 128:(m + 1) * 128], HID[:, j, :], j == 0, j == NFF - 1, [bWD, bHID], [bpd])
                        K.stt("dve", Z[:, m, :], pd, modap(l, mi + 2, m, col), H[:, m, :], ALU.mult, ALU.add, [bpd, bH, bMOD], [bZ])
                        K.copy("pool", ZB[:, m, :], Z[:, m, :], [bZ], [bZB])
                        K.act(ZQ[:, m, :], Z[:, m, :], AF.Square, [bZ], [bZQ])

                def s3(idx):
                    ti = tiles[idx]
                    Z, bZ = Zs[idx % 2], bZs[idx % 2]
                    ZB, bZB, ZQ, bZQ = ZBs[idx % 2], bZBs[idx % 2], ZQs[idx % 2], bZQs[idx % 2]
                    pm, bpm = PS[6][:, 0:T], PSB[6][0]
                    pq, bpq = PS[7][:, 0:T], PSB[7][0]
                    for m in range(8):
                        K.mm(pm, ONES[:, 0, :], ZB[:, m, :], m == 0, m == 7, [bZB, bONES], [bpm])
                    for m in range(8):
                        K.mm(pq, ONES[:, 0, :], ZQ[:, m, :], m == 0, m == 7, [bZQ, bONES], [bpq])
                    mean, var, rstd = ST[:, 0, :], ST[:, 1, :], ST[:, 2, :]
                    K.copy("dve", mean, pm, [bpm], [bST])
                    K.tt("dve", rstd, mean, mean, ALU.mult, [bST], [bST])
                    K.stt("dve", var, pq, eps, rstd, ALU.add, ALU.subtract, [bpq, bST], [bST])
                    K.act(var, var, AF.Sqrt, [bST], [bST])
                    K.recip(rstd, var, [bST], [bST])
                    for m in range(8):
                        K.tt("pool", Z[:, m, :], Z[:, m, :], mean, ALU.subtract, [bZ, bST], [bZ])
                        K.tt("pool", Z[:, m, :], Z[:, m, :], rstd, ALU.mult, [bZ, bST], [bZ])
                        K.ts("pool", Z[:, m, :], Z[:, m, :], LNG[:, gcol * 8 + m:gcol * 8 + m + 1], ALU.mult, [bZ, bLN], [bZ],
                             s2=LNB[:, gcol * 8 + m:gcol * 8 + m + 1], op1=ALU.add)
                    c1 = dst_col(ti)
                    K.dma("sp", dstv[:, :, c1:c1 + T], Z[:], [bZ], [K.db(dstname)])

                load(0)
                if n > 1:
                    load(1)
                s1a(0)
                s1b(0)
                s2(0)
                if n > 1:
                    s1a(1)
                for idx in range(n):
                    if idx + 2 < n:
                        load(idx + 2)
                    if idx + 1 < n:
                        s1b(idx + 1)
                    if idx + 2 < n:
                        s1a(idx + 2)
                    if idx + 1 < n:
                        s2(idx + 1)
                    s3(idx)
                P.flush()

        NTL = int(os.environ.get("NTILES", "33"))
        if NTL > 0:
            ffn_phase(0, 0, xt, "xt", H1, "H1", TILES_ALL[:NTL], lambda ti: ti * 256, lambda ti: ti * 256)

        class Rot:
            def __init__(self, st, name, n, shape, dt):
                self.t = [sbuf(st, "%s%d" % (name, i), shape, dt) for i in range(n)]
                self.b = [P.buf("%s%d" % (name, i)) for i in range(n)]
                self.i = 0

            def get(self):
                k = self.i % len(self.t)
                self.i += 1
                return self.t[k], self.b[k]

        bank_ctr = [0]

        def bank(lo=0, hi=8):
            k = lo + bank_ctr[0] % (hi - lo)
            bank_ctr[0] += 1
            return PS[k], PSB[k][0]

        PHASES = os.environ.get("PHASES", "ABCDEFGHIJK")

        def proj0_phase():
            with contextlib.ExitStack() as st:
                WIN, bWIN = load_w(st, "WIN", ev_w_in, 8, 1952)
                WPM, bWPM = load_w(st, "WPM", ev_w_in_perm, 8, 1024)
                WKPE, bWKPE = load_w(st, "WKPE", ev_w_kpe, 8, 192)
                WUQ, bWUQ = load_w(st, "WUQ", ev_w_uq, 2, 768)
                WUQP, bWUQP = load_w(st, "WUQP", ev_w_uq_perm, 2, 768)
                WUK, bWUK = load_w(st, "WUK", ev_w_ukv_k, 1, 768)
                WUV, bWUV = load_w(st, "WUV", ev_w_ukv_v, 1, 512)
                SM = sbuf(st, "SM", [128, 4], F32)
                bSM = P.buf("SM")
                K.dma("sp", SM[:], ev_small[:, :], [], [bSM])
                Hs = [sbuf(st, "Hp%d" % i, [128, 8, T], F32) for i in range(2)]
                bHs = [P.buf("Hp0"), P.buf("Hp1")]
                TAB = [sbuf(st, "TAB%d" % i, [128, 4, T], F32) for i in range(2)]
                bTAB = [P.buf("TAB0"), P.buf("TAB1")]
                XM = sbuf(st, "XMp", [128, 8, T], BF16)
                bXM = P.buf("XMp")
                TMP = Rot(st, "TMP", 4, [128, T], F32)
                OB = Rot(st, "OB", 6, [128, T], BF16)
                OV = Rot(st, "OV", 2, [128, 512], BF16)
                VAt = sbuf(st, "VAt", [128, 2, 8, 128], BF16)
                bVAt = [P.buf("VAt0"), P.buf("VAt1")]
                K.memset("pool", VAt[:, :, :, 64:128], 1.0, bVAt)
                KVL = sbuf(st, "KVL", [128, T], F32)
                SQ = sbuf(st, "SQ", [128, 2, T], BF16)
                RS = sbuf(st, "RS", [128, T], F32)
                KVN = sbuf(st, "KVN", [128, T], BF16)
                KPE = sbuf(st, "KPE", [96, T], F32)
                QL = sbuf(st, "QL", [128, 2, T], F32)
                QN = sbuf(st, "QN", [128, 2, T], BF16)
                bKVL, bSQ, bRS, bKVN, bKPE, bQL, bQN = [P.buf(n) for n in "KVL SQ RS KVN KPE QL QN".split()]
                srcv = H1.rearrange("(c p) t -> p c t", p=128)

                def load(idx):
                    ti = TILES_ALL[idx]
                    c0 = ti * 256
                    K.dma("sp", Hs[idx % 2][:], srcv[:, :, c0:c0 + T], [K.db("H1")], [bHs[idx % 2]])
                    tb_, btb = TAB[idx % 2], bTAB[idx % 2]
                    K.dma("sp", tb_[0:96, 0, :], ca[:, c0:c0 + T], [], [btb])
                    K.dma("sp", tb_[0:96, 1, :], sa[:, c0:c0 + T], [], [btb])
                    K.dma("sp", tb_[:, 2, :], cb[:, c0:c0 + T], [], [btb])
                    K.dma("sp", tb_[:, 3, :], sb_[:, c0:c0 + T], [], [btb])

                def rope_out(psa, bpa, psb, bpb, c_ap, s_ap, btb, rows, dst, dstname):
                    t1, b1 = TMP.get()
                    t2, b2 = TMP.get()
                    K.tt("dve", t1[0:rows, :], psa, c_ap, ALU.mult, [bpa, btb], [b1])
                    K.tt("dve", t2[0:rows, :], psb, s_ap, ALU.mult, [bpb, btb], [b2])
                    ob, bob = OB.get()
                    K.tt("pool", ob[0:rows, :], t1[0:rows, :], t2[0:rows, :], ALU.add, [b1, b2], [bob])
                    K.dma("sp", dst, ob[0:rows, :], [bob], [K.db(dstname)])

                load(0)
                for idx, ti in enumerate(TILES_ALL):
                    if idx + 1 < len(TILES_ALL):
                        load(idx + 1)
                    H, bH = Hs[idx % 2], bHs[idx % 2]
                    tb_, btb = TAB[idx % 2], bTAB[idx % 2]
                    col = 1 if ti == 32 else 0
                    isq = ti in TILES_Q
                    t0 = ti * 256
                    q0 = qcol(ti)
                    for kc in range(8):
                        K.act(XM[:, kc, :], H[:, kc, :], AF.Identity, [bH, bMOD], [bXM],
                              bias=modap(0, 3, kc, col), scale=modap(0, 4, kc, col))
                    for (isneeded, wc0, pc0, dst3, dname, dcol) in ((True, 928, 512, KBs, "KBs", t0), (isq, 256, 0, QB, "QB", q0)):
                        if not isneeded:
                            continue
                        for h in range(4):
                            p1, bp1 = bank()
                            p2, bp2 = bank()
                            for kc in range(8):
                                K.mm(p1[:, 0:T], WIN[:, kc, wc0 + h * 128:wc0 + (h + 1) * 128], XM[:, kc, :], kc == 0, kc == 7, [bWIN, bXM], [bp1])
                            for kc in range(8):
                                K.mm(p2[:, 0:T], WPM[:, kc, pc0 + h * 128:pc0 + (h + 1) * 128], XM[:, kc, :], kc == 0, kc == 7, [bWPM, bXM], [bp2])
                            rope_out(p1[:, 0:T], bp1, p2[:, 0:T], bp2, tb_[:, 2, :], tb_[:, 3, :], btb, 128,
                                     dst3[h, :, dcol:dcol + T], dname)
                    for tb in range(2):
                        pv, bpv = bank()
                        for kc in range(8):
                            K.mm(pv[:, :], XM[:, kc, tb * 128:(tb + 1) * 128], WIN[:, kc, 1440:1952], kc == 0, kc == 7, [bWIN, bXM], [bpv])
                        ov, bov = OV.get()
                        K.copy("act", ov[:], pv[:, :], [bpv], [bov])
                        K.dma("sp", VB[t0 + tb * 128:t0 + (tb + 1) * 128, :], ov[:], [bov], [K.db("VB")])
                    pk, bpk = bank()
                    for kc in range(8):
                        K.mm(pk[:, 0:T], WIN[:, kc, 768:896], XM[:, kc, :], kc == 0, kc == 7, [bWIN, bXM], [bpk])
                    K.copy("dve", KVL[:], pk[:, 0:T], [bpk], [bKVL])
                    K.act(SQ[:, 0, :], pk[:, 0:T], AF.Square, [bpk], [bSQ])
                    pss, bpss = bank()
                    K.mm(pss[:, 0:T], ONES[:, 1, :], SQ[:, 0, :], True, True, [bSQ, bONES], [bpss])
                    K.ts("dve", RS[:], pss[:, 0:T], 1e-6, ALU.add, [bpss], [bRS])
                    K.act(RS[:], RS[:], AF.Sqrt, [bRS], [bRS])
                    K.recip(RS[:], RS[:], [bRS], [bRS])
                    K.tt("dve", KVL[:], KVL[:], RS[:], ALU.mult, [bKVL, bRS], [bKVL])
                    K.act(KVN[:], KVL[:], AF.Identity, [bKVL, bSM], [bKVN], scale=SM[:, 2:3])
                    pp1, bpp1 = bank()
                    pp2, bpp2 = bank()
                    for kc in range(8):
                        K.mm(pp1[0:96, 0:T], WKPE[:, kc, 0:96], XM[:, kc, :], kc == 0, kc == 7, [bWKPE, bXM], [bpp1])
                    for kc in range(8):
                        K.mm(pp2[0:96, 0:T], WKPE[:, kc, 96:192], XM[:, kc, :], kc == 0, kc == 7, [bWKPE, bXM], [bpp2])
                    t1, b1 = TMP.get()
                    t2, b2 = TMP.get()
                    K.tt("dve", t1[0:96, :], pp1[0:96, 0:T], tb_[0:96, 0, :], ALU.mult, [bpp1, btb], [b1])
                    K.tt("dve", t2[0:96, :], pp2[0:96, 0:T], tb_[0:96, 1, :], ALU.mult, [bpp2, btb], [b2])
                    K.tt("pool", KPE[:], t1[0:96, :], t2[0:96, :], ALU.add, [b1, b2], [bKPE])
                    for h in range(8):
                        pkh, bpkh = bank()
                        K.mm(pkh[0:96, 0:T], WUK[:, 0, 96 * h:96 * h + 96], KVN[:], True, True, [bWUK, bKVN], [bpkh])
                        ob, bob = OB.get()
                        K.tt("dve", ob[0:96, :], pkh[0:96, 0:T], KPE[:], ALU.add, [bpkh, bKPE], [bob])
                        K.dma("sp", KA[h, :, t0:t0 + T], ob[0:96, :], [bob], [K.db("KA")])
                    for tb in range(2):
                        pv, bpv = bank()
                        K.mm(pv[:, :], KVN[:, tb * 128:(tb + 1) * 128], WUV[:, 0, :], True, True, [bWUV, bKVN], [bpv])
                        K.copy("act", VAt[:, tb, :, 0:64], pv[:, :].rearrange("p (h c) -> p h c", c=64), [bpv], [bVAt[tb]])
                        K.dma("sp", VA[t0 + tb * 128:t0 + (tb + 1) * 128, :].rearrange("p (h c) -> p h c", c=128),
                              VAt[:, tb, :, :], [bVAt[tb]], [K.db("VA")])
                    if isq:
                        for c in range(2):
                            pq_, bpq_ = bank()
                            for kc in range(8):
                                K.mm(pq_[:, 0:T], WIN[:, kc, c * 128:(c + 1) * 128], XM[:, kc, :], kc == 0, kc == 7, [bWIN, bXM], [bpq_])
                            K.copy("dve", QL[:, c, :], pq_[:, 0:T], [bpq_], [bQL])
                            K.act(SQ[:, c, :], pq_[:, 0:T], AF.Square, [bpq_], [bSQ])
                        pss, bpss = bank()
                        for c in range(2):
                            K.mm(pss[:, 0:T], ONES[:, 2, :], SQ[:, c, :], c == 0, c == 1, [bSQ, bONES], [bpss])
                        K.ts("dve", RS[:], pss[:, 0:T], 1e-6, ALU.add, [bpss], [bRS])
                        K.act(RS[:], RS[:], AF.Sqrt, [bRS], [bRS])
                        K.recip(RS[:], RS[:], [bRS], [bRS])
                        for c in range(2):
                            K.tt("dve", QL[:, c, :], QL[:, c, :], RS[:], ALU.mult, [bQL, bRS], [bQL])
                            K.act(QN[:, c, :], QL[:, c, :], AF.Identity, [bQL, bSM], [bQN], scale=SM[:, c:c + 1])
                        for h in range(8):
                            p1, bp1 = bank()
                            p2, bp2 = bank()
                            for c in range(2):
                                K.mm(p1[0:96, 0:T], WUQ[:, c, 96 * h:96 * h + 96], QN[:, c, :], c == 0, c == 1, [bWUQ, bQN], [bp1])
                            for c in range(2):
                                K.mm(p2[0:96, 0:T], WUQP[:, c, 96 * h:96 * h + 96], QN[:, c, :], c == 0, c == 1, [bWUQP, bQN], [bp2])
                            rope_out(p1[0:96, 0:T], bp1, p2[0:96, 0:T], bp2, tb_[0:96, 0, :], tb_[0:96, 1, :], btb, 96,
                                     QA[h, :, q0:q0+T], "QA")
                P.flush()

        if "C" in PHASES:
            proj0_phase()

        QTILES = [(i * 512, 512, 0, 66) for i in range(8)] + [(4096, 256, 0, 66), (4352, 256, 64, 66)]

        def att0_mla_phase(heads):
            with contextlib.ExitStack() as st:
                Kt = [sbuf(st, "Kt%d" % i, [96, NT], BF16) for i in range(2)]
                Vt = [sbuf(st, "Vt%d" % i, [128, 66, 128], BF16) for i in range(2)]
                Qt = [sbuf(st, "Qt%d" % i, [96, NQ], BF16) for i in range(2)]
                bKt = [P.buf("Kt%d" % i) for i in range(2)]
                bVt = [P.buf("Vt%d" % i) for i in range(2)]
                bQt = [P.buf("Qt%d" % i) for i in range(2)]
                PT = Rot(st, "PT", 6, [128, 512], BF16)
                RR = sbuf(st, "RR", [64, 512], F32)
                bRR = P.buf("RR")
                OO = Rot(st, "OO", 2, [64, 512], BF16)
                VAv = VA.rearrange("(kt p) (h c) -> p kt h c", p=128, h=8)

                def load(i):
                    h = heads[i]
                    K.dma("sp", Kt[i % 2][:], KA[h, :, :], [K.db("KA")], [bKt[i % 2]])
                    K.dma("sp", Qt[i % 2][:], QA[h, :, :], [K.db("QA")], [bQt[i % 2]])
                    for kq in range(6):
                        K.dma("act" if kq % 2 else "sp", Vt[i % 2][:, kq * 11:(kq + 1) * 11, :], VAv[:, kq * 11:(kq + 1) * 11, h, :], [K.db("VA")], [bVt[i % 2]])
                load(0)
                for i, h in enumerate(heads):
                    if i + 1 < len(heads):
                        load(i + 1)
                    kt_, vt_, qt_ = Kt[i % 2], Vt[i % 2], Qt[i % 2]
                    bk, bv, bq = bKt[i % 2], bVt[i % 2], bQt[i % 2]
                    steps = [(qi, q0, N, kt, kt == klo, kt == khi - 1) for qi, (q0, N, klo, khi) in enumerate(QTILES) for kt in range(klo, khi)]
                    Sq = {}

                    def emitS(s):
                        qi, q0, N, kt, first, last = steps[s]
                        S, bS = bank(0, 6)
                        K.mm(S[:, 0:N], kt_[:, kt * 128:(kt + 1) * 128], qt_[:, q0:q0 + N], True, True, [bk, bq], [bS])
                        Sq[s] = (S, bS)
                    DEPTH = 3
                    for s in range(min(DEPTH, len(steps))):
                        emitS(s)
                    for s in range(len(steps)):
                        if s + DEPTH < len(steps):
                            emitS(s + DEPTH)
                        qi, q0, N, kt, first, last = steps[s]
                        S, bS = Sq.pop(s)
                        O, bO = PS[6 + qi % 2], PSB[6 + qi % 2][0]
                        pt, bpt = PT.get()
                        K.act(pt[:, 0:N], S[:, 0:N], AF.Exp, [bS], [bpt], scale=A_SCALE)
                        K.mm(O[:, 0:N], vt_[:, kt, :], pt[:, 0:N], first, last, [bv, bpt], [bO])
                        if last:
                            K.recip(RR[:, 0:N], O[64:128, 0:N], [bO], [bRR])
                            oo, boo = OO.get()
                            K.tt("dve", oo[:, 0:N], O[0:64, 0:N], RR[:, 0:N], ALU.mult, [bO, bRR], [boo])
                            K.dma("sp", MIX[64 * h:64 * h + 64, q0:q0 + N], oo[:, 0:N], [boo], [K.db("MIX")])
                P.flush()

        def att0_diff_phase(heads):
            with contextlib.ExitStack() as st:
                Kt = [sbuf(st, "Kd%d" % i, [128, NT], BF16) for i in range(2)]
                Vt = [sbuf(st, "Vd%d" % i, [128, 66, 128], BF16) for i in range(2)]
                Qt = [sbuf(st, "Qd%d" % i, [128, 2, NQ], BF16) for i in range(2)]
                ACC = [sbuf(st, "ACC%d" % i, [128, 512], F32) for i in range(2)]
                bACC = [P.buf("ACC0"), P.buf("ACC1")]
                ONESF = sbuf(st, "ONESF", [128, 128], F32)
                bONESF = P.buf("ONESF")
                K.memset("pool", ONESF[:], 1.0, [bONESF])
                bKt = [P.buf("Kd%d" % i) for i in range(2)]
                bVt = [P.buf("Vd%d" % i) for i in range(2)]
                bQt = [P.buf("Qd%d" % i) for i in range(2)]
                PT = Rot(st, "PTd", 4, [128, 512], BF16)
                R1 = sbuf(st, "R1", [128, 512], F32)
                R2 = sbuf(st, "R2", [128, 512], F32)
                OA = sbuf(st, "OA", [128, 512], F32)
                OBd = sbuf(st, "OBd", [128, 512], F32)
                SQd = sbuf(st, "SQd", [128, 512], BF16)
                bR1, bR2, bOA, bOBd, bSQd = [P.buf(n) for n in "R1 R2 OA OBd SQd".split()]
                OO = Rot(st, "OOd", 2, [128, 512], BF16)
                LV = sbuf(st, "LV", [128, 256], F32)
                LP = sbuf(st, "LP", [128, 2, 64], F32)
                LS = sbuf(st, "LS", [128, 4], F32)
                GS = sbuf(st, "GS", [128, 4], F32)
                bLV, bLS, bGS = P.buf("LV"), P.buf("LS"), P.buf("GS")
                K.dma("sp", LV[:], ev_lam_bc[:, :], [], [bLV])
                K.dma("sp", GS[:], ev_small[:, :], [], [bGS])
                K.tt("dve", LP[:, 0, :], LV[:, 0:64], LV[:, 64:128], ALU.mult, [bLV], [bLV])
                K.tt("dve", LP[:, 1, :], LV[:, 128:192], LV[:, 192:256], ALU.mult, [bLV], [bLV])
                P.op("dve", lambda e: e.reduce_sum(out=LS[:, 0:2], in_=LP[:, :, :], axis=AX.X), [bLV], [bLS])
                K.act(LS[:, 0:2], LS[:, 0:2], AF.Exp, [bLS], [bLS])
                K.tt("dve", LS[:, 2:3], LS[:, 1:2], LS[:, 0:1], ALU.subtract, [bLS], [bLS])
                K.ts("dve", LS[:, 2:3], LS[:, 2:3], -LAM_INIT0, ALU.add, [bLS], [bLS])
                K.ts("dve", GS[:, 3:4], GS[:, 3:4], 1.0 - LAM_INIT0, ALU.mult, [bGS], [bGS])
                VBv = VB.rearrange("(kt p) (h c) -> p kt h c", p=128, h=4)

                def load(i):
                    h = heads[i]
                    K.dma("sp", Kt[i % 2][:], KBs[h, :, :], [K.db("KBs")], [bKt[i % 2]])
                    K.dma("sp", Qt[i % 2][0:64, 0, :], QB[h, 0:64, :], [K.db("QB")], [bQt[i % 2]])
                    K.dma("sp", Qt[i % 2][64:128, 1, :], QB[h, 64:128, :], [K.db("QB")], [bQt[i % 2]])
                    for kq in range(6):
                        K.dma("act" if kq % 2 else "sp", Vt[i % 2][:, kq * 11:(kq + 1) * 11, :], VBv[:, kq * 11:(kq + 1) * 11, h, :], [K.db("VB")], [bVt[i % 2]])
                for i_ in range(2):
                    K.memset("pool", Qt[i_][64:128, 0, :], 0.0, [bQt[i_]])
                    K.memset("pool", Qt[i_][0:64, 1, :], 0.0, [bQt[i_]])
                load(0)
                for i, h in enumerate(heads):
                    if i + 1 < len(heads):
                        load(i + 1)
                    kt_, vt_, qt_ = Kt[i % 2], Vt[i % 2], Qt[i % 2]
                    bk, bv, bq = bKt[i % 2], bVt[i % 2], bQt[i % 2]
                    O1, bO1 = PS[4], PSB[4][0]
                    L1, bL1 = PS[5], PSB[5][0]
                    O2, bO2 = PS[6], PSB[6][0]
                    L2, bL2 = PS[7], PSB[7][0]
                    steps = [(qi, q0, N, kt, mp, kt == klo, kt == khi - 1) for qi, (q0, N, klo, khi) in enumerate(QTILES)
                             for kt in range(klo, khi) for mp in range(2)]
                    Sq = {}

                    def emitS(s):
                        qi, q0, N, kt, mp, first, last = steps[s]
                        S, bS = bank(0, 4)
                        K.mm(S[:, 0:N], kt_[:, kt * 128:(kt + 1) * 128], qt_[:, mp, q0:q0 + N], True, True, [bk, bq], [bS])
                        Sq[s] = (S, bS)
                    DEPTH = 2
                    for s in range(min(DEPTH, len(steps))):
                        emitS(s)
                    for s in range(len(steps)):
                        if s + DEPTH < len(steps):
                            emitS(s + DEPTH)
                        qi, q0, N, kt, mp, first, last = steps[s]
                        S, bS = Sq.pop(s)
                        O_, bO_, L_, bL_ = ((O1, bO1, L1, bL1), (O2, bO2, L2, bL2))[mp]
                        pt, bpt = PT.get()
                        K.act(pt[:, 0:N], S[:, 0:N], AF.Exp, [bS], [bpt], scale=B_SCALE)
                        K.mm(O_[:, 0:N], vt_[:, kt, :], pt[:, 0:N], first, last, [bv, bpt], [bO_])
                        if mp == 0:
                            K.mm(L1[:, 0:N], ONES[:, 3, :], pt[:, 0:N], first, last, [bONES, bpt], [bL1])
                        else:
                            klo_ = QTILES[qi][2]
                            par = (kt - klo_) % 2
                            aeng = "dve" if par == 0 else "pool"
                            if kt - klo_ < 2:
                                K.copy(aeng, ACC[par][:, 0:N], pt[:, 0:N], [bpt], [bACC[par]])
                            else:
                                K.tt(aeng, ACC[par][:, 0:N], ACC[par][:, 0:N], pt[:, 0:N], ALU.add, [bpt, bACC[par]], [bACC[par]])
                        if not (last and mp == 1):
                            continue
                        K.mm(L2[:, 0:N], ONESF[:], ACC[0][:, 0:N], True, False, [bONESF, bACC[0]], [bL2])
                        K.mm(L2[:, 0:N], ONESF[:], ACC[1][:, 0:N], False, True, [bONESF, bACC[1]], [bL2])
                        K.recip(R1[:, 0:N], L1[:, 0:N], [bL1], [bR1])
                        K.recip(R2[:, 0:N], L2[:, 0:N], [bL2], [bR2])
                        K.tt("dve", OA[:, 0:N], O1[:, 0:N], R1[:, 0:N], ALU.mult, [bO1, bR1], [bOA])
                        K.tt("dve", OBd[:, 0:N], O2[:, 0:N], R2[:, 0:N], ALU.mult, [bO2, bR2], [bOBd])
                        K.stt("dve", OA[:, 0:N], OBd[:, 0:N], LS[:, 2:3], OA[:, 0:N], ALU.mult, ALU.add, [bOBd, bOA, bLS], [bOA])
                        K.act(SQd[:, 0:N], OA[:, 0:N], AF.Square, [bOA], [bSQd])
                        K.mm(L1[:, 0:N], ONES[:, 1, :], SQd[:, 0:N], True, True, [bONES, bSQd], [bL1])
                        K.ts("dve", R1[:, 0:N], L1[:, 0:N], 1e-5, ALU.add, [bL1], [bR1])
                        K.act(R1[:, 0:N], R1[:, 0:N], AF.Sqrt, [bR1], [bR1])
                        K.recip(R1[:, 0:N], R1[:, 0:N], [bR1], [bR1])
                        K.tt("pool", OA[:, 0:N], OA[:, 0:N], R1[:, 0:N], ALU.mult, [bOA, bR1], [bOA])
                        oo, boo = OO.get()
                        K.act(oo[:, 0:N], OA[:, 0:N], AF.Identity, [bOA, bGS], [boo], scale=GS[:, 3:4])
                        K.dma("sp", MIX[512 + 128 * h:512 + 128 * (h + 1), q0:q0 + N], oo[:, 0:N], [boo], [K.db("MIX")])
                P.flush()

        if "D" in PHASES:
            att0_mla_phase([0, 1, 2, 3])
            att0_mla_phase([4, 5, 6, 7])
            att0_diff_phase([0, 1])
            att0_diff_phase([2, 3])

        def out_phase(l, wout, mix, mixname, mixcol, hin, hinname, hincol, hout, houtname, houtcol, tiles):
            with contextlib.ExitStack() as st:
                WO, bWO = load_w(st, "WO", wout, 8, D)
                Hs = [sbuf(st, "Ho%d" % i, [128, 8, T], F32) for i in range(3)]
                MX = [sbuf(st, "MX%d" % i, [128, 8, T], BF16) for i in range(3)]
                bHs = [P.buf("Ho%d" % i) for i in range(3)]
                bMX = [P.buf("MX%d" % i) for i in range(3)]
                Zs = [sbuf(st, "Zo%d" % i, [128, 8, T], F32) for i in range(2)]
                bZs = [P.buf("Zo0"), P.buf("Zo1")]
                ZB = sbuf(st, "ZBo", [128, 8, T], BF16)
                ZQ = sbuf(st, "ZQo", [128, 8, T], BF16)
                ST = sbuf(st, "STo", [128, 3, T], F32)
                HO = sbuf(st, "HOo", [128, 8, T], F32)
                bZB, bZQ, bST, bHO = [P.buf(n) for n in "ZBo ZQo STo HOo".split()]
                hv = hin.rearrange("(c p) t -> p c t", p=128)
                mv = mix.rearrange("(c p) t -> p c t", p=128)
                ov = hout.rearrange("(c p) t -> p c t", p=128)
                n = len(tiles)

                def load(idx):
                    c0 = hincol(tiles[idx])
                    K.dma("sp", Hs[idx % 3][:], hv[:, :, c0:c0 + T], [K.db(hinname)], [bHs[idx % 3]])
                    c0 = mixcol(tiles[idx])
                    K.dma("sp", MX[idx % 3][:], mv[:, :, c0:c0 + T], [K.db(mixname)], [bMX[idx % 3]])

                def sa(idx):
                    ti = tiles[idx]
                    H, bH, M_, bM = Hs[idx % 3], bHs[idx % 3], MX[idx % 3], bMX[idx % 3]
                    Z, bZ = Zs[idx % 2], bZs[idx % 2]
                    col = 1 if ti == 32 else 0
                    for m in range(8):
                        py, bpy = bank(0, 6)
                        for kc in range(8):
                            K.mm(py[:, 0:T], WO[:, kc, m * 128:(m + 1) * 128], M_[:, kc, :], kc == 0, kc == 7, [bWO, bM], [bpy])
                        K.stt("dve", Z[:, m, :], py[:, 0:T], modap(l, 5, m, col), H[:, m, :], ALU.mult, ALU.add, [bpy, bH, bMOD], [bZ])

                def sb(idx):
                    ti = tiles[idx]
                    Z, bZ = Zs[idx % 2], bZs[idx % 2]
                    layer_norm(Z, bZ, ZB, bZB, ZQ, bZQ, ST, bST, HO, bHO, l * 3 + 1, T, 6, LN_EPS / (DN_ALPHA ** 2))
                    c1 = houtcol(ti)
                    K.dma("sp", ov[:, :, c1:c1 + T], HO[:], [bHO], [K.db(houtname)])

                load(0)
                if n > 1:
                    load(1)
                sa(0)
                for idx in range(n):
                    if idx + 2 < n:
                        load(idx + 2)
                    if idx + 1 < n:
                        sa(idx + 1)
                    sb(idx)
                P.flush()

        natcol = lambda ti: ti * 256
        if "E" in PHASES:
            out_phase(0, ev_w_out, MIX, "MIX", qcol, H1, "H1", natcol, H2, "H2", qcol, TILES_Q)
        if "F" in PHASES:
            ffn_phase(0, 1, H2, "H2", H3, "H3", TILES_Q, qcol, qcol)
        if "G" in PHASES:
            ffn_phase(1, 0, H3, "H3", H4, "H4", TILES_Q, qcol, qcol)

        def proj1_phase():
            with contextlib.ExitStack() as st:
                WIN, bWIN = load_w(st, "WIN1", od_w_in, 8, 2048)
                Hs = [sbuf(st, "Hq%d" % i, [128, 8, T], F32) for i in range(2)]
                bHs = [P.buf("Hq0"), P.buf("Hq1")]
                XM = sbuf(st, "XMq", [128, 8, T], BF16)
                bXM = P.buf("XMq")
                ZR = sbuf(st, "ZR", [128, 1024], BF16)
                bZR = P.buf("ZR")
                OU = Rot(st, "OU", 3, [128, T], F32)
                OB = Rot(st, "OB1", 4, [128, T], BF16)
                VDt = sbuf(st, "VDt", [128, 2, 8, 128], BF16)
                bVDt = [P.buf("VDt0"), P.buf("VDt1")]
                K.memset("pool", VDt[:, :, :, 64:128], 1.0, bVDt)
                K.memset("pool", ZR[:], 0.0, [bZR])
                for c in range(4):
                    K.dma("sp", KD[c, :, 0:256], ZR[:, 0:256], [bZR], [K.db("KD")])
                    K.dma("sp", KD[c, :, NKP - 256:NKP], ZR[:, 0:256], [bZR], [K.db("KD")])
                for r in range(2):
                    K.dma("sp", VD[r * 128:(r + 1) * 128, :], ZR[:], [bZR], [K.db("VD")])
                    K.dma("sp", VD[NKP - 256 + r * 128:NKP - 256 + (r + 1) * 128, :], ZR[:], [bZR], [K.db("VD")])
                srcv = H4.rearrange("(c p) t -> p c t", p=128)

                def load(idx):
                    c0 = qcol(TILES_Q[idx])
                    K.dma("sp", Hs[idx % 2][:], srcv[:, :, c0:c0 + T], [K.db("H4")], [bHs[idx % 2]])
                load(0)
                for idx, ti in enumerate(TILES_Q):
                    if idx + 1 < len(TILES_Q):
                        load(idx + 1)
                    H, bH = Hs[idx % 2], bHs[idx % 2]
                    col = 1 if ti == 32 else 0
                    lat = ti != 32
                    t0 = ti * 256
                    for kc in range(8):
                        K.act(XM[:, kc, :], H[:, kc, :], AF.Identity, [bH, bMOD], [bXM],
                              bias=modap(1, 3, kc, col), scale=modap(1, 4, kc, col))
                    if lat:
                        for c in range(4):
                            pu_, bpu_ = bank()
                            for kc in range(8):
                                K.mm(pu_[:, 0:T], WIN[:, kc, c * 128:(c + 1) * 128], XM[:, kc, :], kc == 0, kc == 7, [bWIN, bXM], [bpu_])
                            ou, bou = OU.get()
                            K.copy("act", ou[:], pu_[:, 0:T], [bpu_], [bou])
                            K.dma("sp", U[c * 128:(c + 1) * 128, t0:t0 + T], ou[:], [bou], [K.db("U")])
                        for c in range(4):
                            pq_, bpq_ = bank()
                            for kc in range(8):
                                K.mm(pq_[:, 0:T], WIN[:, kc, 512 + c * 128:512 + (c + 1) * 128], XM[:, kc, :], kc == 0, kc == 7, [bWIN, bXM], [bpq_])
                            ob, bob = OB.get()
                            K.copy("dve", ob[:], pq_[:, 0:T], [bpq_], [bob])
                            K.dma("sp", QD[c, :, t0:t0 + T], ob[:], [bob], [K.db("QD")])
                    for c in range(4):
                        pk_, bpk_ = bank()
                        for kc in range(8):
                            K.mm(pk_[:, 0:T], WIN[:, kc, 1024 + c * 128:1024 + (c + 1) * 128], XM[:, kc, :], kc == 0, kc == 7, [bWIN, bXM], [bpk_])
                        ob, bob = OB.get()
                        K.copy("dve", ob[:], pk_[:, 0:T], [bpk_], [bob])
                        if lat:
                            K.dma("sp", KD[c, :, 256 + t0:256 + t0 + T], ob[:], [bob], [K.db("KD")])
                        else:
                            K.dma("sp", KDC[c, :, :], ob[:], [bob], [K.db("KDC")])
                    for tb in range(2):
                        pv, bpv = bank()
                        for kc in range(8):
                            K.mm(pv[:, :], XM[:, kc, tb * 128:(tb + 1) * 128], WIN[:, kc, 1536:2048], kc == 0, kc == 7, [bWIN, bXM], [bpv])
                        K.copy("act", VDt[:, tb, :, 0:64], pv[:, :].rearrange("p (h c) -> p h c", c=64), [bpv], [bVDt[tb]])
                        if lat:
                            dst = VD[256 + t0 + tb * 128:256 + t0 + (tb + 1) * 128, :]
                            dn = "VD"
                        else:
                            dst = VDC[tb * 128:(tb + 1) * 128, :]
                            dn = "VDC"
                        K.dma("sp", dst.rearrange("p (h c) -> p h c", c=128), VDt[:, tb, :, :], [bVDt[tb]], [K.db(dn)])
                P.flush()

        if "H" in PHASES:
            proj1_phase()

        def pool_phase():
            with contextlib.ExitStack() as st:
                NP = NQL + 16
                UT = sbuf(st, "UT", [128, NP], F32)
                X1 = sbuf(st, "X1", [128, NP], F32)
                X2 = sbuf(st, "X2", [128, NP], F32)
                IC = sbuf(st, "IC", [128, NQL], F32)
                PMf = sbuf(st, "PMf", [128, NQL], F32)
                PMb = sbuf(st, "PMb", [128, NQL], BF16)
                WP = sbuf(st, "WP", [128, 128], BF16)
                PSC = sbuf(st, "PSC", [128, 4], F32)
                bUT, bX1, bX2, bIC, bPMf, bPMb, bWP, bPSC = [P.buf(n) for n in "UT X1 X2 IC PMf PMb WP PSC".split()]
                OO = Rot(st, "OOp", 3, [128, 512], BF16)
                K.dma("sp", PSC[:], od_ps_t[:, :], [], [bPSC])
                for g in range(4):
                    K.memset("pool", UT[:, 0:8], 0.0, [bUT])
                    K.memset("pool", UT[:, NP - 8:NP], 0.0, [bUT])
                    K.dma("sp", UT[:, 8:8 + NQL], U[g * 128:(g + 1) * 128, :], [K.db("U")], [bUT])
                    K.dma("sp", IC[:], icnt[:, g, :], [], [bIC])
                    K.dma("pool", WP[:], od_w_pool[g, :, :], [], [bWP])
                    A, bA = UT, bUT
                    outs = [(X1, bX1), (X2, bX2)]
                    for s_ in range(g + 1):
                        sh = 1 << s_
                        B, bB = outs[s_ % 2]
                        K.copy("pool", B[:, 0:sh], A[:, 0:sh], [bA], [bB])
                        K.tt("dve", B[:, sh:NP], A[:, sh:NP], A[:, 0:NP - sh], ALU.add, [bA], [bB])
                        A, bA = B, bB
                    right = (0, 1, 3, 7)[g]
                    K.tt("dve", PMf[:], A[:, 8 + right:8 + right + NQL], IC[:], ALU.mult, [bA, bIC], [bPMf])
                    K.tt("pool", PMb[:], PMf[:], UT[:, 8:8 + NQL], ALU.subtract, [bPMf, bUT], [bPMb])
                    for ch in range(9):
                        c0 = ch * 512
                        N = 512 if ch < 8 else 256
                        pp, bpp = bank()
                        K.mm(pp[:, 0:N], WP[:], PMb[:, c0:c0 + N], True, True, [bWP, bPMb], [bpp])
                        oo, boo = OO.get()
                        K.act(oo[:, 0:N], pp[:, 0:N], AF.Identity, [bpp, bPSC], [boo], scale=PSC[:, g:g + 1])
                        K.dma("sp", MIXD[g * 128:(g + 1) * 128, c0:c0 + N], oo[:, 0:N], [boo], [K.db("MIXD")])
                P.flush()

        def na_phase():
            with contextlib.ExitStack() as st:
                BIAS = sbuf(st, "BIAS", [128, 3, 6 * 8 * 256], BF16)
                bBIAS = P.buf("BIAS")
                IDN = sbuf(st, "IDN", [128, 128], BF16)
                bIDN = P.buf("IDN")
                K.dma("pool", IDN[:], ident[:, :], [], [bIDN])
                for t_ in range(3):
                    for hh in range(2):
                        K.dma("pool", BIAS[:, t_, hh * 6144:(hh + 1) * 6144], na_b[t_, :, hh * 6144:(hh + 1) * 6144], [], [bBIAS])
                for t_ in range(3):
                    K.ts("dve" if t_ % 2 == 0 else "pool", BIAS[:, t_, :], BIAS[:, t_, :], 1.0 / D_SCALE, ALU.mult, [bBIAS], [bBIAS])
                KDb = [sbuf(st, "KDb%d" % i, [128, 4, 768], BF16) for i in range(2)]
                VDb = [sbuf(st, "VDb%d" % i, [128, 6, 1024], BF16) for i in range(2)]
                QDb = [sbuf(st, "QDb%d" % i, [128, 4, 256], BF16) for i in range(2)]
                bKDb = [P.buf("KDb%d" % i) for i in range(2)]
                bVDb = [P.buf("VDb%d" % i) for i in range(2)]
                bQDb = [P.buf("QDb%d" % i) for i in range(2)]
                KC = sbuf(st, "KC", [128, 4, 256], BF16)
                VC = sbuf(st, "VC", [128, 2, 1024], BF16)
                bKC, bVC = P.buf("KC"), P.buf("VC")
                K.dma("sp", KC[:], KDC.rearrange("c p n -> p c n"), [K.db("KDC")], [bKC])
                K.dma("sp", VC[:], VDC.rearrange("(kt p) n -> p kt n", p=128), [K.db("VDC")], [bVC])
                SBt = Rot(st, "SBt", 4, [128, 256], F32)
                PT = Rot(st, "PTn", 6, [128, 256], BF16)
                RR = sbuf(st, "RRn", [64, 256], F32)
                bRR = P.buf("RRn")
                OO = Rot(st, "OOn", 3, [64, 256], BF16)
                KDv = KD.rearrange("c p n -> p c n")
                QDv = QD.rearrange("c p n -> p c n")
                VDv = VD.rearrange("(kt p) n -> p kt n", p=128)

                def load(bi):
                    i0 = 4 * bi
                    K.dma("sp", KDb[bi % 2][:], KDv[:, :, i0 * 64:i0 * 64 + 768], [K.db("KD")], [bKDb[bi % 2]])
                    K.dma("sp", VDb[bi % 2][:], VDv[:, i0 // 2:i0 // 2 + 6, :], [K.db("VD")], [bVDb[bi % 2]])
                    K.dma("sp", QDb[bi % 2][:], QDv[:, :, bi * 256:(bi + 1) * 256], [K.db("QD")], [bQDb[bi % 2]])
                load(0)
                for bi in range(17):
                    if bi + 1 < 17:
                        load(bi + 1)
                    kd, vd, qd = KDb[bi % 2], VDb[bi % 2], QDb[bi % 2]
                    bk, bv, bq = bKDb[bi % 2], bVDb[bi % 2], bQDb[bi % 2]
                    tab = 0 if bi == 0 else (2 if bi == 16 else 1)
                    steps = [(h, kt) for h in range(8) for kt in range(8)]
                    Sq = {}

                    def emitS(s):
                        h, kt = steps[s]
                        c, po = h // 2, (h % 2) * 64
                        S, bS = bank(0, 6)
                        if kt < 6:
                            boff = (kt * 8 + h) * 256
                            K.mm(S[:, 0:256], kd[po:po + 64, c, kt * 128:(kt + 1) * 128], qd[po:po + 64, c, :], True, False, [bk, bq], [bS])
                            K.mm(S[:, 0:256], IDN[:], BIAS[:, tab, boff:boff + 256], False, True, [bIDN, bBIAS], [bS])
                        else:
                            kk = kt - 6
                            K.mm(S[:, 0:256], KC[po:po + 64, c, kk * 128:(kk + 1) * 128], qd[po:po + 64, c, :], True, True, [bKC, bq], [bS])
                        Sq[s] = (S, bS)
                    DEPTH = 3
                    for s in range(DEPTH):
                        emitS(s)
                    for s in range(len(steps)):
                        if s + DEPTH < len(steps):
                            emitS(s + DEPTH)
                        h, kt = steps[s]
                        S, bS = Sq.pop(s)
                        O, bO = PS[6 + h % 2], PSB[6 + h % 2][0]
                        pt, bpt = PT.get()
                        if kt < 6:
                            K.act(pt[:], S[:, 0:256], AF.Exp, [bS], [bpt], scale=D_SCALE)
                            K.mm(O[:, 0:256], vd[:, kt, h * 128:(h + 1) * 128], pt[:], kt == 0, False, [bv, bpt], [bO])
                        else:
                            kk = kt - 6
                            K.act(pt[:], S[:, 0:256], AF.Exp, [bS], [bpt], scale=D_SCALE)
                            K.mm(O[:, 0:256], VC[:, kk, h * 128:(h + 1) * 128], pt[:], False, kt == 7, [bVC, bpt], [bO])
                        if kt == 7:
                            K.recip(RR[:], O[64:128, 0:256], [bO], [bRR])
                            oo, boo = OO.get()
                            K.tt("dve", oo[:], O[0:64, 0:256], RR[:], ALU.mult, [bO, bRR], [boo])
                            K.dma("sp", MIXD[512 + 64 * h:512 + 64 * (h + 1), bi * 256:(bi + 1) * 256], oo[:], [boo], [K.db("MIXD")])
                P.flush()

        if "I" in PHASES:
            pool_phase()
            na_phase()
        TILES_L = list(range(17))
        if "J" in PHASES:
            out_phase(1, od_w_out, MIXD, "MIXD", natcol, H4, "H4", qcol, H5, "H5", natcol, TILES_L)
        if "K" in PHASES:
            ffn_phase(1, 1, H5, "H5", outT, "outT", TILES_L, natcol, natcol)
    return nc


def _perm_sign(rot_dim):
    q = rot_dim // 4
    perm = np.zeros(rot_dim, np.int64)
    sign = np.zeros(rot_dim, np.float32)
    for i in range(rot_dim):
        qq = i // q
        if qq % 2 == 0:
            perm[i] = i + q
            sign[i] = -1.0
        else:
            perm[i] = i - q
            sign[i] = 1.0
    return perm, sign


def _rope_tables(pos, rot_dim):
    q = rot_dim // 4
    row = (pos // 64).astype(np.float32)
    col = (pos % 64).astype(np.float32)
    inv = (np.float32(10000.0) ** (-np.arange(q, dtype=np.float32) / np.float32(q))).astype(np.float32)
    ang_r = row[:, None] * inv[None, :]
    ang_c = col[:, None] * inv[None, :]
    ang = np.concatenate([ang_r, ang_r, ang_c, ang_c], -1).astype(np.float32)
    return np.cos(ang).astype(np.float32), np.sin(ang).astype(np.float32)


def core_positions(half):
    s0 = 0 if half == 0 else 3840
    own = np.arange(s0, s0 + NQL)
    rest = np.concatenate([np.arange(0, s0), np.arange(s0 + NQL, NL)])
    return np.concatenate([own, rest]), s0


def prepare_inputs(inp):
    f = lambda a: np.ascontiguousarray(np.asarray(a, dtype=np.float32))
    x, c, ctx, c_ctx = f(inp["x"]), f(inp["c"]), f(inp["ctx"]), f(inp["c_ctx"])
    shared = {}
    shared["ada_w"] = f(inp["ada_w"])
    ada_b = f(inp["ada_b"])
    abt = ada_b.reshape(2, 72, 128).transpose(2, 0, 1)
    shared["ada_b_t"] = np.ascontiguousarray(np.stack([abt, abt], -1))
    shared["ln_g_t"] = np.ascontiguousarray(f(inp["ln_g"]).reshape(2, 3, 8, 128).transpose(3, 0, 1, 2).reshape(128, 48))
    shared["ln_b_t"] = np.ascontiguousarray(f(inp["ln_b"]).reshape(2, 3, 8, 128).transpose(3, 0, 1, 2).reshape(128, 48))
    shared["wg"] = f(inp["ffn_w_gate"])
    shared["wu"] = f(inp["ffn_w_up"])
    shared["wd"] = f(inp["ffn_w_down"])
    w_in = f(inp["ev_w_in"])[0]
    shared["ev_w_in"] = w_in
    pa, sga = _perm_sign(32)
    pb, sgb = _perm_sign(64)
    wperm = np.zeros((D, 1024), np.float32)
    for blk in range(8):
        wperm[:, blk * 64:(blk + 1) * 64] = w_in[:, 256 + blk * 64 + pb]
        wperm[:, 512 + blk * 64:512 + (blk + 1) * 64] = w_in[:, 928 + blk * 64 + pb]
    shared["ev_w_in_perm"] = wperm
    wkpe = np.zeros((D, 2, 96), np.float32)
    wkpe[:, 0, 64:] = w_in[:, 896:928]
    wkpe[:, 1, 64:] = w_in[:, 896 + pa]
    shared["ev_w_kpe"] = wkpe.reshape(D, 192)
    w_uq = f(inp["ev_w_uq"])[0]
    shared["ev_w_uq"] = w_uq
    wuqp = w_uq.copy()
    for h in range(8):
        wuqp[:, 96 * h + 64:96 * h + 96] = w_uq[:, 96 * h + 64 + pa]
    shared["ev_w_uq_perm"] = wuqp
    w_ukv = f(inp["ev_w_ukv"])[0].reshape(128, 8, 128)
    wk = np.zeros((128, 8, 96), np.float32)
    wk[:, :, :64] = w_ukv[:, :, :64]
    shared["ev_w_ukv_k"] = wk.reshape(128, 768)
    shared["ev_w_ukv_v"] = np.ascontiguousarray(w_ukv[:, :, 64:].reshape(128, 512))
    small = np.zeros((128, 4), np.float32)
    gq = f(inp["ev_g_qlat"])[0]
    small[:, 0] = gq[:128]
    small[:, 1] = gq[128:]
    small[:, 2] = f(inp["ev_g_kvlat"])[0]
    small[:, 3] = f(inp["ev_g_sub"])[0]
    shared["ev_small"] = small
    shared["ev_lam_bc"] = np.ascontiguousarray(np.broadcast_to(f(inp["ev_lam"])[0].reshape(1, 256), (128, 256)))
    shared["ev_w_out"] = f(inp["ev_w_out"])[0]
    shared["od_w_in"] = f(inp["od_w_in"])[0]
    shared["od_w_out"] = f(inp["od_w_out"])[0]
    shared["od_w_pool"] = f(inp["od_w_pool"])[0]
    shared["od_ps_t"] = np.ascontiguousarray(f(inp["od_pool_scale"])[0].reshape(4, 128).T)
    rpb = f(inp["od_rpb"])[0]

    per_half = []
    for half in range(2):
        d = {}
        pos, s0 = core_positions(half)
        cA, sA = _rope_tables(pos, 32)
        cB, sB = _rope_tables(pos, 64)
        ca = np.ones((96, NT), np.float32)
        sa = np.zeros((96, NT), np.float32)
        ca[64:, :NL] = cA.T
        sa[64:, :NL] = (sA * sga[None, :]).T
        cbt = np.ones((128, NT), np.float32)
        sbt = np.zeros((128, NT), np.float32)
        cbt[:64, :NL] = cB.T
        cbt[64:, :NL] = cB.T
        sbt[:64, :NL] = (sB * sgb[None, :]).T
        sbt[64:, :NL] = (sB * sgb[None, :]).T
        d["ca"], d["sa"], d["cb"], d["sb"] = ca, sa, cbt, sbt
        ic = np.ones((4, NQL), np.float32)
        tg = s0 + np.arange(NQL)
        for g, w in enumerate((2, 4, 8, 16)):
            left = w // 2
            right = w - 1 - left
            lo = np.clip(tg - left, 0, NL)
            hi = np.clip(tg + right + 1, 0, NL)
            ic[g] = 1.0 / (hi - lo).astype(np.float32)
        d["icnt"] = np.ascontiguousarray(np.broadcast_to(ic[None], (128, 4, NQL)))
        roff = 0 if half == 0 else 60
        gtab = np.zeros((3, 128, 6, 8, 256), np.float32)
        mtab = np.zeros((3, 128, 6, 8, 256), np.float32)
        qc = np.arange(64)
        kc = np.arange(64)
        cs = np.clip(qc - 8, 0, 48)
        colv = (kc[None, :] >= cs[:, None]) & (kc[None, :] < cs[:, None] + 16)
        cidx = np.clip(kc[None, :] - qc[:, None] + 15, 0, 30)
        for tix, bi in enumerate((0, 8, 16)):
            i0 = 4 * bi
            for a in range(4):
                R = i0 + a + roff
                rs = int(np.clip(R - 4, 0, 120))
                for cc in range(12):
                    kr = i0 - 4 + cc
                    KR = kr + roff
                    rowv = (rs <= KR < rs + 8) and (0 <= kr < 68)
                    kt, ph = cc // 2, (cc % 2) * 64
                    if rowv:
                        vals = rpb[:, KR - R + 7, :][:, cidx]
                        gtab[tix, ph:ph + 64, kt, :, a * 64:(a + 1) * 64] = vals.transpose(2, 0, 1)
                        mtab[tix, ph:ph + 64, kt, :, a * 64:(a + 1) * 64] = np.where(colv.T[:, None, :], 0.0, -1e30)
                    else:
                        mtab[tix, ph:ph + 64, kt, :, a * 64:(a + 1) * 64] = -1e30
        d["na_b"] = np.where(mtab < -1.0, np.float32(-1e30), gtab).reshape(3, 128, 6 * 8 * 256)
        d["pos"] = pos
        per_half.append(d)

    in_maps = []
    for cid in range(8):
        b, half = cid // 2, cid % 2
        d = per_half[half]
        m = dict(shared)
        xtc = np.empty((D, NT), np.float32)
        xtc[:, :NL] = x[b][d["pos"]].T
        xtc[:, NL:] = ctx[b].T
        m["xt"] = xtc
        ctt = np.empty((128, 8, 2), np.float32)
        ctt[:, :, 0] = c[b].reshape(8, 128).T
        ctt[:, :, 1] = c_ctx.reshape(8, 128).T
        m["ct"] = ctt
        for k in ("ca", "sa", "cb", "sb", "icnt", "na_b"):
            m[k] = d[k]
        m["ident"] = np.eye(128, dtype=np.float32)
        in_maps.append(m)
    return in_maps


_NC_CACHE = {}


def kernel(**inputs):
    in_maps = prepare_inputs(inputs)
    if "nc" not in _NC_CACHE:
        _NC_CACHE["nc"] = build_program(False)
    nc = _NC_CACHE["nc"]
    res = run_bass_kernel_spmd(nc, in_maps, core_ids=list(range(8)))
    out = np.empty((4, NL, D), np.float32)
    for cid in range(8):
        b, half = cid // 2, cid % 2
        o = res.results[cid]["outT"]
        if half == 0:
            out[b, 0:4096] = o[:, 0:4096].T
        else:
            out[b, 4096:8192] = o[:, 256:4352].T
    return out
```
